# Optimizing a Trainium2 kernel written in Bass

```python
import jax, jax.numpy as jnp
from jax import lax
import numpy as np

D_MODEL = 2048
BATCH = 4
SEQ = 4096
DEPTH = 2

CHUNK = 64
Q_BLOCK = 128
SB_HEAD_DIM = 128
SB_WIDTH = D_MODEL // 4
SB_HEADS = SB_WIDTH // SB_HEAD_DIM
FOX_HEAD_DIM = 128
FOX_WIDTH = D_MODEL // 4
FOX_HEADS = FOX_WIDTH // FOX_HEAD_DIM
RWKV_HEAD_DIM = 64
RWKV_WIDTH = D_MODEL // 2
RWKV_HEADS = RWKV_WIDTH // RWKV_HEAD_DIM
DECAY_LORA = 64
AAA_LORA = 64
GATE_LORA = 160
RWKV_IN = 3 * RWKV_WIDTH + DECAY_LORA + AAA_LORA + GATE_LORA
N_BRANCHES = 3
N_IN = 3 * SB_WIDTH + 3 * FOX_WIDTH + FOX_HEADS + RWKV_IN + N_BRANCHES * D_MODEL
D_FF = 4 * D_MODEL
RMS_EPS = 1e-6
RWKV_GN_EPS = 64e-5

kernel_name = 'hybrid_sb_fox_rwkv7_block'


def _split(x, sizes):
    points = np.cumsum(sizes)[:-1].tolist()
    return jnp.split(x, points, axis=-1)


def _rms_norm(x, gain):
    xf = x.astype(jnp.float32)
    y = xf * lax.rsqrt(jnp.mean(xf * xf, axis=-1, keepdims=True) + RMS_EPS)
    return (y * gain.astype(jnp.float32)).astype(x.dtype)


def _heads(x, n_heads):
    b, s, _ = x.shape
    return x.reshape(b, s, n_heads, -1).transpose(0, 2, 1, 3)


def _merge_heads(x):
    b, h, s, d = x.shape
    return x.transpose(0, 2, 1, 3).reshape(b, s, h * d)


def _query_blocks(x):
    b, h, s = x.shape[:3]
    x = x.reshape((b, h, s // Q_BLOCK, Q_BLOCK) + x.shape[3:])
    return jnp.moveaxis(x, 2, 0)


def _unblock(o):
    nb, b, h, qb, d = o.shape
    return o.transpose(1, 2, 0, 3, 4).reshape(b, h, nb * qb, d)


def stick_breaking_attention(q, k, v):
    s, d = q.shape[2], q.shape[3]
    scale = d ** -0.5
    kpos = jnp.arange(s)

    def block(args):
        qb, start = args
        z = jnp.einsum('bhqd,bhkd->bhqk', qb, k).astype(jnp.float32) * scale
        qpos = start + jnp.arange(Q_BLOCK)
        strict = kpos[None, :] < qpos[:, None]
        log_beta = jax.nn.log_sigmoid(z)
        log_rest = jnp.where(strict, jax.nn.log_sigmoid(-z), 0.0)
        between = lax.cumsum(log_rest, axis=3, reverse=True) - log_rest
        a = jnp.where(strict, jnp.exp(log_beta + between), 0.0)
        return jnp.einsum('bhqk,bhkd->bhqd', a.astype(v.dtype), v)

    starts = jnp.arange(s // Q_BLOCK, dtype=jnp.int32) * Q_BLOCK
    return _unblock(lax.map(block, (_query_blocks(q), starts)))


def forgetting_attention(q, k, v, log_f):
    s, d = q.shape[2], q.shape[3]
    scale = d ** -0.5
    kpos = jnp.arange(s)
    c = lax.cumsum(log_f.astype(jnp.float32), axis=2)

    def block(args):
        qb, cb, start = args
        z = jnp.einsum('bhqd,bhkd->bhqk', qb, k).astype(jnp.float32) * scale
        z = z + (cb[..., :, None] - c[..., None, :])
        qpos = start + jnp.arange(Q_BLOCK)
        causal = kpos[None, :] <= qpos[:, None]
        p = jax.nn.softmax(jnp.where(causal, z, -jnp.inf), axis=-1)
        return jnp.einsum('bhqk,bhkd->bhqd', p.astype(v.dtype), v)

    starts = jnp.arange(s // Q_BLOCK, dtype=jnp.int32) * Q_BLOCK
    return _unblock(lax.map(block, (_query_blocks(q), _query_blocks(c), starts)))


def rwkv7_time_mix(z, mu, w0, w_up, a0, a_up, g_up, k_k, k_a, r_k, ln_w, ln_b):
    b, s, _ = z.shape
    h, n = RWKV_HEADS, RWKV_HEAD_DIM
    z_prev = jnp.pad(z[:, :-1], ((0, 0), (1, 0), (0, 0)))
    z = z + (z_prev - z) * mu
    r, k, v, wd, ad, gd = _split(z, (RWKV_WIDTH,) * 3 + (DECAY_LORA, AAA_LORA, GATE_LORA))
    w_log = -jax.nn.softplus(-(w0 + jnp.tanh(wd) @ w_up)) - 0.5
    a = jax.nn.sigmoid(a0 + ad @ a_up)
    g = jax.nn.sigmoid(gd) @ g_up
    kk_raw = k * k_k
    k = k * (1 + (a - 1) * k_a)

    def per_head(t):
        return t.reshape(b, s, h, n).astype(jnp.float32)

    r_h, k_h, v_h, a_h, kk = map(per_head, (r, k, v, a, kk_raw))
    kk = kk / jnp.maximum(jnp.sqrt(jnp.sum(kk * kk, axis=-1, keepdims=True)), 1e-12)
    decay = jnp.exp(-jnp.exp(per_head(w_log)))

    def time_major(t):
        return t.transpose(1, 0, 2, 3).reshape(s // CHUNK, CHUNK, b, h, n)

    inputs = tuple(time_major(t) for t in (r_h, decay, k_h, v_h, kk, kk * a_h))

    def frame(state, inp):
        r_t, w_t, k_t, v_t, kk_t, kka_t = inp
        sa = jnp.einsum('bhvk,bhk->bhv', state, kk_t)
        state = (state * w_t[:, :, None, :] - sa[..., None] * kka_t[:, :, None, :]
                 + v_t[..., None] * k_t[:, :, None, :])
        return state, jnp.einsum('bhvk,bhk->bhv', state, r_t)

    def chunk(state, inp):
        return lax.scan(frame, state, inp)

    _, y = lax.scan(chunk, jnp.zeros((b, h, n, n), jnp.float32), inputs)
    y = y.reshape(s, b, h, n).transpose(1, 0, 2, 3)
    mean = jnp.mean(y, axis=-1, keepdims=True)
    var = jnp.mean(jnp.square(y - mean), axis=-1, keepdims=True)
    y = ((y - mean) * lax.rsqrt(var + RWKV_GN_EPS) * ln_w.astype(jnp.float32).reshape(h, n)
         + ln_b.astype(jnp.float32).reshape(h, n))
    y = y + jnp.sum(r_h * k_h * r_k.astype(jnp.float32), axis=-1, keepdims=True) * v_h
    return y.reshape(b, s, RWKV_WIDTH).astype(z.dtype) * g


def hybrid_mixer(u, w_in, b_forget, rwkv_mu, rwkv_w0, rwkv_w_up, rwkv_a0, rwkv_a_up, rwkv_g_up,
                 rwkv_k_k, rwkv_k_a, rwkv_r_k, rwkv_ln_w, rwkv_ln_b,
                 w_branch_a, w_branch_b, w_branch_c, w_out):
    proj = u @ w_in
    sb, fox, rw, gates = _split(proj, (3 * SB_WIDTH, 3 * FOX_WIDTH + FOX_HEADS, RWKV_IN,
                                       N_BRANCHES * D_MODEL))
    q_a, k_a, v_a = _split(sb, (SB_WIDTH,) * 3)
    y_a = _merge_heads(stick_breaking_attention(_heads(q_a, SB_HEADS), _heads(k_a, SB_HEADS),
                                                _heads(v_a, SB_HEADS)))
    q_b, k_b, v_b, f_b = _split(fox, (FOX_WIDTH,) * 3 + (FOX_HEADS,))
    log_f = jax.nn.log_sigmoid((f_b + b_forget).astype(jnp.float32)).transpose(0, 2, 1)
    y_b = _merge_heads(forgetting_attention(_heads(q_b, FOX_HEADS), _heads(k_b, FOX_HEADS),
                                            _heads(v_b, FOX_HEADS), log_f))
    y_c = rwkv7_time_mix(rw, rwkv_mu, rwkv_w0, rwkv_w_up, rwkv_a0, rwkv_a_up, rwkv_g_up,
                         rwkv_k_k, rwkv_k_a, rwkv_r_k, rwkv_ln_w, rwkv_ln_b)
    g_a, g_b, g_c = _split(jax.nn.sigmoid(gates), (D_MODEL,) * 3)
    m = g_a * (y_a @ w_branch_a) + g_b * (y_b @ w_branch_b) + g_c * (y_c @ w_branch_c)
    return m @ w_out


def setup_inputs(seed: int = 0) -> dict:
    key = jax.random.key(seed)
    ks = jax.random.split(key, 24)
    L, D = DEPTH, D_MODEL

    def nrm(k, shape, scale):
        return jax.random.normal(k, shape, jnp.float32) * scale

    return {
        'x': nrm(ks[0], (BATCH, SEQ, D), 1.0),
        'norm_mix_pre': 1.0 + nrm(ks[1], (L, D), 0.05),
        'norm_mix_post': 1.0 + nrm(ks[2], (L, D), 0.05),
        'norm_mlp_pre': 1.0 + nrm(ks[3], (L, D), 0.05),
        'norm_mlp_post': 1.0 + nrm(ks[4], (L, D), 0.05),
        'w_in': nrm(ks[5], (L, D, N_IN), D ** -0.5),
        'b_forget': 1.0 + nrm(ks[6], (L, FOX_HEADS), 0.5),
        'rwkv_mu': jax.random.uniform(ks[7], (L, RWKV_IN), jnp.float32),
        'rwkv_w0': nrm(ks[8], (L, RWKV_WIDTH), 0.5),
        'rwkv_w_up': nrm(ks[9], (L, DECAY_LORA, RWKV_WIDTH), 0.5 * DECAY_LORA ** -0.5),
        'rwkv_a0': nrm(ks[10], (L, RWKV_WIDTH), 0.1),
        'rwkv_a_up': nrm(ks[11], (L, AAA_LORA, RWKV_WIDTH), 0.5 * AAA_LORA ** -0.5),
        'rwkv_g_up': nrm(ks[12], (L, GATE_LORA, RWKV_WIDTH), GATE_LORA ** -0.5),
        'rwkv_k_k': 0.85 + nrm(ks[13], (L, RWKV_WIDTH), 0.05),
        'rwkv_k_a': 1.0 + nrm(ks[14], (L, RWKV_WIDTH), 0.05),
        'rwkv_r_k': nrm(ks[15], (L, RWKV_HEADS, RWKV_HEAD_DIM), 0.1),
        'rwkv_ln_w': 1.0 + nrm(ks[16], (L, RWKV_WIDTH), 0.05),
        'rwkv_ln_b': nrm(ks[17], (L, RWKV_WIDTH), 0.01),
        'w_branch_a': nrm(ks[18], (L, SB_WIDTH, D), SB_WIDTH ** -0.5),
        'w_branch_b': nrm(ks[19], (L, FOX_WIDTH, D), FOX_WIDTH ** -0.5),
        'w_branch_c': nrm(ks[20], (L, RWKV_WIDTH, D), RWKV_WIDTH ** -0.5),
        'w_out': nrm(ks[21], (L, D, D), D ** -0.5),
        'w_mlp_up': nrm(ks[22], (L, D, D_FF), D ** -0.5),
        'w_mlp_down': nrm(ks[23], (L, D_FF, D), D_FF ** -0.5),
    }


def reference(x, norm_mix_pre, norm_mix_post, norm_mlp_pre, norm_mlp_post, w_in, b_forget,
              rwkv_mu, rwkv_w0, rwkv_w_up, rwkv_a0, rwkv_a_up, rwkv_g_up, rwkv_k_k, rwkv_k_a,
              rwkv_r_k, rwkv_ln_w, rwkv_ln_b, w_branch_a, w_branch_b, w_branch_c, w_out,
              w_mlp_up, w_mlp_down):
    for l in range(DEPTH):
        u = _rms_norm(x, norm_mix_pre[l])
        mix = hybrid_mixer(u, w_in[l], b_forget[l], rwkv_mu[l], rwkv_w0[l], rwkv_w_up[l],
                           rwkv_a0[l], rwkv_a_up[l], rwkv_g_up[l], rwkv_k_k[l], rwkv_k_a[l],
                           rwkv_r_k[l], rwkv_ln_w[l], rwkv_ln_b[l], w_branch_a[l], w_branch_b[l],
                           w_branch_c[l], w_out[l])
        x = x + _rms_norm(mix, norm_mix_post[l])
        hdn = jnp.square(jax.nn.relu(_rms_norm(x, norm_mlp_pre[l]) @ w_mlp_up[l]))
        x = x + _rms_norm(hdn @ w_mlp_down[l], norm_mlp_post[l])
    return x
```

```python
import contextlib
import numpy as np
import concourse.bass as bass
import concourse.mybir as mybir
from concourse.bass_utils import run_bass_kernel_spmd

F32 = mybir.dt.float32
BF16 = mybir.dt.bfloat16
AF = mybir.ActivationFunctionType
ALU = mybir.AluOpType
AX = mybir.AxisListType


class Counter:
    def __init__(self, prog, name, step, epoch):
        self.prog, self.name, self.step, self.epoch = prog, name, step, epoch
        self.n = 0
        self.sems = []

    def next(self):
        self.n += 1
        ep = (self.n - 1) // self.epoch
        while len(self.sems) <= ep:
            self.sems.append(self.prog.new_sem(f"{self.name}_{len(self.sems)}"))
        return self.n

    def sem_val(self, n):
        ep = (n - 1) // self.epoch
        return self.sems[ep], ((n - 1) % self.epoch + 1) * self.step


class Buf:
    def __init__(self, t, name=""):
        self.t = t
        self.name = name
        self.last_write = None
        self.reads = {}

    def __getitem__(self, idx):
        return self.t[idx]

    def ap(self):
        return self.t.ap()


class Prog:
    ENGS = ("pe", "act", "dve", "pool", "sp")

    def __init__(self, nc):
        self.nc = nc
        self.root = contextlib.ExitStack()
        self.stacks = [self.root]
        self.eng = {"pe": nc.tensor, "act": nc.scalar, "dve": nc.vector,
                    "pool": nc.gpsimd, "sp": nc.sync}
        self.cnt = {e: Counter(self, "c" + e, 1, 30000) for e in self.ENGS}
        self.observed = {e: {} for e in self.ENGS}
        self.all_counters = list(self.cnt.values())
        self.n_inst = 0
        self.uid = 0

    def new_sem(self, name):
        return self.root.enter_context(self.nc.semaphore(name))

    def chan(self, name, step=16):
        self.uid += 1
        c = Counter(self, f"d{name}{self.uid}", step, 1800 if step == 16 else 30000)
        self.all_counters.append(c)
        return c

    @contextlib.contextmanager
    def scope(self):
        st = contextlib.ExitStack()
        self.stacks.append(st)
        try:
            yield
        finally:
            self.barrier()
            self.stacks.pop()
            st.close()

    def sbuf(self, name, shape, dtype):
        self.uid += 1
        t = self.stacks[-1].enter_context(
            self.nc.sbuf_tensor(f"{name}_{self.uid}", list(shape), dtype))
        return Buf(t, name)

    def psum(self, name, shape, dtype=F32):
        self.uid += 1
        t = self.stacks[-1].enter_context(
            self.nc.psum_tensor(f"{name}_{self.uid}", list(shape), dtype))
        return Buf(t, name)

    def dram(self, name, shape, dtype, kind="Internal"):
        return Buf(self.nc.dram_tensor(name, list(shape), dtype, kind=kind), name)

    def _need(self, engine, tok, waits):
        if tok is None:
            return
        c, n, teng = tok
        if teng == "pe" and engine == "pe":
            return
        if teng == "dma":
            n = c.n
        ob = self.observed[engine]
        if ob.get(id(c), 0) >= n:
            return
        ob[id(c)] = n
        waits.append((c, n))

    def op(self, engine, fn, reads=(), writes=(), chan=None):
        waits = []
        for b in reads:
            self._need(engine, b.last_write, waits)
        for b in writes:
            self._need(engine, b.last_write, waits)
            for t in b.reads.values():
                self._need(engine, t, waits)
        e = self.eng[engine]
        for c, n in waits:
            s, v = c.sem_val(n)
            e.wait_ge(s, v)
        ins = fn(e)
        c = chan if chan is not None else self.cnt[engine]
        n = c.next()
        s, _ = c.sem_val(n)
        ins.then_inc(s, c.step)
        tok = (c, n, engine if chan is None else "dma")
        for b in reads:
            b.reads[id(c)] = tok
        for b in writes:
            b.last_write = tok
            b.reads = {}
        self.n_inst += 1
        return tok

    def dma(self, queue, chan, out, in_, reads=(), writes=(), **kw):
        return self.op(queue, lambda e: e.dma_start(out=out, in_=in_, **kw),
                       reads=reads, writes=writes, chan=chan)

    def barrier(self):
        toks = [(c, c.n, "x") for c in self.all_counters if c.n > 0]
        for engine in self.ENGS:
            waits = []
            for t in toks:
                self._need(engine, t, waits)
            e = self.eng[engine]
            for c, n in waits:
                s, v = c.sem_val(n)
                e.wait_ge(s, v)

    def close(self):
        self.barrier()
        self.root.close()


D = 2048
T = 4096
KC = 16
L = 2
NCORES = 4
SCALE = 128.0 ** -0.5
EPS = 1e-6
NQK = 16
NRW = 27
V_NRM = 0
V_MU = 64
V_W0, V_A0, V_KK, V_KA, V_RK, V_LNW, V_LNB = 96, 104, 112, 120, 128, 136, 144
V_BF = 152
VEC_L = 288
C_ID = 0
C_ONES = 128
C_UTRI = 256
C_NEGT = 384
C_MUP = 512
C_MLO = 640
C_BLK = 768
C_RST = 896
C_FOXB = 1408
NCF = 1408 + 2048
C_SBM = NCF
NCONST = NCF + 2048
NCB = 896
EM05 = float(np.exp(-0.5))


class K:
    pass


def mm(P, ps, out_ap, lb, lhsT, rb, rhs, start, stop):
    P.op("pe", lambda e: e.matmul(out_ap, lhsT, rhs, start=start, stop=stop),
         reads=[lb, rb], writes=[ps])


def act(P, out_b, out_ap, in_b, in_ap, func, extra=(), **kw):
    P.op("act", lambda e: e.activation(out=out_ap, in_=in_ap, func=func, **kw),
         reads=[in_b] + list(extra), writes=[out_b])


def tt(P, eng, out_b, out_ap, a_b, a_ap, b_b, b_ap, op):
    P.op(eng, lambda e: e.tensor_tensor(out=out_ap, in0=a_ap, in1=b_ap, op=op),
         reads=[a_b, b_b], writes=[out_b])


def stt(P, eng, out_b, out_ap, a_b, a_ap, scalar, b_b, b_ap, op0, op1, extra=()):
    P.op(eng, lambda e: e.scalar_tensor_tensor(out=out_ap, in0=a_ap, scalar=scalar, in1=b_ap, op0=op0, op1=op1),
         reads=[a_b, b_b] + list(extra), writes=[out_b])


def ts(P, eng, out_b, out_ap, a_b, a_ap, s1, s2, op0, op1, extra=()):
    P.op(eng, lambda e: e.tensor_scalar(out=out_ap, in0=a_ap, scalar1=s1, scalar2=s2, op0=op0, op1=op1),
         reads=[a_b] + list(extra), writes=[out_b])


def cp(P, eng, out_b, out_ap, in_b, in_ap):
    if eng == "act":
        P.op("act", lambda e: e.copy(out=out_ap, in_=in_ap), reads=[in_b], writes=[out_b])
    else:
        P.op(eng, lambda e: e.tensor_copy(out=out_ap, in_=in_ap), reads=[in_b], writes=[out_b])


def load_consts(P, k):
    k.cf = P.sbuf("cf", [128, NCF], F32)
    k.cb = P.sbuf("cb", [128, NCB], BF16)
    k.sbm = P.sbuf("sbm", [128, 2048], BF16)
    k.vec = P.sbuf("vec", [128, L * VEC_L], F32)
    k.epsb = P.sbuf("epsb", [128, 4], F32)
    k.omka = P.sbuf("omka", [128, L * 8], F32)
    for i, v in enumerate((EPS, 1.0, 64e-5, 1e-24)):
        P.op("dve", lambda e: e.memset(k.epsb[:, i:i + 1], v), writes=[k.epsb])
    ch = P.chan("const")
    P.dma("sp", ch, k.cf[:], k.consts.ap()[:, 0:NCF], reads=[k.consts], writes=[k.cf])
    P.dma("pool", ch, k.cb[:], k.consts.ap()[:, 0:NCB], reads=[k.consts], writes=[k.cb])
    P.dma("pool", ch, k.sbm[:], k.consts.ap()[:, C_SBM:C_SBM + 2048], reads=[k.consts], writes=[k.sbm])
    P.dma("sp", ch, k.vec[:], k.vecs.ap(), reads=[k.vecs], writes=[k.vec])
    for l in range(L):
        o = l * VEC_L + V_KA
        ts(P, "dve", k.omka, k.omka[:, l * 8:l * 8 + 8], k.vec, k.vec[:, o:o + 8], -1.0, 1.0, ALU.mult, ALU.add)


def rms_stats(P, k, src, sq, ps, rs):
    for g in range(4):
        act(P, sq, sq[:, 4 * g:4 * g + 4, :], src, src[:, 4 * g:4 * g + 4, :], AF.Square)
    for c in range(KC):
        mm(P, ps, ps[:], k.cb, k.cb[:, C_ONES:C_ONES + 128], sq, sq[:, c, :], c == 0, c == KC - 1)
    act(P, rs, rs[:], ps, ps[:], AF.Ln, extra=[k.epsb], scale=1.0 / D, bias=k.epsb[:, 0:1])
    act(P, rs, rs[:], rs, rs[:], AF.Exp, scale=-0.5)


def phase_norm(P, k, src, gain_col, dst):
    with P.scope():
        xs = [P.sbuf("nx", [128, KC, 512], F32) for _ in range(2)]
        sq = [P.sbuf("nsq", [128, KC, 512], BF16) for _ in range(2)]
        us = [P.sbuf("nu", [128, KC, 512], BF16) for _ in range(2)]
        rs = [P.sbuf("nr", [128, 512], F32) for _ in range(2)]
        ps = [P.psum("nps", [128, 512]) for _ in range(2)]
        chl = [P.chan("nl") for _ in range(2)]
        chs = [P.chan("ns") for _ in range(2)]
        sv = src.ap().rearrange("(c p) t -> p c t", p=128)
        dv = dst.ap().rearrange("(c p) t -> p c t", p=128)
        nb = T // 512

        def load(tb):
            i = tb % 2
            for g in range(4):
                P.dma("sp", chl[i], xs[i][:, 4 * g:4 * g + 4, :],
                      sv[:, 4 * g:4 * g + 4, tb * 512:(tb + 1) * 512], reads=[src], writes=[xs[i]])
        load(0)
        for tb in range(nb):
            i = tb % 2
            if tb + 1 < nb:
                load(tb + 1)
            rms_stats(P, k, xs[i], sq[i], ps[i], rs[i])
            for c in range(KC):
                stt(P, "dve", us[i], us[i][:, c, :], xs[i], xs[i][:, c, :], k.vec[:, gain_col + c:gain_col + c + 1],
                    rs[i], rs[i][:], ALU.mult, ALU.mult, extra=[k.vec])
            for g in range(4):
                P.dma("sp", chs[i], dv[:, 4 * g:4 * g + 4, tb * 512:(tb + 1) * 512],
                      us[i][:, 4 * g:4 * g + 4, :], reads=[us[i]], writes=[dst])


def phase_b1(P, k, l):
    with P.scope():
        uT = P.sbuf("uT", [128, KC, T], BF16)
        wt = [P.sbuf("wt", [128, KC, 512], BF16) for _ in range(2)]
        wft = P.sbuf("wft", [128, KC, 4], BF16)
        qst = [P.sbuf("qst", [128, 512], BF16) for _ in range(2)]
        zst = [P.sbuf("zst", [128, 512], F32) for _ in range(2)]
        vst = [P.sbuf("vst", [128, 512], BF16) for _ in range(2)]
        zraw = [P.sbuf("zraw", [128, 513], F32) for _ in range(2)]
        dt = P.sbuf("dt", [128, 512], F32)
        fsb = P.sbuf("fsb", [128, 128], F32)
        fsp = P.sbuf("fsp", [128, 128], F32)
        tot = P.sbuf("tot", [128, 128], F32)
        pre = P.sbuf("pre", [128, 128], F32)
        cT = P.sbuf("cT", [32, 4, 128], F32)
        ps = [P.psum("b1ps", [128, 512]) for _ in range(4)]
        psf = P.psum("psf", [128, 128])
        psc = P.psum("psc", [128, 128])
        pst = P.psum("pst", [128, 128])
        chu = P.chan("u")
        chw = [P.chan("w") for _ in range(2)]
        chq = [P.chan("q") for _ in range(2)]
        chz = [P.chan("z") for _ in range(2)]
        chv = [P.chan("v") for _ in range(2)]
        chm = P.chan("m")

        uv = k.uT.ap().rearrange("(c p) t -> p c t", p=128)
        for hh in range(2):
            for g in range(4):
                P.dma("sp", chu, uT[:, 4 * g:4 * g + 4, hh * 2048:(hh + 1) * 2048],
                      uv[:, 4 * g:4 * g + 4, hh * 2048:(hh + 1) * 2048], reads=[k.uT], writes=[uT])
        P.dma("pool", chm, wft[:].rearrange("p c n -> p (c n)"), k.wf.ap()[l], reads=[k.wf], writes=[wft])

        tiles = [("qk", k.wqk, i) for i in range(4)] + [("v", k.wv, i) for i in range(2)] + \
                [("rw", k.wrw, i) for i in range(7)]

        def loadw(idx):
            kind, src, ti = tiles[idx]
            i = idx % 2
            P.dma("pool", chw[i], wt[i][:].rearrange("p c n -> p (c n)"), src.ap()[l, ti], reads=[src], writes=[wt[i]])
        loadw(0)
        pr = [0]
        zi = [0]
        for idx, (kind, src, ti) in enumerate(tiles):
            w = wt[idx % 2]
            if idx + 1 < len(tiles):
                loadw(idx + 1)
            if kind == "qk":
                for j in range(4):
                    nci = ti * 4 + j
                    isq = ti in (0, 2)
                    for tb in range(8):
                        p_ = ps[pr[0] % 4]
                        pr[0] += 1
                        for c in range(KC):
                            mm(P, p_, p_[:], w, w[:, c, j * 128:(j + 1) * 128], uT, uT[:, c, tb * 512:(tb + 1) * 512],
                               c == 0, c == KC - 1)
                        st = qst[tb % 2]
                        act(P, st, st[:], p_, p_[:], AF.Copy, scale=SCALE if isq else 1.0)
                        P.dma("sp", chq[tb % 2], k.qkT.ap()[nci, :, tb * 512:(tb + 1) * 512], st[:],
                              reads=[st], writes=[k.qkT])
            elif kind == "v":
                vview = k.vs.ap()[ti * 4:ti * 4 + 4].rearrange("h p s d -> p s h d")
                for s in range(32):
                    p_ = ps[pr[0] % 4]
                    pr[0] += 1
                    for c in range(KC):
                        mm(P, p_, p_[:], uT, uT[:, c, s * 128:(s + 1) * 128], w, w[:, c, :], c == 0, c == KC - 1)
                    st = vst[s % 2]
                    cp(P, "dve", st, st[:], p_, p_[:])
                    P.dma("sp", chv[s % 2], vview[:, s, :, :], st[:].rearrange("p (h d) -> p h d", h=4),
                          reads=[st], writes=[k.vs])
                if ti == 1:
                    for s in range(32):
                        for c in range(KC):
                            mm(P, psf, psf[:, 4 * s:4 * s + 4], uT, uT[:, c, s * 128:(s + 1) * 128], wft, wft[:, c, :],
                               c == 0, c == KC - 1)
                    vb = l * VEC_L + V_BF
                    tt(P, "dve", fsb, fsb[:], psf, psf[:], k.vec, k.vec[:, vb:vb + 128], ALU.add)
                    act(P, fsp, fsp[:], fsb, fsb[:], AF.Exp, scale=-1.0)
                    act(P, fsp, fsp[:], fsp, fsp[:], AF.Ln, extra=[k.epsb], bias=k.epsb[:, 1:2])
                    mm(P, psc, psc[:], k.cf, k.cf[:, C_UTRI:C_UTRI + 128], fsp, fsp[:], True, True)
                    mm(P, pst, pst[:], k.cf, k.cf[:, C_ONES:C_ONES + 128], fsp, fsp[:], True, True)
                    cp(P, "dve", tot, tot[:], pst, pst[:])
                    P.op("dve", lambda e: e.memset(pre[:, 0:4], 0.0), writes=[pre])
                    for s in range(1, 32):
                        tt(P, "dve", pre, pre[:, 4 * s:4 * s + 4], pre, pre[:, 4 * s - 4:4 * s],
                           tot, tot[:, 4 * s - 4:4 * s], ALU.add)
                    tt(P, "dve", k.csum, k.csum[:].rearrange("p h s -> p s h"),
                       psc, psc[:].rearrange("p (s h) -> p s h", h=4),
                       pre, pre[:].rearrange("p (s h) -> p s h", h=4), ALU.add)
                    for h in range(4):
                        mm(P, ps[h], ps[h][0:32, 0:128], k.csum, k.csum[:, h, :], k.cf, k.cf[:, C_ID:C_ID + 128],
                           True, True)
                        act(P, cT, cT[:, h, :], ps[h], ps[h][0:32, 0:128], AF.Copy, scale=-1.0)
                    P.dma("sp", chm, k.cdram.ap().rearrange("h (s p) -> s h p", p=128), cT[:],
                          reads=[cT], writes=[k.cdram])
            else:
                nj = 4 if ti < 6 else 3
                for j in range(nj):
                    nci = ti * 4 + j
                    M = 32 if nci == 26 else 128
                    mcol = l * VEC_L + V_MU + nci
                    for tb in range(8):
                        p_ = ps[pr[0] % 4]
                        pr[0] += 1
                        for c in range(KC):
                            mm(P, p_, p_[0:M, :], w, w[:, c, j * 128:j * 128 + M], uT, uT[:, c, tb * 512:(tb + 1) * 512],
                               c == 0, c == KC - 1)
                        zr = zraw[zi[0] % 2]
                        zp = zraw[(zi[0] + 1) % 2]
                        zi[0] += 1
                        if tb == 0:
                            P.op("dve", lambda e: e.memset(zr[0:M, 0:1], 0.0), writes=[zr])
                        else:
                            cp(P, "act", zr, zr[0:M, 0:1], zp, zp[0:M, 512:513])
                        cp(P, "act", zr, zr[0:M, 1:513], p_, p_[0:M, :])
                        tt(P, "dve", dt, dt[0:M, :], zr, zr[0:M, 0:512], zr, zr[0:M, 1:513], ALU.subtract)
                        st = zst[tb % 2]
                        stt(P, "dve", st, st[0:M, :], dt, dt[0:M, :],
                            k.vec[0:M, mcol:mcol + 1], zr, zr[0:M, 1:513], ALU.mult, ALU.add, extra=[k.vec])
                        P.dma("sp", chz[tb % 2], k.zsT.ap()[nci * 128:nci * 128 + M, tb * 512:(tb + 1) * 512],
                              st[0:M, :], reads=[st], writes=[k.zsT])


def phase_b2(P, k, l):
    with P.scope():
        qT = [P.sbuf("qT", [128, T], BF16) for _ in range(2)]
        kT = [P.sbuf("kT", [128, T], BF16) for _ in range(2)]
        vv = [P.sbuf("vv", [128, 32, 128], BF16) for _ in range(2)]
        cq = P.sbuf("cq", [128, T], F32)
        tts = [P.sbuf("tt", [128, 512], F32) for _ in range(3)]
        t2s = [P.sbuf("t2", [128, 512], F32) for _ in range(2)]
        sps = [P.sbuf("sp", [128, 512], F32) for _ in range(2)]
        spb = [P.sbuf("spb", [128, 512], BF16) for _ in range(2)]
        pps = [P.sbuf("pp", [128, 512], BF16) for _ in range(3)]
        Rt = P.sbuf("Rt", [128, 512], F32)
        rec = P.sbuf("rec", [128, 512], F32)
        yst = [P.sbuf("yst", [128, 512], BF16) for _ in range(2)]
        pS = [P.psum("pS", [128, 512]) for _ in range(2)]
        pB = [P.psum("pB", [128, 512]) for _ in range(2)]
        pC = [P.psum("pC", [128, 512]) for _ in range(2)]
        pO = [P.psum("pO", [128, 512]) for _ in range(2)]
        chl = [P.chan("al") for _ in range(2)]
        chc = P.chan("ac")
        chy = [P.chan("ay") for _ in range(2)]
        heads = [("sb", h) for h in range(4)] + [("fox", h) for h in range(4)]

        def load(hi):
            kind, h = heads[hi]
            i = hi % 2
            base = 0 if kind == "sb" else 8
            P.dma("sp", chl[i], qT[i][:], k.qkT.ap()[base + h], reads=[k.qkT], writes=[qT[i]])
            P.dma("sp", chl[i], kT[i][:], k.qkT.ap()[base + 4 + h], reads=[k.qkT], writes=[kT[i]])
            P.dma("sp", chl[i], vv[i][:], k.vs.ap()[(0 if kind == "sb" else 4) + h], reads=[k.vs], writes=[vv[i]])
        load(0)
        it = [0]
        yi = [0]
        ones = k.cb[:, C_ONES:C_ONES + 128]
        for hi, (kind, h) in enumerate(heads):
            i = hi % 2
            if hi + 1 < len(heads):
                load(hi + 1)
            q_, k_, v_ = qT[i], kT[i], vv[i]
            if kind == "fox":
                P.dma("sp", chc, cq[:], k.cdram.ap()[h:h + 1, :].partition_broadcast(128), reads=[k.cdram], writes=[cq])
            for QB in range(8):
                nkb = 4 * QB + 4
                O = pO[QB % 2]
                qs = slice(QB * 512, (QB + 1) * 512)
                if kind == "fox":
                    DN = pC[QB % 2]
                    for kb in range(nkb):
                        n = it[0]
                        it[0] += 1
                        S = pS[n % 2]
                        mm(P, S, S[:], k_, k_[:, kb * 128:(kb + 1) * 128], q_, q_[:, qs], True, True)
                        t_ = tts[n % 3]
                        stt(P, "dve", t_, t_[:], S, S[:], k.csum[:, h, kb:kb + 1], cq, cq[:, qs], ALU.add, ALU.add,
                            extra=[k.csum])
                        if kb >= 4 * QB:
                            j = kb - 4 * QB
                            tt(P, "pool", t_, t_[:], t_, t_[:], k.cf, k.cf[:, C_FOXB + j * 512:C_FOXB + (j + 1) * 512],
                               ALU.add)
                        p_ = pps[n % 3]
                        act(P, p_, p_[:], t_, t_[:], AF.Exp)
                        mm(P, O, O[:], v_, v_[:, kb, :], p_, p_[:], kb == 0, kb == nkb - 1)
                        mm(P, DN, DN[:], k.cb, ones, p_, p_[:], kb == 0, kb == nkb - 1)
                    P.op("dve", lambda e: e.reciprocal(out=rec[:], in_=DN[:]), reads=[DN], writes=[rec])
                    ys = yst[yi[0] % 2]
                    tt(P, "dve", ys, ys[:], O, O[:], rec, rec[:], ALU.mult)
                    row0 = 512 + h * 128
                else:
                    P.op("pool", lambda e: e.memset(Rt[:], 0.0), writes=[Rt])
                    for kb in reversed(range(nkb)):
                        n = it[0]
                        it[0] += 1
                        S = pS[n % 2]
                        Bp = pB[n % 2]
                        Cp = pC[n % 2]
                        mm(P, S, S[:], k_, k_[:, kb * 128:(kb + 1) * 128], q_, q_[:, qs], True, True)
                        sp_ = sps[n % 2]
                        sb_ = spb[n % 2]
                        t1 = tts[n % 3]
                        t2 = t2s[n % 2]
                        act(P, t1, t1[:], S, S[:], AF.Exp)
                        act(P, sp_, sp_[:], t1, t1[:], AF.Ln, extra=[k.epsb], bias=k.epsb[:, 1:2])
                        diag = kb >= 4 * QB
                        if diag:
                            j = kb - 4 * QB
                            msk = k.sbm[:, j * 512:(j + 1) * 512]
                            tt(P, "pool", sb_, sb_[:], sp_, sp_[:], k.sbm, msk, ALU.mult)
                        else:
                            cp(P, "pool", sb_, sb_[:], sp_, sp_[:])
                        mm(P, Bp, Bp[:], k.cb, k.cb[:, C_NEGT:C_NEGT + 128], sb_, sb_[:], True, True)
                        mm(P, Cp, Cp[:], k.cb, ones, sb_, sb_[:], True, True)
                        tt(P, "dve", t1, t1[:], S, S[:], sp_, sp_[:], ALU.subtract)
                        tt(P, "dve", t2, t2[:], Bp, Bp[:], Rt, Rt[:], ALU.add)
                        tt(P, "pool", t1, t1[:], t1, t1[:], t2, t2[:], ALU.add)
                        p_ = pps[n % 3]
                        act(P, p_, p_[:], t1, t1[:], AF.Exp)
                        if diag:
                            tt(P, "pool", p_, p_[:], p_, p_[:], k.sbm, msk, ALU.mult)
                        mm(P, O, O[:], v_, v_[:, kb, :], p_, p_[:], kb == nkb - 1, kb == 0)
                        tt(P, "dve", Rt, Rt[:], Rt, Rt[:], Cp, Cp[:], ALU.subtract)
                    ys = yst[yi[0] % 2]
                    cp(P, "act", ys, ys[:], O, O[:])
                    row0 = h * 128
                P.dma("sp", chy[yi[0] % 2], k.yT.ap()[row0:row0 + 128, qs], ys[:], reads=[ys], writes=[k.yT])
                yi[0] += 1


class _Stop(Exception):
    pass


def phase_b3(P, k, l):
    try:
        _phase_b3(P, k, l)
    except _Stop:
        pass


def _phase_b3(P, k, l):
    import os
    STOP = int(os.environ.get("B3_STOP", "99"))

    def stop(n):
        if STOP <= n:
            raise _Stop()
    with P.scope():
        vo = l * VEC_L
        NL = 6
        ld = {n: [P.sbuf("ld" + n, [128, 512], F32) for _ in range(2)] for n in ("r", "k", "v", "wa", "g0", "g1")}
        f32n = ("lw", "a", "kk", "tmp", "kp", "kka", "cl", "ex", "e4", "bv", "sq", "gT")
        f = {n: P.sbuf("f" + n, [128, 512], F32) for n in f32n}
        b16n = ("thad", "sg", "sg1", "rt", "kt", "bt", "kap", "Kp", "Bp", "vb")
        b = {n: P.sbuf("b" + n, [128, 512], BF16) for n in b16n}
        WL = P.sbuf("WL", [128, 4], F32)
        tm = {n: P.sbuf("tm" + n, [128, 4, 128], BF16) for n in ("V", "K", "B")}
        wa_t = P.sbuf("wa_t", [128, 128], BF16)
        g0_t = P.sbuf("g0_t", [128, 128], BF16)
        g1_t = P.sbuf("g1_t", [32, 128], BF16)
        def m16(n):
            return [[P.sbuf(n, [128, 128], BF16) for _ in range(2)] for _ in range(2)]
        A1T, B1T, B2T, TIV = m16("A1T"), m16("B1T"), m16("B2T"), m16("TIV")
        Mp = [P.sbuf("Mp", [128, 128], BF16) for _ in range(2)]
        Np = [P.sbuf("Np", [128, 128], BF16) for _ in range(2)]
        R32 = P.sbuf("R32", [128, 128], F32)
        R16 = [P.sbuf("R16", [128, 128], BF16) for _ in range(2)]
        S32 = P.sbuf("S32", [128, 128], F32)
        S16 = P.sbuf("S16", [128, 128], BF16)
        Gsb = P.sbuf("Gsb", [128, 128], BF16)
        nP = P.sbuf("nP", [128, 128], BF16)
        ysb = P.sbuf("ysb", [128, 128], F32)
        ysq = P.sbuf("ysq", [128, 128], F32)
        yn = P.sbuf("yn", [128, 128], F32)
        st1 = P.sbuf("st1", [128, 8], F32)
        o1 = P.sbuf("o1", [128, 128], F32)
        yst = [P.sbuf("yst3", [128, 512], BF16) for _ in range(2)]
        pbig = [P.psum("pbig", [128, 512]) for _ in range(2)]
        ptr_t = [P.psum("ptr", [128, 512]) for _ in range(2)]
        ptr = [Buf(ptr_t[i][:, 0:128], f"ptr{i}") for i in range(2)]
        pm_t = [P.psum("pm", [128, 512]) for _ in range(2)]
        pm = [Buf(pm_t[i][:, 0:128], f"pm{i}") for i in range(2)]
        p3_t = [P.psum("p3", [128, 512]) for _ in range(2)]
        p3 = [Buf(p3_t[i][:, 0:128], f"p3{i}") for i in range(2)]
        chl = [P.chan("rl") for _ in range(2)]
        chw = P.chan("rw")
        chy = [P.chan("ry") for _ in range(2)]
        cnt = {"big": 0, "tr": 0, "pm": 0, "p3": 0, "y": 0}

        def nxt(kind, lst):
            i = cnt[kind]
            cnt[kind] += 1
            return lst[i % len(lst)]
        idf = k.cf[:, C_ID:C_ID + 128]
        idb = k.cb[:, C_ID:C_ID + 128]
        blk = k.cf[:, C_BLK:C_BLK + 128]
        mup = k.cf[:, C_MUP:C_MUP + 128]
        mlo = k.cf[:, C_MLO:C_MLO + 128]
        mui = k.cf[:, C_UTRI:C_UTRI + 128]

        def load(idx):
            hp, tb = idx // 8, idx % 8
            i = idx % 2
            cs = slice(tb * 512, (tb + 1) * 512)
            for n, nci, M in (("r", hp, 128), ("k", 8 + hp, 128), ("v", 16 + hp, 128), ("wa", 24, 128),
                              ("g0", 25, 128), ("g1", 26, 32)):
                P.dma("sp", chl[i], ld[n][i][0:M, :], k.zsT.ap()[nci * 128:nci * 128 + M, cs],
                      reads=[k.zsT], writes=[ld[n][i]])
        load(0)
        import os
        NHP = int(os.environ.get("B3_HP", "8"))
        NTB = int(os.environ.get("B3_TB", "8"))
        for hp in range(NHP):
            hc = slice(hp * 128, (hp + 1) * 128)
            P.dma("pool", chw, wa_t[0:64, :], k.wup.ap()[l, :, hc], reads=[k.wup], writes=[wa_t])
            P.dma("pool", chw, wa_t[64:128, :], k.aup.ap()[l, :, hc], reads=[k.aup], writes=[wa_t])
            P.dma("pool", chw, g0_t[:], k.gup.ap()[l, 0:128, hc], reads=[k.gup], writes=[g0_t])
            P.dma("pool", chw, g1_t[:], k.gup.ap()[l, 128:160, hc], reads=[k.gup], writes=[g1_t])
            P.op("dve", lambda e: e.memset(S32[:], 0.0), writes=[S32])
            P.op("dve", lambda e: e.memset(S16[:], 0.0), writes=[S16])

            def vc(col):
                return k.vec[:, vo + col + hp:vo + col + hp + 1]
            for tb in range(NTB):
                idx = hp * 8 + tb
                i = idx % 2
                if tb + 1 < NTB or hp + 1 < NHP:
                    load(idx + 1 if tb + 1 < NTB else (hp + 1) * 8)
                r_, k_, v_, wa_, g0_, g1_ = (ld[n][i] for n in ("r", "k", "v", "wa", "g0", "g1"))
                act(P, b["thad"], b["thad"][0:64, :], wa_, wa_[0:64, :], AF.Tanh)
                cp(P, "pool", b["thad"], b["thad"][64:128, :], wa_, wa_[64:128, :])
                act(P, b["sg"], b["sg"][:], g0_, g0_[:], AF.Sigmoid)
                act(P, b["sg1"], b["sg1"][0:32, :], g1_, g1_[0:32, :], AF.Sigmoid)
                pw = nxt("big", pbig)
                mm(P, pw, pw[:], wa_t, wa_t[0:64, :], b["thad"], b["thad"][0:64, :], True, True)
                act(P, f["lw"], f["lw"][:], pw, pw[:], AF.Sigmoid, extra=[k.vec], bias=vc(V_W0))
                pa = nxt("big", pbig)
                mm(P, pa, pa[:], wa_t, wa_t[64:128, :], b["thad"], b["thad"][64:128, :], True, True)
                act(P, f["a"], f["a"][:], pa, pa[:], AF.Sigmoid, extra=[k.vec], bias=vc(V_A0))
                pg = nxt("big", pbig)
                mm(P, pg, pg[:], g0_t, g0_t[:], b["sg"], b["sg"][:], True, False)
                mm(P, pg, pg[:], g1_t, g1_t[0:32, :], b["sg1"], b["sg1"][0:32, :], False, True)
                cp(P, "act", f["gT"], f["gT"][:], pg, pg[:])
                stop(1)
                P.op("pool", lambda e: e.tensor_scalar_mul(out=f["lw"][:], in0=f["lw"][:], scalar1=-EM05),
                     reads=[f["lw"]], writes=[f["lw"]])
                P.op("dve", lambda e: e.tensor_scalar_mul(out=f["kk"][:], in0=k_[:], scalar1=vc(V_KK)),
                     reads=[k_, k.vec], writes=[f["kk"]])
                ts(P, "dve", f["tmp"], f["tmp"][:], f["a"], f["a"][:], vc(V_KA), k.omka[:, l * 8 + hp:l * 8 + hp + 1],
                   ALU.mult, ALU.add, extra=[k.vec, k.omka])
                tt(P, "dve", f["kp"], f["kp"][:], k_, k_[:], f["tmp"], f["tmp"][:], ALU.mult)
                tt(P, "pool", f["sq"], f["sq"][:], f["kk"], f["kk"][:], f["kk"], f["kk"][:], ALU.mult)
                pq = nxt("big", pbig)
                mm(P, pq, pq[:], k.cf, blk, f["sq"], f["sq"][:], True, True)
                act(P, f["ex"], f["ex"][:], pq, pq[:], AF.Ln, extra=[k.epsb], bias=k.epsb[:, 3:4])
                act(P, f["ex"], f["ex"][:], f["ex"], f["ex"][:], AF.Exp, scale=-0.5)
                tt(P, "dve", f["kk"], f["kk"][:], f["kk"], f["kk"][:], f["ex"], f["ex"][:], ALU.mult)
                tt(P, "pool", f["kka"], f["kka"][:], f["kk"], f["kk"][:], f["a"], f["a"][:], ALU.mult)
                tt(P, "dve", f["tmp"], f["tmp"][:], r_, r_[:], f["kp"], f["kp"][:], ALU.mult)
                P.op("pool", lambda e: e.tensor_scalar_mul(out=f["sq"][:], in0=f["tmp"][:], scalar1=vc(V_RK)),
                     reads=[f["tmp"], k.vec], writes=[f["sq"]])
                pb_ = nxt("big", pbig)
                mm(P, pb_, pb_[:], k.cf, blk, f["sq"], f["sq"][:], True, True)
                tt(P, "dve", f["bv"], f["bv"][:], pb_, pb_[:], v_, v_[:], ALU.mult)
                cp(P, "pool", b["vb"], b["vb"][:], v_, v_[:])
                stop(2)
                P.op("dve", lambda e: e.tensor_tensor_scan(out=f["cl"][:], data0=k.cf[:, C_RST:C_RST + 512],
                                                           data1=f["lw"][:], initial=0.0, op0=ALU.mult, op1=ALU.add),
                     reads=[k.cf, f["lw"]], writes=[f["cl"]])
                act(P, f["ex"], f["ex"][:], f["cl"], f["cl"][:], AF.Exp)
                tt(P, "dve", b["rt"], b["rt"][:], r_, r_[:], f["ex"], f["ex"][:], ALU.mult)
                act(P, f["ex"], f["ex"][:], f["cl"], f["cl"][:], AF.Exp, scale=-1.0)
                tt(P, "dve", b["kt"], b["kt"][:], f["kp"], f["kp"][:], f["ex"], f["ex"][:], ALU.mult)
                tt(P, "pool", b["bt"], b["bt"][:], f["kka"], f["kka"][:], f["ex"], f["ex"][:], ALU.mult)
                tt(P, "dve", f["tmp"], f["tmp"][:], f["cl"], f["cl"][:], f["lw"], f["lw"][:], ALU.subtract)
                act(P, f["ex"], f["ex"][:], f["tmp"], f["tmp"][:], AF.Exp)
                tt(P, "dve", b["kap"], b["kap"][:], f["kk"], f["kk"][:], f["ex"], f["ex"][:], ALU.mult)
                for c in range(4):
                    act(P, f["e4"], f["e4"][:, c * 128:(c + 1) * 128], f["cl"], f["cl"][:, c * 128:(c + 1) * 128],
                        AF.Exp, scale=-1.0, bias=f["cl"][:, c * 128 + 127:c * 128 + 128])
                    act(P, WL, WL[:, c:c + 1], f["cl"], f["cl"][:, c * 128 + 127:c * 128 + 128], AF.Exp)
                tt(P, "dve", b["Kp"], b["Kp"][:], f["kp"], f["kp"][:], f["e4"], f["e4"][:], ALU.mult)
                tt(P, "pool", b["Bp"], b["Bp"][:], f["kka"], f["kka"][:], f["e4"], f["e4"][:], ALU.mult)
                stop(3)
                TRN = os.environ.get("B3_TRN", "VKBe")
                for c in range(4):
                    for n, src_ in (("V", b["vb"]), ("K", b["Kp"]), ("B", b["Bp"])):
                        if n not in TRN:
                            continue
                        pt = nxt("tr", ptr)
                        mm(P, pt, pt[:], src_, src_[:, c * 128:(c + 1) * 128], k.cb, idb, True, True)
                        if "e" in TRN:
                            cp(P, "act" if n == "V" else "dve", tm[n], tm[n][:, c, :], pt, pt[:])
                stop(4)
                for c in range(4):
                    cc = slice(c * 128, (c + 1) * 128)
                    par = c % 2
                    for e in range(2):
                        er = slice(e * 64, (e + 1) * 64)
                        p1 = nxt("pm", pm)
                        mm(P, p1, p1[:], b["kt"], b["kt"][er, cc], b["kap"], b["kap"][er, cc], True, True)
                        tt(P, "dve", A1T[par][e], A1T[par][e][:], p1, p1[:], k.cf, mup, ALU.mult)
                        p2 = nxt("pm", pm)
                        mm(P, p2, p2[:], b["bt"], b["bt"][er, cc], b["kap"], b["kap"][er, cc], True, True)
                        stt(P, "dve", Mp[0], Mp[0][:], p2, p2[:], -1.0, k.cf, mup, ALU.mult, ALU.mult)
                        p3_ = nxt("pm", pm)
                        mm(P, p3_, p3_[:], b["kap"], b["kap"][er, cc], b["bt"], b["bt"][er, cc], True, True)
                        stt(P, "dve", Np[0], Np[0][:], p3_, p3_[:], -1.0, k.cf, mlo, ALU.mult, ALU.mult)
                        p4 = nxt("pm", pm)
                        mm(P, p4, p4[:], b["kt"], b["kt"][er, cc], b["rt"], b["rt"][er, cc], True, True)
                        tt(P, "dve", B1T[par][e], B1T[par][e][:], p4, p4[:], k.cf, mui, ALU.mult)
                        p5 = nxt("pm", pm)
                        mm(P, p5, p5[:], b["bt"], b["bt"][er, cc], b["rt"], b["rt"][er, cc], True, True)
                        tt(P, "dve", B2T[par][e], B2T[par][e][:], p5, p5[:], k.cf, mui, ALU.mult)
                        tt(P, "pool", R32, R32[:], Mp[0], Mp[0][:], k.cf, idf, ALU.add)
                        tt(P, "pool", R16[0], R16[0][:], Mp[0], Mp[0][:], k.cf, idf, ALU.add)
                        cur = 0
                        for it_ in range(NL):
                            nx_ = 1 - cur
                            pn = nxt("pm", pm)
                            mm(P, pn, pn[:], Mp[cur], Mp[cur][:], Np[cur], Np[cur][:], True, True)
                            cp(P, "act", Np[nx_], Np[nx_][:], pn, pn[:])
                            if it_ < NL - 1:
                                pm_ = nxt("pm", pm)
                                mm(P, pm_, pm_[:], Np[cur], Np[cur][:], Mp[cur], Mp[cur][:], True, True)
                                cp(P, "act", Mp[nx_], Mp[nx_][:], pm_, pm_[:])
                            pr_ = nxt("pm", pm)
                            mm(P, pr_, pr_[:], Np[nx_], Np[nx_][:], R16[cur], R16[cur][:], True, True)
                            tt(P, "dve", R32, R32[:], R32, R32[:], pr_, pr_[:], ALU.add)
                            last = it_ == NL - 1
                            dst = TIV[par][e] if last else R16[nx_]
                            cp(P, "pool", dst, dst[:], R32, R32[:])
                            cur = nx_
                    stop(5)
                    G = nxt("p3", p3)
                    mm(P, G, G[:], b["kap"], b["kap"][:, cc], S16, S16[:], True, False)
                    for e in range(2):
                        ec = slice(e * 64, (e + 1) * 64)
                        mm(P, G, G[:, ec], A1T[par][e], A1T[par][e][:], tm["V"], tm["V"][:, c, ec], False, e == 1)
                    cp(P, "act", Gsb, Gsb[:], G, G[:])
                    Pp = nxt("p3", p3)
                    for e in range(2):
                        ec = slice(e * 64, (e + 1) * 64)
                        mm(P, Pp, Pp[:, ec], TIV[par][e], TIV[par][e][:], Gsb, Gsb[:, ec], True, True)
                    act(P, nP, nP[:], Pp, Pp[:], AF.Copy, scale=-1.0)
                    Y = nxt("p3", p3)
                    mm(P, Y, Y[:], b["rt"], b["rt"][:, cc], S16, S16[:], True, False)
                    for e in range(2):
                        ec = slice(e * 64, (e + 1) * 64)
                        mm(P, Y, Y[:, ec], B1T[par][e], B1T[par][e][:], tm["V"], tm["V"][:, c, ec], False, False)
                        mm(P, Y, Y[:, ec], B2T[par][e], B2T[par][e][:], nP, nP[:, ec], False, e == 1)
                    U = nxt("p3", p3)
                    mm(P, U, U[:], tm["K"], tm["K"][:, c, :], tm["V"], tm["V"][:, c, :], True, False)
                    mm(P, U, U[:], tm["B"], tm["B"][:, c, :], nP, nP[:], False, True)
                    for e in range(2):
                        er = slice(e * 64, (e + 1) * 64)
                        stt(P, "dve", S32, S32[er, er], S32, S32[er, er], WL[er, c:c + 1], U, U[er, er],
                            ALU.mult, ALU.add, extra=[WL])
                        cp(P, "pool", S16, S16[er, er], S32, S32[er, er])
                    stop(6)
                    cp(P, "act", ysb, ysb[:], Y, Y[:])
                    y3 = ysb[:].rearrange("p (e v) -> p e v", e=2)
                    P.op("dve", lambda e_: e_.reduce_sum(out=st1[:, 0:2], in_=y3, axis=AX.X), reads=[ysb], writes=[st1])
                    tt(P, "pool", ysq, ysq[:], ysb, ysb[:], ysb, ysb[:], ALU.mult)
                    P.op("dve", lambda e_: e_.reduce_sum(out=st1[:, 2:4], in_=ysq[:].rearrange("p (e v) -> p e v", e=2),
                                                         axis=AX.X), reads=[ysq], writes=[st1])
                    P.op("dve", lambda e_: e_.tensor_scalar_mul(out=st1[:, 0:2], in0=st1[:, 0:2], scalar1=1.0 / 64),
                         reads=[st1], writes=[st1])
                    tt(P, "dve", st1, st1[:, 4:6], st1, st1[:, 0:2], st1, st1[:, 0:2], ALU.mult)
                    stt(P, "dve", st1, st1[:, 6:8], st1, st1[:, 2:4], 1.0 / 64, st1, st1[:, 4:6], ALU.mult, ALU.subtract)
                    act(P, st1, st1[:, 6:8], st1, st1[:, 6:8], AF.Ln, extra=[k.epsb], bias=k.epsb[:, 2:3])
                    act(P, st1, st1[:, 6:8], st1, st1[:, 6:8], AF.Exp, scale=-0.5)
                    for e in range(2):
                        ec = slice(e * 64, (e + 1) * 64)
                        ts(P, "dve", yn, yn[:, ec], ysb, ysb[:, ec], st1[:, e:e + 1], st1[:, 6 + e:7 + e],
                           ALU.subtract, ALU.mult, extra=[st1])
                    YT = nxt("p3", p3)
                    mm(P, YT, YT[:], yn, yn[:], k.cf, idf, True, True)
                    stt(P, "dve", o1, o1[:], YT, YT[:], vc(V_LNW), f["bv"], f["bv"][:, cc], ALU.mult, ALU.add,
                        extra=[k.vec])
                    ys = yst[cnt["y"] % 2]
                    stt(P, "dve", ys, ys[:, cc], o1, o1[:], vc(V_LNB), f["gT"], f["gT"][:, cc], ALU.add, ALU.mult,
                        extra=[k.vec])
                ys = yst[cnt["y"] % 2]
                P.dma("sp", chy[cnt["y"] % 2], k.yT.ap()[1024 + hp * 128:1024 + (hp + 1) * 128, tb * 512:(tb + 1) * 512],
                      ys[:], reads=[ys], writes=[k.yT])
                cnt["y"] += 1


def phase_c(P, k, l, xin, xout):
    with P.scope():
        vo = l * VEC_L + V_NRM
        xs = P.sbuf("cx", [128, KC, 512], F32)
        acc = P.sbuf("cacc", [128, KC, 512], F32)
        uT = P.sbuf("cu", [128, KC, 512], BF16)
        mT = P.sbuf("cm", [128, KC, 512], BF16)
        hd = P.sbuf("chd", [128, KC, 512], BF16)
        yb = P.sbuf("cy", [128, KC, 512], BF16)
        wt = [P.sbuf("cw", [128, KC, 512], BF16) for _ in range(2)]
        gs = [[P.sbuf("cg", [128, 512], BF16) for _ in range(4)] for _ in range(3)]
        rs = P.sbuf("crs", [128, 512], F32)
        tmp = [P.sbuf("ctmp", [128, 512], F32) for _ in range(2)]
        pg = [P.psum("cpg", [128, 512]) for _ in range(3)]
        pbr = [P.psum("cpb", [128, 512]) for _ in range(3)]
        pss = P.psum("cpss", [128, 512])
        chw = [P.chan("cw") for _ in range(2)]
        chx = P.chan("cx")
        chu = P.chan("cu")
        chyy = P.chan("cy")
        cho = P.chan("co")
        order = []
        for ng in range(4):
            order += [(k.wg, i * 4 + ng) for i in range(3)] + [(k.wbr, ng)]
        order += [(k.wout, i) for i in range(4)]
        for q in range(4):
            order += [(k.wup_mlp, q * 4 + i) for i in range(4)] + [(k.wdn, q * 4 + i) for i in range(4)]
        NT = len(order)
        seq = [0]

        def loadw(gidx):
            src, ti = order[gidx % NT]
            i = gidx % 2
            P.dma("pool", chw[i], wt[i][:].rearrange("p c n -> p (c n)"), src.ap()[l, ti], reads=[src], writes=[wt[i]])

        def nextw():
            g = seq[0]
            seq[0] += 1
            if g + 1 < NT * 8:
                loadw(g + 1)
            return wt[g % 2]
        loadw(0)
        xv = xin.ap().rearrange("(c p) t -> p c t", p=128)
        ov = xout.ap().rearrange("(c p) t -> p c t", p=128)
        uv = k.uT.ap().rearrange("(c p) t -> p c t", p=128)
        yv = k.yT.ap().rearrange("(c p) t -> p c t", p=128)
        gi = [0]

        def gemm(w, j, rhs_b, out_ps):
            for c in range(KC):
                mm(P, out_ps, out_ps[:], w, w[:, c, j * 128:(j + 1) * 128], rhs_b, rhs_b[:, c, :], c == 0, c == KC - 1)

        def post_norm(gcol):
            rms_stats(P, k, acc, hd, pss, rs)
            for c in range(KC):
                stt(P, "dve", acc, acc[:, c, :], acc, acc[:, c, :], k.vec[:, gcol + c:gcol + c + 1], rs, rs[:],
                    ALU.mult, ALU.mult, extra=[k.vec])
                tt(P, "dve", xs, xs[:, c, :], xs, xs[:, c, :], acc, acc[:, c, :], ALU.add)
        for tb in range(8):
            cs = slice(tb * 512, (tb + 1) * 512)
            for g in range(4):
                gsl = slice(4 * g, 4 * g + 4)
                P.dma("sp", chx, xs[:, gsl, :], xv[:, gsl, cs], reads=[xin], writes=[xs])
                P.dma("sp", chu, uT[:, gsl, :], uv[:, gsl, cs], reads=[k.uT], writes=[uT])
                P.dma("sp", chyy, yb[:, gsl, :], yv[:, gsl, cs], reads=[k.yT], writes=[yb])
            for ng in range(4):
                for i in range(3):
                    w = nextw()
                    for j in range(4):
                        p_ = pg[gi[0] % 3]
                        gi[0] += 1
                        gemm(w, j, uT, p_)
                        act(P, gs[i][j], gs[i][j][:], p_, p_[:], AF.Sigmoid)
                w = nextw()
                for j in range(4):
                    n = ng * 4 + j
                    for i, (k0, nk) in enumerate(((0, 4), (4, 4), (8, 8))):
                        for c in range(nk):
                            mm(P, pbr[i], pbr[i][:], w, w[:, k0 + c, j * 128:(j + 1) * 128], yb, yb[:, k0 + c, :],
                               c == 0, c == nk - 1)
                    tt(P, "dve", tmp[0], tmp[0][:], pbr[0], pbr[0][:], gs[0][j], gs[0][j][:], ALU.mult)
                    tt(P, "dve", tmp[1], tmp[1][:], pbr[1], pbr[1][:], gs[1][j], gs[1][j][:], ALU.mult)
                    tt(P, "dve", tmp[0], tmp[0][:], tmp[0], tmp[0][:], tmp[1], tmp[1][:], ALU.add)
                    tt(P, "dve", tmp[1], tmp[1][:], pbr[2], pbr[2][:], gs[2][j], gs[2][j][:], ALU.mult)
                    tt(P, "dve", mT, mT[:, n, :], tmp[0], tmp[0][:], tmp[1], tmp[1][:], ALU.add)
            for t_ in range(4):
                w = nextw()
                for j in range(4):
                    p_ = pg[gi[0] % 3]
                    gi[0] += 1
                    gemm(w, j, mT, p_)
                    cp(P, "act", acc, acc[:, t_ * 4 + j, :], p_, p_[:])
            post_norm(vo + 16)
            rms_stats(P, k, xs, hd, pss, rs)
            for c in range(KC):
                stt(P, "dve", mT, mT[:, c, :], xs, xs[:, c, :], k.vec[:, vo + 32 + c:vo + 32 + c + 1], rs, rs[:],
                    ALU.mult, ALU.mult, extra=[k.vec])
            for q in range(4):
                for t_ in range(4):
                    w = nextw()
                    for j in range(4):
                        p_ = pg[gi[0] % 3]
                        gi[0] += 1
                        gemm(w, j, mT, p_)
                        tq = tmp[gi[0] % 2]
                        P.op("dve", lambda e: e.tensor_scalar_max(out=tq[:], in0=p_[:], scalar1=0.0), reads=[p_], writes=[tq])
                        tt(P, "dve", hd, hd[:, t_ * 4 + j, :], tq, tq[:], tq, tq[:], ALU.mult)
                for cg in range(4):
                    w = nextw()
                    for j in range(4):
                        p_ = pg[gi[0] % 3]
                        gi[0] += 1
                        gemm(w, j, hd, p_)
                        n = cg * 4 + j
                        if q == 0:
                            cp(P, "act", acc, acc[:, n, :], p_, p_[:])
                        else:
                            tt(P, "dve", acc, acc[:, n, :], acc, acc[:, n, :], p_, p_[:], ALU.add)
            post_norm(vo + 48)
            for g in range(4):
                gsl = slice(4 * g, 4 * g + 4)
                P.dma("sp", cho, ov[:, gsl, cs], xs[:, gsl, :], reads=[xs], writes=[xout])


def build(dbg=(), stages="abcdC", nl=L):
    nc = bass.Bass("TRN2", target_bir_lowering=False)
    P = Prog(nc)
    k = K()

    def dk(name):
        return "ExternalOutput" if name in dbg else "Internal"

    def inp(name, shape):
        return P.dram(name, shape, F32, kind="ExternalInput")
    k.xT = inp("xT", [D, T])
    k.consts = inp("consts", [128, NCONST])
    k.vecs = inp("vecs", [128, L * VEC_L])
    k.wqk = inp("wqk", [L, 4, 128, KC * 512])
    k.wv = inp("wv", [L, 2, 128, KC * 512])
    k.wf = inp("wf", [L, 128, KC * 4])
    k.wrw = inp("wrw", [L, 7, 128, KC * 512])
    k.wup = inp("wup", [L, 64, 1024])
    k.aup = inp("aup", [L, 64, 1024])
    k.gup = inp("gup", [L, 160, 1024])
    if "C" in stages:
        k.wg = inp("wg", [L, 12, 128, KC * 512])
        k.wbr = inp("wbr", [L, 4, 128, KC * 512])
        k.wout = inp("wout", [L, 4, 128, KC * 512])
        k.wup_mlp = inp("wmup", [L, 16, 128, KC * 512])
        k.wdn = inp("wmdn", [L, 16, 128, KC * 512])
    k.uT = P.dram("uT", [D, T], BF16, kind=dk("uT"))
    k.qkT = P.dram("qkT", [NQK, 128, T], BF16, kind=dk("qkT"))
    k.vs = P.dram("vs", [8, 128, 32, 128], BF16, kind=dk("vs"))
    k.zsT = P.dram("zsT", [NRW * 128, T], F32, kind=dk("zsT"))
    k.cdram = P.dram("cdram", [4, T], F32, kind=dk("cdram"))
    k.yT = P.dram("yT", [D, T], BF16, kind=dk("yT"))
    k.x1T = P.dram("x1T", [D, T], F32, kind=dk("x1T"))
    k.out = P.dram("outT", [D, T], F32, kind="ExternalOutput")
    k.csum = P.sbuf("csum", [128, 4, 32], F32)
    load_consts(P, k)
    for l in range(nl):
        xin = k.xT if l == 0 else k.x1T
        xout = k.x1T if l == 0 and nl == 2 else k.out
        if "a" in stages:
            phase_norm(P, k, xin, l * VEC_L + V_NRM + 0, k.uT)
        if "b" in stages:
            phase_b1(P, k, l)
        if "c" in stages:
            phase_b2(P, k, l)
        if "d" in stages:
            phase_b3(P, k, l)
        if "C" in stages:
            phase_c(P, k, l, xin, xout)
    P.close()
    print("instructions:", P.n_inst)
    return nc


QA0, KA0, VA0, QB0, KB0, VB0, FB0, RW0, GT0 = 0, 512, 1024, 1536, 2048, 2560, 3072, 3076, 6436


def _tile(w):
    kk, n = w.shape
    out = np.zeros((kk // 128, 128, 512), np.float32)
    out[:, :, :n] = w.reshape(kk // 128, 128, n)
    return np.ascontiguousarray(out.transpose(1, 0, 2)).reshape(128, -1)


def make_consts():
    c = np.zeros((128, NCONST), np.float32)
    i = np.arange(128)
    c[:, C_ID:C_ID + 128] = np.eye(128)
    c[:, C_ONES:C_ONES + 128] = 1.0
    c[:, C_UTRI:C_UTRI + 128] = (i[:, None] <= i[None, :])
    c[:, C_NEGT:C_NEGT + 128] = -1.0 * (i[:, None] > i[None, :])
    c[:, C_MUP:C_MUP + 128] = (i[:, None] < i[None, :])
    c[:, C_MLO:C_MLO + 128] = (i[:, None] > i[None, :])
    c[:, C_BLK:C_BLK + 128] = ((i[:, None] // 64) == (i[None, :] // 64))
    q = np.arange(512)
    c[:, C_RST:C_RST + 512] = (q % 128 != 0)[None, :]
    for j in range(4):
        kk = j * 128 + i
        c[:, C_FOXB + j * 512:C_FOXB + (j + 1) * 512] = np.where(kk[:, None] <= q[None, :], 0.0, -30000.0)
        c[:, C_SBM + j * 512:C_SBM + (j + 1) * 512] = (kk[:, None] < q[None, :])
    return c


def prep_shared(inp, with_c=True):
    m = {}
    m["consts"] = make_consts()
    vec = np.zeros((128, L * VEC_L), np.float32)
    wqk = np.zeros((L, 4, 128, KC * 512), np.float32)
    wv = np.zeros((L, 2, 128, KC * 512), np.float32)
    wf = np.zeros((L, 128, KC * 4), np.float32)
    wrw = np.zeros((L, 7, 128, KC * 512), np.float32)
    for l in range(L):
        o = l * VEC_L
        for wi, nm in enumerate(("norm_mix_pre", "norm_mix_post", "norm_mlp_pre", "norm_mlp_post")):
            vec[:, o + V_NRM + wi * 16:o + V_NRM + wi * 16 + 16] = inp[nm][l].reshape(16, 128).T
        mu_p = np.zeros(NRW * 128, np.float32)
        mu_p[:3360] = inp["rwkv_mu"][l]
        vec[:, o + V_MU:o + V_MU + NRW] = mu_p.reshape(NRW, 128).T
        for col, nm in ((V_W0, "rwkv_w0"), (V_A0, "rwkv_a0"), (V_KK, "rwkv_k_k"), (V_KA, "rwkv_k_a"),
                        (V_RK, "rwkv_r_k"), (V_LNW, "rwkv_ln_w"), (V_LNB, "rwkv_ln_b")):
            vec[:, o + col:o + col + 8] = inp[nm][l].reshape(8, 128).T
        vec[:, o + V_BF:o + V_BF + 128] = np.tile(inp["b_forget"][l], 32)[None, :]
        w = inp["w_in"][l]
        for ti, c0 in enumerate((QA0, KA0, QB0, KB0)):
            wqk[l, ti] = _tile(w[:, c0:c0 + 512])
        wv[l, 0] = _tile(w[:, VA0:VA0 + 512])
        wv[l, 1] = _tile(w[:, VB0:VB0 + 512])
        wf[l] = np.ascontiguousarray(w[:, FB0:FB0 + 4].reshape(16, 128, 4).transpose(1, 0, 2)).reshape(128, 64)
        for ti in range(7):
            wrw[l, ti] = _tile(w[:, RW0 + ti * 512:min(RW0 + (ti + 1) * 512, RW0 + 3360)])
    m["vecs"] = vec
    m["wqk"], m["wv"], m["wf"], m["wrw"] = wqk, wv, wf, wrw
    m["wup"] = np.ascontiguousarray(inp["rwkv_w_up"])
    m["aup"] = np.ascontiguousarray(inp["rwkv_a_up"])
    m["gup"] = np.ascontiguousarray(inp["rwkv_g_up"])
    if with_c:
        wg = np.zeros((L, 12, 128, KC * 512), np.float32)
        wbr = np.zeros((L, 4, 128, KC * 512), np.float32)
        wout = np.zeros((L, 4, 128, KC * 512), np.float32)
        wmup = np.zeros((L, 16, 128, KC * 512), np.float32)
        wmdn = np.zeros((L, 16, 128, KC * 512), np.float32)
        for l in range(L):
            w = inp["w_in"][l]
            for t in range(12):
                wg[l, t] = _tile(w[:, GT0 + t * 512:GT0 + (t + 1) * 512])
            br = np.concatenate([inp["w_branch_a"][l], inp["w_branch_b"][l], inp["w_branch_c"][l]], 0)
            for t in range(4):
                wbr[l, t] = _tile(br[:, t * 512:(t + 1) * 512])
                wout[l, t] = _tile(inp["w_out"][l][:, t * 512:(t + 1) * 512])
            for t in range(16):
                wmup[l, t] = _tile(inp["w_mlp_up"][l][:, t * 512:(t + 1) * 512])
                q, cg = t // 4, t % 4
                wmdn[l, t] = _tile(inp["w_mlp_down"][l][q * 2048:(q + 1) * 2048, cg * 512:(cg + 1) * 512])
        m["wg"], m["wbr"], m["wout"], m["wmup"], m["wmdn"] = wg, wbr, wout, wmup, wmdn
    return m


_CACHE = {}


def kernel(**inputs):
    inp = {k_: np.asarray(v, dtype=np.float32) for k_, v in inputs.items()}
    if "nc" not in _CACHE:
        _CACHE["nc"] = build()
    nc = _CACHE["nc"]
    shared = prep_shared(inp)
    maps = []
    for c in range(NCORES):
        m = dict(shared)
        m["xT"] = np.ascontiguousarray(inp["x"][c].T)
        maps.append(m)
    res = run_bass_kernel_spmd(nc, maps, core_ids=list(range(NCORES)))
    out = np.stack([np.ascontiguousarray(np.asarray(res.results[c]["outT"]).T) for c in range(NCORES)], 0)
    return out.astype(np.float32)
```

```python
import contextlib
import numpy as np
import concourse.bass as bass
import concourse.mybir as mybir
from concourse.bass_utils import run_bass_kernel_spmd

F32 = mybir.dt.float32
BF16 = mybir.dt.bfloat16
AF = mybir.ActivationFunctionType
ALU = mybir.AluOpType
AX = mybir.AxisListType


class Counter:
    def __init__(self, prog, name, step, epoch):
        self.prog, self.name, self.step, self.epoch = prog, name, step, epoch
        self.n = 0
        self.sems = []

    def next(self):
        self.n += 1
        ep = (self.n - 1) // self.epoch
        while len(self.sems) <= ep:
            self.sems.append(self.prog.new_sem(f"{self.name}_{len(self.sems)}"))
        return self.n

    def sem_val(self, n):
        ep = (n - 1) // self.epoch
        return self.sems[ep], ((n - 1) % self.epoch + 1) * self.step


class Buf:
    def __init__(self, t, name=""):
        self.t = t
        self.name = name
        self.last_write = None
        self.reads = {}

    def __getitem__(self, idx):
        return self.t[idx]

    def ap(self):
        return self.t.ap()


class Prog:
    ENGS = ("pe", "act", "dve", "pool", "sp")

    def __init__(self, nc):
        self.nc = nc
        self.root = contextlib.ExitStack()
        self.stacks = [self.root]
        self.eng = {"pe": nc.tensor, "act": nc.scalar, "dve": nc.vector,
                    "pool": nc.gpsimd, "sp": nc.sync}
        self.cnt = {e: Counter(self, "c" + e, 1, 30000) for e in self.ENGS}
        self.observed = {e: {} for e in self.ENGS}
        self.all_counters = list(self.cnt.values())
        self.n_inst = 0
        self.uid = 0

    def new_sem(self, name):
        return self.root.enter_context(self.nc.semaphore(name))

    def chan(self, name, step=16):
        self.uid += 1
        c = Counter(self, f"d{name}{self.uid}", step, 1800 if step == 16 else 30000)
        self.all_counters.append(c)
        return c

    @contextlib.contextmanager
    def scope(self):
        st = contextlib.ExitStack()
        self.stacks.append(st)
        try:
            yield
        finally:
            self.barrier()
            self.stacks.pop()
            st.close()

    def sbuf(self, name, shape, dtype):
        self.uid += 1
        t = self.stacks[-1].enter_context(
            self.nc.sbuf_tensor(f"{name}_{self.uid}", list(shape), dtype))
        return Buf(t, name)

    def psum(self, name, shape, dtype=F32):
        self.uid += 1
        t = self.stacks[-1].enter_context(
            self.nc.psum_tensor(f"{name}_{self.uid}", list(shape), dtype))
        return Buf(t, name)

    def dram(self, name, shape, dtype, kind="Internal"):
        return Buf(self.nc.dram_tensor(name, list(shape), dtype, kind=kind), name)

    def _need(self, engine, tok, waits):
        if tok is None:
            return
        c, n, teng = tok
        if teng == "pe" and engine == "pe":
            return
        if teng == "dma":
            n = c.n
        ob = self.observed[engine]
        if ob.get(id(c), 0) >= n:
            return
        ob[id(c)] = n
        waits.append((c, n))

    def op(self, engine, fn, reads=(), writes=(), chan=None):
        waits = []
        for b in reads:
            self._need(engine, b.last_write, waits)
        for b in writes:
            self._need(engine, b.last_write, waits)
            for t in b.reads.values():
                self._need(engine, t, waits)
        e = self.eng[engine]
        for c, n in waits:
            s, v = c.sem_val(n)
            e.wait_ge(s, v)
        ins = fn(e)
        c = chan if chan is not None else self.cnt[engine]
        n = c.next()
        s, _ = c.sem_val(n)
        ins.then_inc(s, c.step)
        tok = (c, n, engine if chan is None else "dma")
        for b in reads:
            b.reads[id(c)] = tok
        for b in writes:
            b.last_write = tok
            b.reads = {}
        self.n_inst += 1
        return tok

    def dma(self, queue, chan, out, in_, reads=(), writes=(), **kw):
        return self.op(queue, lambda e: e.dma_start(out=out, in_=in_, **kw),
                       reads=reads, writes=writes, chan=chan)

    def barrier(self):
        toks = [(c, c.n, "x") for c in self.all_counters if c.n > 0]
        for engine in self.ENGS:
            waits = []
            for t in toks:
                self._need(engine, t, waits)
            e = self.eng[engine]
            for c, n in waits:
                s, v = c.sem_val(n)
                e.wait_ge(s, v)

    def close(self):
        self.barrier()
        self.root.close()


D = 2048
T = 4096
KC = 16
L = 2
NCORES = 4
SCALE = 128.0 ** -0.5
EPS = 1e-6
NQK = 16
NRW = 27
V_NRM = 0
V_MU = 64
V_W0, V_A0, V_KK, V_KA, V_RK, V_LNW, V_LNB = 96, 104, 112, 120, 128, 136, 144
V_BF = 152
VEC_L = 288
C_ID = 0
C_ONES = 128
C_UTRI = 256
C_NEGT = 384
C_MUP = 512
C_MLO = 640
C_BLK = 768
C_RST = 896
C_FOXB = 1408
NCF = 1408 + 2048
C_SBM = NCF
NCONST = NCF + 2048
NCB = 896
EM05 = float(np.exp(-0.5))


class K:
    pass


def mm(P, ps, out_ap, lb, lhsT, rb, rhs, start, stop):
    P.op("pe", lambda e: e.matmul(out_ap, lhsT, rhs, start=start, stop=stop),
         reads=[lb, rb], writes=[ps])


def act(P, out_b, out_ap, in_b, in_ap, func, extra=(), **kw):
    P.op("act", lambda e: e.activation(out=out_ap, in_=in_ap, func=func, **kw),
         reads=[in_b] + list(extra), writes=[out_b])


def tt(P, eng, out_b, out_ap, a_b, a_ap, b_b, b_ap, op):
    P.op(eng, lambda e: e.tensor_tensor(out=out_ap, in0=a_ap, in1=b_ap, op=op),
         reads=[a_b, b_b], writes=[out_b])


def stt(P, eng, out_b, out_ap, a_b, a_ap, scalar, b_b, b_ap, op0, op1, extra=()):
    P.op(eng, lambda e: e.scalar_tensor_tensor(out=out_ap, in0=a_ap, scalar=scalar, in1=b_ap, op0=op0, op1=op1),
         reads=[a_b, b_b] + list(extra), writes=[out_b])


def ts(P, eng, out_b, out_ap, a_b, a_ap, s1, s2, op0, op1, extra=()):
    P.op(eng, lambda e: e.tensor_scalar(out=out_ap, in0=a_ap, scalar1=s1, scalar2=s2, op0=op0, op1=op1),
         reads=[a_b] + list(extra), writes=[out_b])


def cp(P, eng, out_b, out_ap, in_b, in_ap):
    if eng == "act":
        P.op("act", lambda e: e.copy(out=out_ap, in_=in_ap), reads=[in_b], writes=[out_b])
    else:
        P.op(eng, lambda e: e.tensor_copy(out=out_ap, in_=in_ap), reads=[in_b], writes=[out_b])


def load_consts(P, k):
    k.cf = P.sbuf("cf", [128, NCF], F32)
    k.cb = P.sbuf("cb", [128, NCB], BF16)
    k.sbm = P.sbuf("sbm", [128, 2048], BF16)
    k.vec = P.sbuf("vec", [128, L * VEC_L], F32)
    k.epsb = P.sbuf("epsb", [128, 4], F32)
    k.omka = P.sbuf("omka", [128, L * 8], F32)
    for i, v in enumerate((EPS, 1.0, 64e-5, 1e-24)):
        P.op("dve", lambda e: e.memset(k.epsb[:, i:i + 1], v), writes=[k.epsb])
    ch = P.chan("const")
    P.dma("sp", ch, k.cf[:], k.consts.ap()[:, 0:NCF], reads=[k.consts], writes=[k.cf])
    P.dma("pool", ch, k.cb[:], k.consts.ap()[:, 0:NCB], reads=[k.consts], writes=[k.cb])
    P.dma("pool", ch, k.sbm[:], k.consts.ap()[:, C_SBM:C_SBM + 2048], reads=[k.consts], writes=[k.sbm])
    P.dma("sp", ch, k.vec[:], k.vecs.ap(), reads=[k.vecs], writes=[k.vec])
    for l in range(L):
        o = l * VEC_L + V_KA
        ts(P, "dve", k.omka, k.omka[:, l * 8:l * 8 + 8], k.vec, k.vec[:, o:o + 8], -1.0, 1.0, ALU.mult, ALU.add)


def rms_stats(P, k, src, sq, ps, rs):
    for g in range(4):
        act(P, sq, sq[:, 4 * g:4 * g + 4, :], src, src[:, 4 * g:4 * g + 4, :], AF.Square)
    for c in range(KC):
        mm(P, ps, ps[:], k.cb, k.cb[:, C_ONES:C_ONES + 128], sq, sq[:, c, :], c == 0, c == KC - 1)
    act(P, rs, rs[:], ps, ps[:], AF.Ln, extra=[k.epsb], scale=1.0 / D, bias=k.epsb[:, 0:1])
    act(P, rs, rs[:], rs, rs[:], AF.Exp, scale=-0.5)


def phase_norm(P, k, src, gain_col, dst):
    with P.scope():
        xs = [P.sbuf("nx", [128, KC, 512], F32) for _ in range(2)]
        sq = [P.sbuf("nsq", [128, KC, 512], BF16) for _ in range(2)]
        us = [P.sbuf("nu", [128, KC, 512], BF16) for _ in range(2)]
        rs = [P.sbuf("nr", [128, 512], F32) for _ in range(2)]
        ps = [P.psum("nps", [128, 512]) for _ in range(2)]
        chl = [P.chan("nl") for _ in range(2)]
        chs = [P.chan("ns") for _ in range(2)]
        sv = src.ap().rearrange("(c p) t -> p c t", p=128)
        dv = dst.ap().rearrange("(c p) t -> p c t", p=128)
        nb = T // 512

        def load(tb):
            i = tb % 2
            for g in range(4):
                P.dma("sp", chl[i], xs[i][:, 4 * g:4 * g + 4, :],
                      sv[:, 4 * g:4 * g + 4, tb * 512:(tb + 1) * 512], reads=[src], writes=[xs[i]])
        load(0)
        for tb in range(nb):
            i = tb % 2
            if tb + 1 < nb:
                load(tb + 1)
            rms_stats(P, k, xs[i], sq[i], ps[i], rs[i])
            for c in range(KC):
                stt(P, "dve", us[i], us[i][:, c, :], xs[i], xs[i][:, c, :], k.vec[:, gain_col + c:gain_col + c + 1],
                    rs[i], rs[i][:], ALU.mult, ALU.mult, extra=[k.vec])
            for g in range(4):
                P.dma("sp", chs[i], dv[:, 4 * g:4 * g + 4, tb * 512:(tb + 1) * 512],
                      us[i][:, 4 * g:4 * g + 4, :], reads=[us[i]], writes=[dst])


def phase_b1(P, k, l):
    with P.scope():
        uT = P.sbuf("uT", [128, KC, T], BF16)
        wt = [P.sbuf("wt", [128, KC, 512], BF16) for _ in range(2)]
        wft = P.sbuf("wft", [128, KC, 4], BF16)
        qst = [P.sbuf("qst", [128, 512], BF16) for _ in range(2)]
        zst = [P.sbuf("zst", [128, 512], F32) for _ in range(2)]
        vst = [P.sbuf("vst", [128, 512], BF16) for _ in range(2)]
        zraw = [P.sbuf("zraw", [128, 513], F32) for _ in range(2)]
        dt = P.sbuf("dt", [128, 512], F32)
        fsb = P.sbuf("fsb", [128, 128], F32)
        fsp = P.sbuf("fsp", [128, 128], F32)
        tot = P.sbuf("tot", [128, 128], F32)
        pre = P.sbuf("pre", [128, 128], F32)
        cT = P.sbuf("cT", [32, 4, 128], F32)
        ps = [P.psum("b1ps", [128, 512]) for _ in range(4)]
        psf = P.psum("psf", [128, 128])
        psc = P.psum("psc", [128, 128])
        pst = P.psum("pst", [128, 128])
        chu = P.chan("u")
        chw = [P.chan("w") for _ in range(2)]
        chq = [P.chan("q") for _ in range(2)]
        chz = [P.chan("z") for _ in range(2)]
        chv = [P.chan("v") for _ in range(2)]
        chm = P.chan("m")

        uv = k.uT.ap().rearrange("(c p) t -> p c t", p=128)
        for hh in range(2):
            for g in range(4):
                P.dma("sp", chu, uT[:, 4 * g:4 * g + 4, hh * 2048:(hh + 1) * 2048],
                      uv[:, 4 * g:4 * g + 4, hh * 2048:(hh + 1) * 2048], reads=[k.uT], writes=[uT])
        P.dma("pool", chm, wft[:].rearrange("p c n -> p (c n)"), k.wf.ap()[l], reads=[k.wf], writes=[wft])

        tiles = [("qk", k.wqk, i) for i in range(4)] + [("v", k.wv, i) for i in range(2)] + \
                [("rw", k.wrw, i) for i in range(7)]

        def loadw(idx):
            kind, src, ti = tiles[idx]
            i = idx % 2
            P.dma("pool", chw[i], wt[i][:].rearrange("p c n -> p (c n)"), src.ap()[l, ti], reads=[src], writes=[wt[i]])
        loadw(0)
        pr = [0]
        zi = [0]
        for idx, (kind, src, ti) in enumerate(tiles):
            w = wt[idx % 2]
            if idx + 1 < len(tiles):
                loadw(idx + 1)
            if kind == "qk":
                for j in range(4):
                    nci = ti * 4 + j
                    isq = ti in (0, 2)
                    for tb in range(8):
                        p_ = ps[pr[0] % 4]
                        pr[0] += 1
                        for c in range(KC):
                            mm(P, p_, p_[:], w, w[:, c, j * 128:(j + 1) * 128], uT, uT[:, c, tb * 512:(tb + 1) * 512],
                               c == 0, c == KC - 1)
                        st = qst[tb % 2]
                        act(P, st, st[:], p_, p_[:], AF.Copy, scale=SCALE if isq else 1.0)
                        P.dma("sp", chq[tb % 2], k.qkT.ap()[nci, :, tb * 512:(tb + 1) * 512], st[:],
                              reads=[st], writes=[k.qkT])
            elif kind == "v":
                vview = k.vs.ap()[ti * 4:ti * 4 + 4].rearrange("h p s d -> p s h d")
                for s in range(32):
                    p_ = ps[pr[0] % 4]
                    pr[0] += 1
                    for c in range(KC):
                        mm(P, p_, p_[:], uT, uT[:, c, s * 128:(s + 1) * 128], w, w[:, c, :], c == 0, c == KC - 1)
                    st = vst[s % 2]
                    cp(P, "dve", st, st[:], p_, p_[:])
                    P.dma("sp", chv[s % 2], vview[:, s, :, :], st[:].rearrange("p (h d) -> p h d", h=4),
                          reads=[st], writes=[k.vs])
                if ti == 1:
                    for s in range(32):
                        for c in range(KC):
                            mm(P, psf, psf[:, 4 * s:4 * s + 4], uT, uT[:, c, s * 128:(s + 1) * 128], wft, wft[:, c, :],
                               c == 0, c == KC - 1)
                    vb = l * VEC_L + V_BF
                    tt(P, "dve", fsb, fsb[:], psf, psf[:], k.vec, k.vec[:, vb:vb + 128], ALU.add)
                    act(P, fsp, fsp[:], fsb, fsb[:], AF.Exp, scale=-1.0)
                    act(P, fsp, fsp[:], fsp, fsp[:], AF.Ln, extra=[k.epsb], bias=k.epsb[:, 1:2])
                    mm(P, psc, psc[:], k.cf, k.cf[:, C_UTRI:C_UTRI + 128], fsp, fsp[:], True, True)
                    mm(P, pst, pst[:], k.cf, k.cf[:, C_ONES:C_ONES + 128], fsp, fsp[:], True, True)
                    cp(P, "dve", tot, tot[:], pst, pst[:])
                    P.op("dve", lambda e: e.memset(pre[:, 0:4], 0.0), writes=[pre])
                    for s in range(1, 32):
                        tt(P, "dve", pre, pre[:, 4 * s:4 * s + 4], pre, pre[:, 4 * s - 4:4 * s],
                           tot, tot[:, 4 * s - 4:4 * s], ALU.add)
                    tt(P, "dve", k.csum, k.csum[:].rearrange("p h s -> p s h"),
                       psc, psc[:].rearrange("p (s h) -> p s h", h=4),
                       pre, pre[:].rearrange("p (s h) -> p s h", h=4), ALU.add)
                    for h in range(4):
                        mm(P, ps[h], ps[h][0:32, 0:128], k.csum, k.csum[:, h, :], k.cf, k.cf[:, C_ID:C_ID + 128],
                           True, True)
                        act(P, cT, cT[:, h, :], ps[h], ps[h][0:32, 0:128], AF.Copy, scale=-1.0)
                    P.dma("sp", chm, k.cdram.ap().rearrange("h (s p) -> s h p", p=128), cT[:],
                          reads=[cT], writes=[k.cdram])
            else:
                nj = 4 if ti < 6 else 3
                for j in range(nj):
                    nci = ti * 4 + j
                    M = 32 if nci == 26 else 128
                    mcol = l * VEC_L + V_MU + nci
                    for tb in range(8):
                        p_ = ps[pr[0] % 4]
                        pr[0] += 1
                        for c in range(KC):
                            mm(P, p_, p_[0:M, :], w, w[:, c, j * 128:j * 128 + M], uT, uT[:, c, tb * 512:(tb + 1) * 512],
                               c == 0, c == KC - 1)
                        zr = zraw[zi[0] % 2]
                        zp = zraw[(zi[0] + 1) % 2]
                        zi[0] += 1
                        if tb == 0:
                            P.op("dve", lambda e: e.memset(zr[0:M, 0:1], 0.0), writes=[zr])
                        else:
                            cp(P, "act", zr, zr[0:M, 0:1], zp, zp[0:M, 512:513])
                        cp(P, "act", zr, zr[0:M, 1:513], p_, p_[0:M, :])
                        tt(P, "dve", dt, dt[0:M, :], zr, zr[0:M, 0:512], zr, zr[0:M, 1:513], ALU.subtract)
                        st = zst[tb % 2]
                        stt(P, "dve", st, st[0:M, :], dt, dt[0:M, :],
                            k.vec[0:M, mcol:mcol + 1], zr, zr[0:M, 1:513], ALU.mult, ALU.add, extra=[k.vec])
                        P.dma("sp", chz[tb % 2], k.zsT.ap()[nci * 128:nci * 128 + M, tb * 512:(tb + 1) * 512],
                              st[0:M, :], reads=[st], writes=[k.zsT])


def phase_b2(P, k, l):
    with P.scope():
        qT = [P.sbuf("qT", [128, T], BF16) for _ in range(2)]
        kT = [P.sbuf("kT", [128, T], BF16) for _ in range(2)]
        vv = [P.sbuf("vv", [128, 32, 128], BF16) for _ in range(2)]
        cq = P.sbuf("cq", [128, T], F32)
        tts = [P.sbuf("tt", [128, 512], F32) for _ in range(3)]
        t2s = [P.sbuf("t2", [128, 512], F32) for _ in range(2)]
        sps = [P.sbuf("sp", [128, 512], F32) for _ in range(2)]
        spb = [P.sbuf("spb", [128, 512], BF16) for _ in range(2)]
        pps = [P.sbuf("pp", [128, 512], BF16) for _ in range(3)]
        Rt = P.sbuf("Rt", [128, 512], F32)
        rec = P.sbuf("rec", [128, 512], F32)
        yst = [P.sbuf("yst", [128, 512], BF16) for _ in range(2)]
        pS = [P.psum("pS", [128, 512]) for _ in range(2)]
        pB = [P.psum("pB", [128, 512]) for _ in range(2)]
        pC = [P.psum("pC", [128, 512]) for _ in range(2)]
        pO = [P.psum("pO", [128, 512]) for _ in range(2)]
        chl = [P.chan("al") for _ in range(2)]
        chc = P.chan("ac")
        chy = [P.chan("ay") for _ in range(2)]
        heads = [("sb", h) for h in range(4)] + [("fox", h) for h in range(4)]

        def load(hi):
            kind, h = heads[hi]
            i = hi % 2
            base = 0 if kind == "sb" else 8
            P.dma("sp", chl[i], qT[i][:], k.qkT.ap()[base + h], reads=[k.qkT], writes=[qT[i]])
            P.dma("sp", chl[i], kT[i][:], k.qkT.ap()[base + 4 + h], reads=[k.qkT], writes=[kT[i]])
            P.dma("sp", chl[i], vv[i][:], k.vs.ap()[(0 if kind == "sb" else 4) + h], reads=[k.vs], writes=[vv[i]])
        load(0)
        it = [0]
        yi = [0]
        ones = k.cb[:, C_ONES:C_ONES + 128]
        for hi, (kind, h) in enumerate(heads):
            i = hi % 2
            if hi + 1 < len(heads):
                load(hi + 1)
            q_, k_, v_ = qT[i], kT[i], vv[i]
            if kind == "fox":
                P.dma("sp", chc, cq[:], k.cdram.ap()[h:h + 1, :].partition_broadcast(128), reads=[k.cdram], writes=[cq])
            for QB in range(8):
                nkb = 4 * QB + 4
                O = pO[QB % 2]
                qs = slice(QB * 512, (QB + 1) * 512)
                if kind == "fox":
                    DN = pC[QB % 2]
                    for kb in range(nkb):
                        n = it[0]
                        it[0] += 1
                        S = pS[n % 2]
                        mm(P, S, S[:], k_, k_[:, kb * 128:(kb + 1) * 128], q_, q_[:, qs], True, True)
                        t_ = tts[n % 3]
                        stt(P, "dve", t_, t_[:], S, S[:], k.csum[:, h, kb:kb + 1], cq, cq[:, qs], ALU.add, ALU.add,
                            extra=[k.csum])
                        if kb >= 4 * QB:
                            j = kb - 4 * QB
                            tt(P, "pool", t_, t_[:], t_, t_[:], k.cf, k.cf[:, C_FOXB + j * 512:C_FOXB + (j + 1) * 512],
                               ALU.add)
                        p_ = pps[n % 3]
                        act(P, p_, p_[:], t_, t_[:], AF.Exp)
                        mm(P, O, O[:], v_, v_[:, kb, :], p_, p_[:], kb == 0, kb == nkb - 1)
                        mm(P, DN, DN[:], k.cb, ones, p_, p_[:], kb == 0, kb == nkb - 1)
                    P.op("dve", lambda e: e.reciprocal(out=rec[:], in_=DN[:]), reads=[DN], writes=[rec])
                    ys = yst[yi[0] % 2]
                    tt(P, "dve", ys, ys[:], O, O[:], rec, rec[:], ALU.mult)
                    row0 = 512 + h * 128
                else:
                    P.op("pool", lambda e: e.memset(Rt[:], 0.0), writes=[Rt])
                    for kb in reversed(range(nkb)):
                        n = it[0]
                        it[0] += 1
                        S = pS[n % 2]
                        Bp = pB[n % 2]
                        Cp = pC[n % 2]
                        mm(P, S, S[:], k_, k_[:, kb * 128:(kb + 1) * 128], q_, q_[:, qs], True, True)
                        sp_ = sps[n % 2]
                        sb_ = spb[n % 2]
                        t1 = tts[n % 3]
                        t2 = t2s[n % 2]
                        act(P, t1, t1[:], S, S[:], AF.Exp)
                        act(P, sp_, sp_[:], t1, t1[:], AF.Ln, extra=[k.epsb], bias=k.epsb[:, 1:2])
                        diag = kb >= 4 * QB
                        if diag:
                            j = kb - 4 * QB
                            msk = k.sbm[:, j * 512:(j + 1) * 512]
                            tt(P, "pool", sb_, sb_[:], sp_, sp_[:], k.sbm, msk, ALU.mult)
                        else:
                            cp(P, "pool", sb_, sb_[:], sp_, sp_[:])
                        mm(P, Bp, Bp[:], k.cb, k.cb[:, C_NEGT:C_NEGT + 128], sb_, sb_[:], True, True)
                        mm(P, Cp, Cp[:], k.cb, ones, sb_, sb_[:], True, True)
                        tt(P, "dve", t1, t1[:], S, S[:], sp_, sp_[:], ALU.subtract)
                        tt(P, "dve", t2, t2[:], Bp, Bp[:], Rt, Rt[:], ALU.add)
                        tt(P, "pool", t1, t1[:], t1, t1[:], t2, t2[:], ALU.add)
                        p_ = pps[n % 3]
                        act(P, p_, p_[:], t1, t1[:], AF.Exp)
                        if diag:
                            tt(P, "pool", p_, p_[:], p_, p_[:], k.sbm, msk, ALU.mult)
                        mm(P, O, O[:], v_, v_[:, kb, :], p_, p_[:], kb == nkb - 1, kb == 0)
                        tt(P, "dve", Rt, Rt[:], Rt, Rt[:], Cp, Cp[:], ALU.subtract)
                    ys = yst[yi[0] % 2]
                    cp(P, "act", ys, ys[:], O, O[:])
                    row0 = h * 128
                P.dma("sp", chy[yi[0] % 2], k.yT.ap()[row0:row0 + 128, qs], ys[:], reads=[ys], writes=[k.yT])
                yi[0] += 1


class _Stop(Exception):
    pass


def phase_b3(P, k, l):
    try:
        _phase_b3(P, k, l)
    except _Stop:
        pass


def _phase_b3(P, k, l):
    import os
    STOP = int(os.environ.get("B3_STOP", "99"))

    def stop(n):
        if STOP <= n:
            raise _Stop()
    with P.scope():
        vo = l * VEC_L
        NL = 6
        ld = {n: [P.sbuf("ld" + n, [128, 512], F32) for _ in range(2)] for n in ("r", "k", "v", "wa", "g0", "g1")}
        f32n = ("lw", "a", "kk", "tmp", "kp", "kka", "cl", "ex", "e4", "bv", "sq", "gT")
        f = {n: P.sbuf("f" + n, [128, 512], F32) for n in f32n}
        b16n = ("thad", "sg", "sg1", "rt", "kt", "bt", "kap", "Kp", "Bp", "vb")
        b = {n: P.sbuf("b" + n, [128, 512], BF16) for n in b16n}
        WL = P.sbuf("WL", [128, 4], F32)
        tm = {n: P.sbuf("tm" + n, [128, 4, 128], BF16) for n in ("V", "K", "B")}
        wa_t = P.sbuf("wa_t", [128, 128], BF16)
        g0_t = P.sbuf("g0_t", [128, 128], BF16)
        g1_t = P.sbuf("g1_t", [32, 128], BF16)
        def m16(n, cnt_):
            return [P.sbuf(n, [128, 128], BF16) for _ in range(cnt_)]
        A1T, B1T, B2T, TIV = m16("A1T", 8), m16("B1T", 8), m16("B2T", 8), m16("TIV", 8)
        Mp = [m16("Mp", 2) for _ in range(8)]
        Np = [m16("Np", 2) for _ in range(8)]
        R16 = [m16("R16", 2) for _ in range(8)]
        R32 = [P.sbuf("R32", [128, 128], F32) for _ in range(8)]
        S32 = P.sbuf("S32", [128, 128], F32)
        S16 = P.sbuf("S16", [128, 128], BF16)
        Gsb = P.sbuf("Gsb", [128, 128], BF16)
        nP = P.sbuf("nP", [128, 128], BF16)
        ysb = P.sbuf("ysb", [128, 128], F32)
        ysq = P.sbuf("ysq", [128, 128], F32)
        yn = P.sbuf("yn", [128, 128], F32)
        st1 = P.sbuf("st1", [128, 8], F32)
        o1 = P.sbuf("o1", [128, 128], F32)
        yst = [P.sbuf("yst3", [128, 512], BF16) for _ in range(2)]
        pbig = [P.psum("pbig", [128, 512]) for _ in range(2)]
        ptr_t = [P.psum("ptr", [128, 512]) for _ in range(2)]
        ptr = [Buf(ptr_t[i][:, 0:128], f"ptr{i}") for i in range(2)]
        pm_t = [P.psum("pm", [128, 512]) for _ in range(2)]
        pm = [Buf(pm_t[i][:, 0:128], f"pm{i}") for i in range(2)]
        p3_t = [P.psum("p3", [128, 512]) for _ in range(2)]
        p3 = [Buf(p3_t[i][:, 0:128], f"p3{i}") for i in range(2)]
        chl = [P.chan("rl") for _ in range(2)]
        chw = P.chan("rw")
        chy = [P.chan("ry") for _ in range(2)]
        cnt = {"big": 0, "tr": 0, "pm": 0, "p3": 0, "y": 0}

        def nxt(kind, lst):
            i = cnt[kind]
            cnt[kind] += 1
            return lst[i % len(lst)]
        idf = k.cf[:, C_ID:C_ID + 128]
        idb = k.cb[:, C_ID:C_ID + 128]
        blk = k.cf[:, C_BLK:C_BLK + 128]
        mup = k.cf[:, C_MUP:C_MUP + 128]
        mlo = k.cf[:, C_MLO:C_MLO + 128]
        mui = k.cf[:, C_UTRI:C_UTRI + 128]

        def load(idx):
            hp, tb = idx // 8, idx % 8
            i = idx % 2
            cs = slice(tb * 512, (tb + 1) * 512)
            for n, nci, M in (("r", hp, 128), ("k", 8 + hp, 128), ("v", 16 + hp, 128), ("wa", 24, 128),
                              ("g0", 25, 128), ("g1", 26, 32)):
                P.dma("sp", chl[i], ld[n][i][0:M, :], k.zsT.ap()[nci * 128:nci * 128 + M, cs],
                      reads=[k.zsT], writes=[ld[n][i]])
        load(0)
        import os
        NHP = int(os.environ.get("B3_HP", "8"))
        NTB = int(os.environ.get("B3_TB", "8"))
        for hp in range(NHP):
            hc = slice(hp * 128, (hp + 1) * 128)
            P.dma("pool", chw, wa_t[0:64, :], k.wup.ap()[l, :, hc], reads=[k.wup], writes=[wa_t])
            P.dma("pool", chw, wa_t[64:128, :], k.aup.ap()[l, :, hc], reads=[k.aup], writes=[wa_t])
            P.dma("pool", chw, g0_t[:], k.gup.ap()[l, 0:128, hc], reads=[k.gup], writes=[g0_t])
            P.dma("pool", chw, g1_t[:], k.gup.ap()[l, 128:160, hc], reads=[k.gup], writes=[g1_t])
            P.op("dve", lambda e: e.memset(S32[:], 0.0), writes=[S32])
            P.op("dve", lambda e: e.memset(S16[:], 0.0), writes=[S16])

            def vc(col):
                return k.vec[:, vo + col + hp:vo + col + hp + 1]
            for tb in range(NTB):
                idx = hp * 8 + tb
                i = idx % 2
                if tb + 1 < NTB or hp + 1 < NHP:
                    load(idx + 1 if tb + 1 < NTB else (hp + 1) * 8)
                r_, k_, v_, wa_, g0_, g1_ = (ld[n][i] for n in ("r", "k", "v", "wa", "g0", "g1"))
                act(P, b["thad"], b["thad"][0:64, :], wa_, wa_[0:64, :], AF.Tanh)
                cp(P, "pool", b["thad"], b["thad"][64:128, :], wa_, wa_[64:128, :])
                act(P, b["sg"], b["sg"][:], g0_, g0_[:], AF.Sigmoid)
                act(P, b["sg1"], b["sg1"][0:32, :], g1_, g1_[0:32, :], AF.Sigmoid)
                pw = nxt("big", pbig)
                mm(P, pw, pw[:], wa_t, wa_t[0:64, :], b["thad"], b["thad"][0:64, :], True, True)
                act(P, f["lw"], f["lw"][:], pw, pw[:], AF.Sigmoid, extra=[k.vec], bias=vc(V_W0))
                pa = nxt("big", pbig)
                mm(P, pa, pa[:], wa_t, wa_t[64:128, :], b["thad"], b["thad"][64:128, :], True, True)
                act(P, f["a"], f["a"][:], pa, pa[:], AF.Sigmoid, extra=[k.vec], bias=vc(V_A0))
                pg = nxt("big", pbig)
                mm(P, pg, pg[:], g0_t, g0_t[:], b["sg"], b["sg"][:], True, False)
                mm(P, pg, pg[:], g1_t, g1_t[0:32, :], b["sg1"], b["sg1"][0:32, :], False, True)
                cp(P, "act", f["gT"], f["gT"][:], pg, pg[:])
                stop(1)
                P.op("pool", lambda e: e.tensor_scalar_mul(out=f["lw"][:], in0=f["lw"][:], scalar1=-EM05),
                     reads=[f["lw"]], writes=[f["lw"]])
                P.op("dve", lambda e: e.tensor_scalar_mul(out=f["kk"][:], in0=k_[:], scalar1=vc(V_KK)),
                     reads=[k_, k.vec], writes=[f["kk"]])
                ts(P, "dve", f["tmp"], f["tmp"][:], f["a"], f["a"][:], vc(V_KA), k.omka[:, l * 8 + hp:l * 8 + hp + 1],
                   ALU.mult, ALU.add, extra=[k.vec, k.omka])
                tt(P, "dve", f["kp"], f["kp"][:], k_, k_[:], f["tmp"], f["tmp"][:], ALU.mult)
                tt(P, "pool", f["sq"], f["sq"][:], f["kk"], f["kk"][:], f["kk"], f["kk"][:], ALU.mult)
                pq = nxt("big", pbig)
                mm(P, pq, pq[:], k.cf, blk, f["sq"], f["sq"][:], True, True)
                act(P, f["ex"], f["ex"][:], pq, pq[:], AF.Ln, extra=[k.epsb], bias=k.epsb[:, 3:4])
                act(P, f["ex"], f["ex"][:], f["ex"], f["ex"][:], AF.Exp, scale=-0.5)
                tt(P, "dve", f["kk"], f["kk"][:], f["kk"], f["kk"][:], f["ex"], f["ex"][:], ALU.mult)
                tt(P, "pool", f["kka"], f["kka"][:], f["kk"], f["kk"][:], f["a"], f["a"][:], ALU.mult)
                tt(P, "dve", f["tmp"], f["tmp"][:], r_, r_[:], f["kp"], f["kp"][:], ALU.mult)
                P.op("pool", lambda e: e.tensor_scalar_mul(out=f["sq"][:], in0=f["tmp"][:], scalar1=vc(V_RK)),
                     reads=[f["tmp"], k.vec], writes=[f["sq"]])
                pb_ = nxt("big", pbig)
                mm(P, pb_, pb_[:], k.cf, blk, f["sq"], f["sq"][:], True, True)
                tt(P, "dve", f["bv"], f["bv"][:], pb_, pb_[:], v_, v_[:], ALU.mult)
                cp(P, "pool", b["vb"], b["vb"][:], v_, v_[:])
                stop(2)
                P.op("dve", lambda e: e.tensor_tensor_scan(out=f["cl"][:], data0=k.cf[:, C_RST:C_RST + 512],
                                                           data1=f["lw"][:], initial=0.0, op0=ALU.mult, op1=ALU.add),
                     reads=[k.cf, f["lw"]], writes=[f["cl"]])
                act(P, f["ex"], f["ex"][:], f["cl"], f["cl"][:], AF.Exp)
                tt(P, "dve", b["rt"], b["rt"][:], r_, r_[:], f["ex"], f["ex"][:], ALU.mult)
                act(P, f["ex"], f["ex"][:], f["cl"], f["cl"][:], AF.Exp, scale=-1.0)
                tt(P, "dve", b["kt"], b["kt"][:], f["kp"], f["kp"][:], f["ex"], f["ex"][:], ALU.mult)
                tt(P, "pool", b["bt"], b["bt"][:], f["kka"], f["kka"][:], f["ex"], f["ex"][:], ALU.mult)
                tt(P, "dve", f["tmp"], f["tmp"][:], f["cl"], f["cl"][:], f["lw"], f["lw"][:], ALU.subtract)
                act(P, f["ex"], f["ex"][:], f["tmp"], f["tmp"][:], AF.Exp)
                tt(P, "dve", b["kap"], b["kap"][:], f["kk"], f["kk"][:], f["ex"], f["ex"][:], ALU.mult)
                for c in range(4):
                    act(P, f["e4"], f["e4"][:, c * 128:(c + 1) * 128], f["cl"], f["cl"][:, c * 128:(c + 1) * 128],
                        AF.Exp, scale=-1.0, bias=f["cl"][:, c * 128 + 127:c * 128 + 128])
                    act(P, WL, WL[:, c:c + 1], f["cl"], f["cl"][:, c * 128 + 127:c * 128 + 128], AF.Exp)
                tt(P, "dve", b["Kp"], b["Kp"][:], f["kp"], f["kp"][:], f["e4"], f["e4"][:], ALU.mult)
                tt(P, "pool", b["Bp"], b["Bp"][:], f["kka"], f["kka"][:], f["e4"], f["e4"][:], ALU.mult)
                stop(3)
                TRN = os.environ.get("B3_TRN", "VKBe")
                for c in range(4):
                    for n, src_ in (("V", b["vb"]), ("K", b["Kp"]), ("B", b["Bp"])):
                        if n not in TRN:
                            continue
                        pt = nxt("tr", ptr)
                        mm(P, pt, pt[:], src_, src_[:, c * 128:(c + 1) * 128], k.cb, idb, True, True)
                        if "e" in TRN:
                            cp(P, "act" if n == "V" else "dve", tm[n], tm[n][:, c, :], pt, pt[:])
                stop(4)
                slots4 = pm + ptr

                def chain(ci, c, e):
                    cc = slice(c * 128, (c + 1) * 128)
                    er = slice(e * 64, (e + 1) * 64)
                    M_, N_, R16_, R32_ = Mp[ci], Np[ci], R16[ci], R32[ci]
                    p1 = nxt("pm", slots4)
                    mm(P, p1, p1[:], b["kt"], b["kt"][er, cc], b["kap"], b["kap"][er, cc], True, True)
                    tt(P, "dve", A1T[ci], A1T[ci][:], p1, p1[:], k.cf, mup, ALU.mult)
                    yield
                    p2 = nxt("pm", slots4)
                    mm(P, p2, p2[:], b["bt"], b["bt"][er, cc], b["kap"], b["kap"][er, cc], True, True)
                    stt(P, "dve", M_[0], M_[0][:], p2, p2[:], -1.0, k.cf, mup, ALU.mult, ALU.mult)
                    yield
                    p3_ = nxt("pm", slots4)
                    mm(P, p3_, p3_[:], b["kap"], b["kap"][er, cc], b["bt"], b["bt"][er, cc], True, True)
                    stt(P, "dve", N_[0], N_[0][:], p3_, p3_[:], -1.0, k.cf, mlo, ALU.mult, ALU.mult)
                    yield
                    p4 = nxt("pm", slots4)
                    mm(P, p4, p4[:], b["kt"], b["kt"][er, cc], b["rt"], b["rt"][er, cc], True, True)
                    tt(P, "dve", B1T[ci], B1T[ci][:], p4, p4[:], k.cf, mui, ALU.mult)
                    yield
                    p5 = nxt("pm", slots4)
                    mm(P, p5, p5[:], b["bt"], b["bt"][er, cc], b["rt"], b["rt"][er, cc], True, True)
                    tt(P, "dve", B2T[ci], B2T[ci][:], p5, p5[:], k.cf, mui, ALU.mult)
                    tt(P, "pool", R32_, R32_[:], M_[0], M_[0][:], k.cf, idf, ALU.add)
                    tt(P, "pool", R16_[0], R16_[0][:], M_[0], M_[0][:], k.cf, idf, ALU.add)
                    yield
                    cur = 0
                    for it_ in range(NL):
                        nx_ = 1 - cur
                        pn = nxt("pm", slots4)
                        mm(P, pn, pn[:], M_[cur], M_[cur][:], N_[cur], N_[cur][:], True, True)
                        cp(P, "act", N_[nx_], N_[nx_][:], pn, pn[:])
                        yield
                        if it_ < NL - 1:
                            pm_ = nxt("pm", slots4)
                            mm(P, pm_, pm_[:], N_[cur], N_[cur][:], M_[cur], M_[cur][:], True, True)
                            cp(P, "act", M_[nx_], M_[nx_][:], pm_, pm_[:])
                            yield
                        pr_ = nxt("pm", slots4)
                        mm(P, pr_, pr_[:], N_[nx_], N_[nx_][:], R16_[cur], R16_[cur][:], True, True)
                        tt(P, "dve", R32_, R32_[:], R32_, R32_[:], pr_, pr_[:], ALU.add)
                        last = it_ == NL - 1
                        dst = TIV[ci] if last else R16_[nx_]
                        cp(P, "pool", dst, dst[:], R32_, R32_[:])
                        yield
                        cur = nx_
                gens = [chain(c * 2 + e, c, e) for c in range(4) for e in range(2)]
                while gens:
                    alive = []
                    for g_ in gens:
                        try:
                            next(g_)
                            alive.append(g_)
                        except StopIteration:
                            pass
                    gens = alive
                for c in range(4):
                    cc = slice(c * 128, (c + 1) * 128)
                    stop(5)
                    G = nxt("p3", p3)
                    mm(P, G, G[:], b["kap"], b["kap"][:, cc], S16, S16[:], True, False)
                    for e in range(2):
                        ec = slice(e * 64, (e + 1) * 64)
                        mm(P, G, G[:, ec], A1T[c * 2 + e], A1T[c * 2 + e][:], tm["V"], tm["V"][:, c, ec], False, e == 1)
                    cp(P, "act", Gsb, Gsb[:], G, G[:])
                    Pp = nxt("p3", p3)
                    for e in range(2):
                        ec = slice(e * 64, (e + 1) * 64)
                        mm(P, Pp, Pp[:, ec], TIV[c * 2 + e], TIV[c * 2 + e][:], Gsb, Gsb[:, ec], True, True)
                    act(P, nP, nP[:], Pp, Pp[:], AF.Copy, scale=-1.0)
                    Y = nxt("p3", p3)
                    mm(P, Y, Y[:], b["rt"], b["rt"][:, cc], S16, S16[:], True, False)
                    for e in range(2):
                        ec = slice(e * 64, (e + 1) * 64)
                        mm(P, Y, Y[:, ec], B1T[c * 2 + e], B1T[c * 2 + e][:], tm["V"], tm["V"][:, c, ec], False, False)
                        mm(P, Y, Y[:, ec], B2T[c * 2 + e], B2T[c * 2 + e][:], nP, nP[:, ec], False, e == 1)
                    U = nxt("p3", p3)
                    mm(P, U, U[:], tm["K"], tm["K"][:, c, :], tm["V"], tm["V"][:, c, :], True, False)
                    mm(P, U, U[:], tm["B"], tm["B"][:, c, :], nP, nP[:], False, True)
                    for e in range(2):
                        er = slice(e * 64, (e + 1) * 64)
                        stt(P, "dve", S32, S32[er, er], S32, S32[er, er], WL[er, c:c + 1], U, U[er, er],
                            ALU.mult, ALU.add, extra=[WL])
                        cp(P, "pool", S16, S16[er, er], S32, S32[er, er])
                    stop(6)
                    cp(P, "act", ysb, ysb[:], Y, Y[:])
                    y3 = ysb[:].rearrange("p (e v) -> p e v", e=2)
                    P.op("dve", lambda e_: e_.reduce_sum(out=st1[:, 0:2], in_=y3, axis=AX.X), reads=[ysb], writes=[st1])
                    tt(P, "pool", ysq, ysq[:], ysb, ysb[:], ysb, ysb[:], ALU.mult)
                    P.op("dve", lambda e_: e_.reduce_sum(out=st1[:, 2:4], in_=ysq[:].rearrange("p (e v) -> p e v", e=2),
                                                         axis=AX.X), reads=[ysq], writes=[st1])
                    P.op("dve", lambda e_: e_.tensor_scalar_mul(out=st1[:, 0:2], in0=st1[:, 0:2], scalar1=1.0 / 64),
                         reads=[st1], writes=[st1])
                    tt(P, "dve", st1, st1[:, 4:6], st1, st1[:, 0:2], st1, st1[:, 0:2], ALU.mult)
                    stt(P, "dve", st1, st1[:, 6:8], st1, st1[:, 2:4], 1.0 / 64, st1, st1[:, 4:6], ALU.mult, ALU.subtract)
                    act(P, st1, st1[:, 6:8], st1, st1[:, 6:8], AF.Ln, extra=[k.epsb], bias=k.epsb[:, 2:3])
                    act(P, st1, st1[:, 6:8], st1, st1[:, 6:8], AF.Exp, scale=-0.5)
                    for e in range(2):
                        ec = slice(e * 64, (e + 1) * 64)
                        ts(P, "dve", yn, yn[:, ec], ysb, ysb[:, ec], st1[:, e:e + 1], st1[:, 6 + e:7 + e],
                           ALU.subtract, ALU.mult, extra=[st1])
                    YT = nxt("p3", p3)
                    mm(P, YT, YT[:], yn, yn[:], k.cf, idf, True, True)
                    stt(P, "dve", o1, o1[:], YT, YT[:], vc(V_LNW), f["bv"], f["bv"][:, cc], ALU.mult, ALU.add,
                        extra=[k.vec])
                    ys = yst[cnt["y"] % 2]
                    stt(P, "dve", ys, ys[:, cc], o1, o1[:], vc(V_LNB), f["gT"], f["gT"][:, cc], ALU.add, ALU.mult,
                        extra=[k.vec])
                ys = yst[cnt["y"] % 2]
                P.dma("sp", chy[cnt["y"] % 2], k.yT.ap()[1024 + hp * 128:1024 + (hp + 1) * 128, tb * 512:(tb + 1) * 512],
                      ys[:], reads=[ys], writes=[k.yT])
                cnt["y"] += 1


def phase_c(P, k, l, xin, xout):
    with P.scope():
        vo = l * VEC_L + V_NRM
        xs = P.sbuf("cx", [128, KC, 512], F32)
        acc = P.sbuf("cacc", [128, KC, 512], F32)
        uT = P.sbuf("cu", [128, KC, 512], BF16)
        mT = P.sbuf("cm", [128, KC, 512], BF16)
        hd = P.sbuf("chd", [128, KC, 512], BF16)
        yb = P.sbuf("cy", [128, KC, 512], BF16)
        wt = [P.sbuf("cw", [128, KC, 512], BF16) for _ in range(2)]
        gs = [[P.sbuf("cg", [128, 512], BF16) for _ in range(4)] for _ in range(3)]
        rs = P.sbuf("crs", [128, 512], F32)
        tmp = [P.sbuf("ctmp", [128, 512], F32) for _ in range(2)]
        pg = [P.psum("cpg", [128, 512]) for _ in range(3)]
        pbr = [P.psum("cpb", [128, 512]) for _ in range(3)]
        pss = P.psum("cpss", [128, 512])
        chw = [P.chan("cw") for _ in range(2)]
        chx = P.chan("cx")
        chu = P.chan("cu")
        chyy = P.chan("cy")
        cho = P.chan("co")
        order = []
        for ng in range(4):
            order += [(k.wg, i * 4 + ng) for i in range(3)] + [(k.wbr, ng)]
        order += [(k.wout, i) for i in range(4)]
        for q in range(4):
            order += [(k.wup_mlp, q * 4 + i) for i in range(4)] + [(k.wdn, q * 4 + i) for i in range(4)]
        NT = len(order)
        seq = [0]

        def loadw(gidx):
            src, ti = order[gidx % NT]
            i = gidx % 2
            P.dma("pool", chw[i], wt[i][:].rearrange("p c n -> p (c n)"), src.ap()[l, ti], reads=[src], writes=[wt[i]])

        def nextw():
            g = seq[0]
            seq[0] += 1
            if g + 1 < NT * 8:
                loadw(g + 1)
            return wt[g % 2]
        loadw(0)
        xv = xin.ap().rearrange("(c p) t -> p c t", p=128)
        ov = xout.ap().rearrange("(c p) t -> p c t", p=128)
        uv = k.uT.ap().rearrange("(c p) t -> p c t", p=128)
        yv = k.yT.ap().rearrange("(c p) t -> p c t", p=128)
        gi = [0]

        def gemm(w, j, rhs_b, out_ps):
            for c in range(KC):
                mm(P, out_ps, out_ps[:], w, w[:, c, j * 128:(j + 1) * 128], rhs_b, rhs_b[:, c, :], c == 0, c == KC - 1)

        def post_norm(gcol):
            rms_stats(P, k, acc, hd, pss, rs)
            for c in range(KC):
                stt(P, "dve", acc, acc[:, c, :], acc, acc[:, c, :], k.vec[:, gcol + c:gcol + c + 1], rs, rs[:],
                    ALU.mult, ALU.mult, extra=[k.vec])
                tt(P, "dve", xs, xs[:, c, :], xs, xs[:, c, :], acc, acc[:, c, :], ALU.add)
        for tb in range(8):
            cs = slice(tb * 512, (tb + 1) * 512)
            for g in range(4):
                gsl = slice(4 * g, 4 * g + 4)
                P.dma("sp", chx, xs[:, gsl, :], xv[:, gsl, cs], reads=[xin], writes=[xs])
                P.dma("sp", chu, uT[:, gsl, :], uv[:, gsl, cs], reads=[k.uT], writes=[uT])
                P.dma("sp", chyy, yb[:, gsl, :], yv[:, gsl, cs], reads=[k.yT], writes=[yb])
            for ng in range(4):
                for i in range(3):
                    w = nextw()
                    for j in range(4):
                        p_ = pg[gi[0] % 3]
                        gi[0] += 1
                        gemm(w, j, uT, p_)
                        act(P, gs[i][j], gs[i][j][:], p_, p_[:], AF.Sigmoid)
                w = nextw()
                for j in range(4):
                    n = ng * 4 + j
                    for i, (k0, nk) in enumerate(((0, 4), (4, 4), (8, 8))):
                        for c in range(nk):
                            mm(P, pbr[i], pbr[i][:], w, w[:, k0 + c, j * 128:(j + 1) * 128], yb, yb[:, k0 + c, :],
                               c == 0, c == nk - 1)
                    tt(P, "dve", tmp[0], tmp[0][:], pbr[0], pbr[0][:], gs[0][j], gs[0][j][:], ALU.mult)
                    tt(P, "dve", tmp[1], tmp[1][:], pbr[1], pbr[1][:], gs[1][j], gs[1][j][:], ALU.mult)
                    tt(P, "dve", tmp[0], tmp[0][:], tmp[0], tmp[0][:], tmp[1], tmp[1][:], ALU.add)
                    tt(P, "dve", tmp[1], tmp[1][:], pbr[2], pbr[2][:], gs[2][j], gs[2][j][:], ALU.mult)
                    tt(P, "dve", mT, mT[:, n, :], tmp[0], tmp[0][:], tmp[1], tmp[1][:], ALU.add)
            for t_ in range(4):
                w = nextw()
                for j in range(4):
                    p_ = pg[gi[0] % 3]
                    gi[0] += 1
                    gemm(w, j, mT, p_)
                    cp(P, "act", acc, acc[:, t_ * 4 + j, :], p_, p_[:])
            post_norm(vo + 16)
            rms_stats(P, k, xs, hd, pss, rs)
            for c in range(KC):
                stt(P, "dve", mT, mT[:, c, :], xs, xs[:, c, :], k.vec[:, vo + 32 + c:vo + 32 + c + 1], rs, rs[:],
                    ALU.mult, ALU.mult, extra=[k.vec])
            for q in range(4):
                for t_ in range(4):
                    w = nextw()
                    for j in range(4):
                        p_ = pg[gi[0] % 3]
                        gi[0] += 1
                        gemm(w, j, mT, p_)
                        tq = tmp[gi[0] % 2]
                        P.op("dve", lambda e: e.tensor_scalar_max(out=tq[:], in0=p_[:], scalar1=0.0), reads=[p_], writes=[tq])
                        tt(P, "dve", hd, hd[:, t_ * 4 + j, :], tq, tq[:], tq, tq[:], ALU.mult)
                for cg in range(4):
                    w = nextw()
                    for j in range(4):
                        p_ = pg[gi[0] % 3]
                        gi[0] += 1
                        gemm(w, j, hd, p_)
                        n = cg * 4 + j
                        if q == 0:
                            cp(P, "act", acc, acc[:, n, :], p_, p_[:])
                        else:
                            tt(P, "dve", acc, acc[:, n, :], acc, acc[:, n, :], p_, p_[:], ALU.add)
            post_norm(vo + 48)
            for g in range(4):
                gsl = slice(4 * g, 4 * g + 4)
                P.dma("sp", cho, ov[:, gsl, cs], xs[:, gsl, :], reads=[xs], writes=[xout])


def build(dbg=(), stages="abcdC", nl=L):
    nc = bass.Bass("TRN2", target_bir_lowering=False)
    P = Prog(nc)
    k = K()

    def dk(name):
        return "ExternalOutput" if name in dbg else "Internal"

    def inp(name, shape):
        return P.dram(name, shape, F32, kind="ExternalInput")
    k.xT = inp("xT", [D, T])
    k.consts = inp("consts", [128, NCONST])
    k.vecs = inp("vecs", [128, L * VEC_L])
    k.wqk = inp("wqk", [L, 4, 128, KC * 512])
    k.wv = inp("wv", [L, 2, 128, KC * 512])
    k.wf = inp("wf", [L, 128, KC * 4])
    k.wrw = inp("wrw", [L, 7, 128, KC * 512])
    k.wup = inp("wup", [L, 64, 1024])
    k.aup = inp("aup", [L, 64, 1024])
    k.gup = inp("gup", [L, 160, 1024])
    if "C" in stages:
        k.wg = inp("wg", [L, 12, 128, KC * 512])
        k.wbr = inp("wbr", [L, 4, 128, KC * 512])
        k.wout = inp("wout", [L, 4, 128, KC * 512])
        k.wup_mlp = inp("wmup", [L, 16, 128, KC * 512])
        k.wdn = inp("wmdn", [L, 16, 128, KC * 512])
    k.uT = P.dram("uT", [D, T], BF16, kind=dk("uT"))
    k.qkT = P.dram("qkT", [NQK, 128, T], BF16, kind=dk("qkT"))
    k.vs = P.dram("vs", [8, 128, 32, 128], BF16, kind=dk("vs"))
    k.zsT = P.dram("zsT", [NRW * 128, T], F32, kind=dk("zsT"))
    k.cdram = P.dram("cdram", [4, T], F32, kind=dk("cdram"))
    k.yT = P.dram("yT", [D, T], BF16, kind=dk("yT"))
    k.x1T = P.dram("x1T", [D, T], F32, kind=dk("x1T"))
    k.out = P.dram("outT", [D, T], F32, kind="ExternalOutput")
    k.csum = P.sbuf("csum", [128, 4, 32], F32)
    load_consts(P, k)
    for l in range(nl):
        xin = k.xT if l == 0 else k.x1T
        xout = k.x1T if l == 0 and nl == 2 else k.out
        if "a" in stages:
            phase_norm(P, k, xin, l * VEC_L + V_NRM + 0, k.uT)
        if "b" in stages:
            phase_b1(P, k, l)
        if "c" in stages:
            phase_b2(P, k, l)
        if "d" in stages:
            phase_b3(P, k, l)
        if "C" in stages:
            phase_c(P, k, l, xin, xout)
    P.close()
    print("instructions:", P.n_inst)
    return nc


QA0, KA0, VA0, QB0, KB0, VB0, FB0, RW0, GT0 = 0, 512, 1024, 1536, 2048, 2560, 3072, 3076, 6436


def _tile(w):
    kk, n = w.shape
    out = np.zeros((kk // 128, 128, 512), np.float32)
    out[:, :, :n] = w.reshape(kk // 128, 128, n)
    return np.ascontiguousarray(out.transpose(1, 0, 2)).reshape(128, -1)


def make_consts():
    c = np.zeros((128, NCONST), np.float32)
    i = np.arange(128)
    c[:, C_ID:C_ID + 128] = np.eye(128)
    c[:, C_ONES:C_ONES + 128] = 1.0
    c[:, C_UTRI:C_UTRI + 128] = (i[:, None] <= i[None, :])
    c[:, C_NEGT:C_NEGT + 128] = -1.0 * (i[:, None] > i[None, :])
    c[:, C_MUP:C_MUP + 128] = (i[:, None] < i[None, :])
    c[:, C_MLO:C_MLO + 128] = (i[:, None] > i[None, :])
    c[:, C_BLK:C_BLK + 128] = ((i[:, None] // 64) == (i[None, :] // 64))
    q = np.arange(512)
    c[:, C_RST:C_RST + 512] = (q % 128 != 0)[None, :]
    for j in range(4):
        kk = j * 128 + i
        c[:, C_FOXB + j * 512:C_FOXB + (j + 1) * 512] = np.where(kk[:, None] <= q[None, :], 0.0, -30000.0)
        c[:, C_SBM + j * 512:C_SBM + (j + 1) * 512] = (kk[:, None] < q[None, :])
    return c


def prep_shared(inp, with_c=True):
    m = {}
    m["consts"] = make_consts()
    vec = np.zeros((128, L * VEC_L), np.float32)
    wqk = np.zeros((L, 4, 128, KC * 512), np.float32)
    wv = np.zeros((L, 2, 128, KC * 512), np.float32)
    wf = np.zeros((L, 128, KC * 4), np.float32)
    wrw = np.zeros((L, 7, 128, KC * 512), np.float32)
    for l in range(L):
        o = l * VEC_L
        for wi, nm in enumerate(("norm_mix_pre", "norm_mix_post", "norm_mlp_pre", "norm_mlp_post")):
            vec[:, o + V_NRM + wi * 16:o + V_NRM + wi * 16 + 16] = inp[nm][l].reshape(16, 128).T
        mu_p = np.zeros(NRW * 128, np.float32)
        mu_p[:3360] = inp["rwkv_mu"][l]
        vec[:, o + V_MU:o + V_MU + NRW] = mu_p.reshape(NRW, 128).T
        for col, nm in ((V_W0, "rwkv_w0"), (V_A0, "rwkv_a0"), (V_KK, "rwkv_k_k"), (V_KA, "rwkv_k_a"),
                        (V_RK, "rwkv_r_k"), (V_LNW, "rwkv_ln_w"), (V_LNB, "rwkv_ln_b")):
            vec[:, o + col:o + col + 8] = inp[nm][l].reshape(8, 128).T
        vec[:, o + V_BF:o + V_BF + 128] = np.tile(inp["b_forget"][l], 32)[None, :]
        w = inp["w_in"][l]
        for ti, c0 in enumerate((QA0, KA0, QB0, KB0)):
            wqk[l, ti] = _tile(w[:, c0:c0 + 512])
        wv[l, 0] = _tile(w[:, VA0:VA0 + 512])
        wv[l, 1] = _tile(w[:, VB0:VB0 + 512])
        wf[l] = np.ascontiguousarray(w[:, FB0:FB0 + 4].reshape(16, 128, 4).transpose(1, 0, 2)).reshape(128, 64)
        for ti in range(7):
            wrw[l, ti] = _tile(w[:, RW0 + ti * 512:min(RW0 + (ti + 1) * 512, RW0 + 3360)])
    m["vecs"] = vec
    m["wqk"], m["wv"], m["wf"], m["wrw"] = wqk, wv, wf, wrw
    m["wup"] = np.ascontiguousarray(inp["rwkv_w_up"])
    m["aup"] = np.ascontiguousarray(inp["rwkv_a_up"])
    m["gup"] = np.ascontiguousarray(inp["rwkv_g_up"])
    if with_c:
        wg = np.zeros((L, 12, 128, KC * 512), np.float32)
        wbr = np.zeros((L, 4, 128, KC * 512), np.float32)
        wout = np.zeros((L, 4, 128, KC * 512), np.float32)
        wmup = np.zeros((L, 16, 128, KC * 512), np.float32)
        wmdn = np.zeros((L, 16, 128, KC * 512), np.float32)
        for l in range(L):
            w = inp["w_in"][l]
            for t in range(12):
                wg[l, t] = _tile(w[:, GT0 + t * 512:GT0 + (t + 1) * 512])
            br = np.concatenate([inp["w_branch_a"][l], inp["w_branch_b"][l], inp["w_branch_c"][l]], 0)
            for t in range(4):
                wbr[l, t] = _tile(br[:, t * 512:(t + 1) * 512])
                wout[l, t] = _tile(inp["w_out"][l][:, t * 512:(t + 1) * 512])
            for t in range(16):
                wmup[l, t] = _tile(inp["w_mlp_up"][l][:, t * 512:(t + 1) * 512])
                q, cg = t // 4, t % 4
                wmdn[l, t] = _tile(inp["w_mlp_down"][l][q * 2048:(q + 1) * 2048, cg * 512:(cg + 1) * 512])
        m["wg"], m["wbr"], m["wout"], m["wmup"], m["wmdn"] = wg, wbr, wout, wmup, wmdn
    return m


_CACHE = {}


def kernel(**inputs):
    inp = {k_: np.asarray(v, dtype=np.float32) for k_, v in inputs.items()}
    if "nc" not in _CACHE:
        _CACHE["nc"] = build()
    nc = _CACHE["nc"]
    shared = prep_shared(inp)
    maps = []
    for c in range(NCORES):
        m = dict(shared)
        m["xT"] = np.ascontiguousarray(inp["x"][c].T)
        maps.append(m)
    res = run_bass_kernel_spmd(nc, maps, core_ids=list(range(NCORES)))
    out = np.stack([np.ascontiguousarray(np.asarray(res.results[c]["outT"]).T) for c in range(NCORES)], 0)
    return out.astype(np.float32)
```

```python
import contextlib
import numpy as np
import concourse.bass as bass
import concourse.mybir as mybir
from concourse.bass_utils import run_bass_kernel_spmd

F32 = mybir.dt.float32
BF16 = mybir.dt.bfloat16
AF = mybir.ActivationFunctionType
ALU = mybir.AluOpType
AX = mybir.AxisListType


class Counter:
    def __init__(self, prog, name, step, epoch):
        self.prog, self.name, self.step, self.epoch = prog, name, step, epoch
        self.n = 0
        self.sems = []

    def next(self):
        self.n += 1
        ep = (self.n - 1) // self.epoch
        while len(self.sems) <= ep:
            self.sems.append(self.prog.new_sem(f"{self.name}_{len(self.sems)}"))
        return self.n

    def sem_val(self, n):
        ep = (n - 1) // self.epoch
        return self.sems[ep], ((n - 1) % self.epoch + 1) * self.step


class Buf:
    def __init__(self, t, name=""):
        self.t = t
        self.name = name
        self.last_write = None
        self.reads = {}

    def __getitem__(self, idx):
        return self.t[idx]

    def ap(self):
        return self.t.ap()


class Prog:
    ENGS = ("pe", "act", "dve", "pool", "sp")

    def __init__(self, nc):
        self.nc = nc
        self.root = contextlib.ExitStack()
        self.stacks = [self.root]
        self.eng = {"pe": nc.tensor, "act": nc.scalar, "dve": nc.vector,
                    "pool": nc.gpsimd, "sp": nc.sync}
        self.cnt = {e: Counter(self, "c" + e, 1, 30000) for e in self.ENGS}
        self.observed = {e: {} for e in self.ENGS}
        self.all_counters = list(self.cnt.values())
        self.n_inst = 0
        self.uid = 0

    def new_sem(self, name):
        return self.root.enter_context(self.nc.semaphore(name))

    def chan(self, name, step=16):
        self.uid += 1
        c = Counter(self, f"d{name}{self.uid}", step, 1800 if step == 16 else 30000)
        self.all_counters.append(c)
        return c

    @contextlib.contextmanager
    def scope(self):
        st = contextlib.ExitStack()
        self.stacks.append(st)
        try:
            yield
        finally:
            self.barrier()
            self.stacks.pop()
            st.close()

    def sbuf(self, name, shape, dtype):
        self.uid += 1
        t = self.stacks[-1].enter_context(
            self.nc.sbuf_tensor(f"{name}_{self.uid}", list(shape), dtype))
        return Buf(t, name)

    def psum(self, name, shape, dtype=F32):
        self.uid += 1
        t = self.stacks[-1].enter_context(
            self.nc.psum_tensor(f"{name}_{self.uid}", list(shape), dtype))
        return Buf(t, name)

    def dram(self, name, shape, dtype, kind="Internal"):
        return Buf(self.nc.dram_tensor(name, list(shape), dtype, kind=kind), name)

    def _need(self, engine, tok, waits):
        if tok is None:
            return
        c, n, teng = tok
        if teng == "pe" and engine == "pe":
            return
        if teng == "dma":
            n = c.n
        ob = self.observed[engine]
        if ob.get(id(c), 0) >= n:
            return
        ob[id(c)] = n
        waits.append((c, n))

    def op(self, engine, fn, reads=(), writes=(), chan=None):
        waits = []
        for b in reads:
            self._need(engine, b.last_write, waits)
        for b in writes:
            self._need(engine, b.last_write, waits)
            for t in b.reads.values():
                self._need(engine, t, waits)
        e = self.eng[engine]
        for c, n in waits:
            s, v = c.sem_val(n)
            e.wait_ge(s, v)
        ins = fn(e)
        c = chan if chan is not None else self.cnt[engine]
        n = c.next()
        s, _ = c.sem_val(n)
        ins.then_inc(s, c.step)
        tok = (c, n, engine if chan is None else "dma")
        for b in reads:
            b.reads[id(c)] = tok
        for b in writes:
            b.last_write = tok
            b.reads = {}
        self.n_inst += 1
        return tok

    def dma(self, queue, chan, out, in_, reads=(), writes=(), **kw):
        return self.op(queue, lambda e: e.dma_start(out=out, in_=in_, **kw),
                       reads=reads, writes=writes, chan=chan)

    def barrier(self):
        toks = [(c, c.n, "x") for c in self.all_counters if c.n > 0]
        for engine in self.ENGS:
            waits = []
            for t in toks:
                self._need(engine, t, waits)
            e = self.eng[engine]
            for c, n in waits:
                s, v = c.sem_val(n)
                e.wait_ge(s, v)

    def close(self):
        self.barrier()
        self.root.close()


D = 2048
T = 4096
KC = 16
L = 2
NCORES = 4
SCALE = 128.0 ** -0.5
EPS = 1e-6
NQK = 16
NRW = 27
V_NRM = 0
V_MU = 64
V_W0, V_A0, V_KK, V_KA, V_RK, V_LNW, V_LNB = 96, 104, 112, 120, 128, 136, 144
V_BF = 152
VEC_L = 288
C_ID = 0
C_ONES = 128
C_UTRI = 256
C_NEGT = 384
C_MUP = 512
C_MLO = 640
C_BLK = 768
C_RST = 896
C_FOXB = 1408
NCF = 1408 + 2048
C_SBM = NCF
NCONST = NCF + 2048
NCB = 896
EM05 = float(np.exp(-0.5))


class K:
    pass


def mm(P, ps, out_ap, lb, lhsT, rb, rhs, start, stop):
    P.op("pe", lambda e: e.matmul(out_ap, lhsT, rhs, start=start, stop=stop),
         reads=[lb, rb], writes=[ps])


def act(P, out_b, out_ap, in_b, in_ap, func, extra=(), **kw):
    P.op("act", lambda e: e.activation(out=out_ap, in_=in_ap, func=func, **kw),
         reads=[in_b] + list(extra), writes=[out_b])


def tt(P, eng, out_b, out_ap, a_b, a_ap, b_b, b_ap, op):
    P.op(eng, lambda e: e.tensor_tensor(out=out_ap, in0=a_ap, in1=b_ap, op=op),
         reads=[a_b, b_b], writes=[out_b])


def stt(P, eng, out_b, out_ap, a_b, a_ap, scalar, b_b, b_ap, op0, op1, extra=()):
    P.op(eng, lambda e: e.scalar_tensor_tensor(out=out_ap, in0=a_ap, scalar=scalar, in1=b_ap, op0=op0, op1=op1),
         reads=[a_b, b_b] + list(extra), writes=[out_b])


def ts(P, eng, out_b, out_ap, a_b, a_ap, s1, s2, op0, op1, extra=()):
    P.op(eng, lambda e: e.tensor_scalar(out=out_ap, in0=a_ap, scalar1=s1, scalar2=s2, op0=op0, op1=op1),
         reads=[a_b] + list(extra), writes=[out_b])


def cp(P, eng, out_b, out_ap, in_b, in_ap):
    if eng == "act":
        P.op("act", lambda e: e.copy(out=out_ap, in_=in_ap), reads=[in_b], writes=[out_b])
    else:
        P.op(eng, lambda e: e.tensor_copy(out=out_ap, in_=in_ap), reads=[in_b], writes=[out_b])


def load_consts(P, k):
    k.cf = P.sbuf("cf", [128, NCF], F32)
    k.cb = P.sbuf("cb", [128, NCB], BF16)
    k.sbm = P.sbuf("sbm", [128, 2048], BF16)
    k.vec = P.sbuf("vec", [128, L * VEC_L], F32)
    k.epsb = P.sbuf("epsb", [128, 4], F32)
    k.omka = P.sbuf("omka", [128, L * 8], F32)
    for i, v in enumerate((EPS, 1.0, 64e-5, 1e-24)):
        P.op("dve", lambda e: e.memset(k.epsb[:, i:i + 1], v), writes=[k.epsb])
    ch = P.chan("const")
    P.dma("sp", ch, k.cf[:], k.consts.ap()[:, 0:NCF], reads=[k.consts], writes=[k.cf])
    P.dma("pool", ch, k.cb[:], k.consts.ap()[:, 0:NCB], reads=[k.consts], writes=[k.cb])
    P.dma("pool", ch, k.sbm[:], k.consts.ap()[:, C_SBM:C_SBM + 2048], reads=[k.consts], writes=[k.sbm])
    P.dma("sp", ch, k.vec[:], k.vecs.ap(), reads=[k.vecs], writes=[k.vec])
    for l in range(L):
        o = l * VEC_L + V_KA
        ts(P, "dve", k.omka, k.omka[:, l * 8:l * 8 + 8], k.vec, k.vec[:, o:o + 8], -1.0, 1.0, ALU.mult, ALU.add)


def rms_stats(P, k, src, sq, ps, rs):
    for g in range(4):
        act(P, sq, sq[:, 4 * g:4 * g + 4, :], src, src[:, 4 * g:4 * g + 4, :], AF.Square)
    for c in range(KC):
        mm(P, ps, ps[:], k.cb, k.cb[:, C_ONES:C_ONES + 128], sq, sq[:, c, :], c == 0, c == KC - 1)
    act(P, rs, rs[:], ps, ps[:], AF.Ln, extra=[k.epsb], scale=1.0 / D, bias=k.epsb[:, 0:1])
    act(P, rs, rs[:], rs, rs[:], AF.Exp, scale=-0.5)


def phase_norm(P, k, src, gain_col, dst):
    with P.scope():
        xs = [P.sbuf("nx", [128, KC, 512], F32) for _ in range(2)]
        sq = [P.sbuf("nsq", [128, KC, 512], BF16) for _ in range(2)]
        us = [P.sbuf("nu", [128, KC, 512], BF16) for _ in range(2)]
        rs = [P.sbuf("nr", [128, 512], F32) for _ in range(2)]
        ps = [P.psum("nps", [128, 512]) for _ in range(2)]
        chl = [P.chan("nl") for _ in range(2)]
        chs = [P.chan("ns") for _ in range(2)]
        sv = src.ap().rearrange("(c p) t -> p c t", p=128)
        dv = dst.ap().rearrange("(c p) t -> p c t", p=128)
        nb = T // 512

        def load(tb):
            i = tb % 2
            for g in range(4):
                P.dma("sp", chl[i], xs[i][:, 4 * g:4 * g + 4, :],
                      sv[:, 4 * g:4 * g + 4, tb * 512:(tb + 1) * 512], reads=[src], writes=[xs[i]])
        load(0)
        for tb in range(nb):
            i = tb % 2
            if tb + 1 < nb:
                load(tb + 1)
            rms_stats(P, k, xs[i], sq[i], ps[i], rs[i])
            for c in range(KC):
                stt(P, "dve", us[i], us[i][:, c, :], xs[i], xs[i][:, c, :], k.vec[:, gain_col + c:gain_col + c + 1],
                    rs[i], rs[i][:], ALU.mult, ALU.mult, extra=[k.vec])
            for g in range(4):
                P.dma("sp", chs[i], dv[:, 4 * g:4 * g + 4, tb * 512:(tb + 1) * 512],
                      us[i][:, 4 * g:4 * g + 4, :], reads=[us[i]], writes=[dst])


def phase_b1(P, k, l):
    with P.scope():
        uT = P.sbuf("uT", [128, KC, T], BF16)
        wt = [P.sbuf("wt", [128, KC, 512], BF16) for _ in range(2)]
        wft = P.sbuf("wft", [128, KC, 4], BF16)
        qst = [P.sbuf("qst", [128, 512], BF16) for _ in range(2)]
        zst = [P.sbuf("zst", [128, 512], F32) for _ in range(2)]
        vst = [P.sbuf("vst", [128, 512], BF16) for _ in range(2)]
        zraw = [P.sbuf("zraw", [128, 513], F32) for _ in range(2)]
        dt = P.sbuf("dt", [128, 512], F32)
        fsb = P.sbuf("fsb", [128, 128], F32)
        fsp = P.sbuf("fsp", [128, 128], F32)
        tot = P.sbuf("tot", [128, 128], F32)
        pre = P.sbuf("pre", [128, 128], F32)
        cT = P.sbuf("cT", [32, 4, 128], F32)
        ps = [P.psum("b1ps", [128, 512]) for _ in range(4)]
        psf = P.psum("psf", [128, 128])
        psc = P.psum("psc", [128, 128])
        pst = P.psum("pst", [128, 128])
        chu = P.chan("u")
        chw = [P.chan("w") for _ in range(2)]
        chq = [P.chan("q") for _ in range(2)]
        chz = [P.chan("z") for _ in range(2)]
        chv = [P.chan("v") for _ in range(2)]
        chm = P.chan("m")

        uv = k.uT.ap().rearrange("(c p) t -> p c t", p=128)
        for hh in range(2):
            for g in range(4):
                P.dma("sp", chu, uT[:, 4 * g:4 * g + 4, hh * 2048:(hh + 1) * 2048],
                      uv[:, 4 * g:4 * g + 4, hh * 2048:(hh + 1) * 2048], reads=[k.uT], writes=[uT])
        P.dma("pool", chm, wft[:].rearrange("p c n -> p (c n)"), k.wf.ap()[l], reads=[k.wf], writes=[wft])

        tiles = [("qk", k.wqk, i) for i in range(4)] + [("v", k.wv, i) for i in range(2)] + \
                [("rw", k.wrw, i) for i in range(7)]

        def loadw(idx):
            kind, src, ti = tiles[idx]
            i = idx % 2
            P.dma("pool", chw[i], wt[i][:].rearrange("p c n -> p (c n)"), src.ap()[l, ti], reads=[src], writes=[wt[i]])
        loadw(0)
        pr = [0]
        zi = [0]
        for idx, (kind, src, ti) in enumerate(tiles):
            w = wt[idx % 2]
            if idx + 1 < len(tiles):
                loadw(idx + 1)
            if kind == "qk":
                for j in range(4):
                    nci = ti * 4 + j
                    isq = ti in (0, 2)
                    for tb in range(8):
                        p_ = ps[pr[0] % 4]
                        pr[0] += 1
                        for c in range(KC):
                            mm(P, p_, p_[:], w, w[:, c, j * 128:(j + 1) * 128], uT, uT[:, c, tb * 512:(tb + 1) * 512],
                               c == 0, c == KC - 1)
                        st = qst[tb % 2]
                        act(P, st, st[:], p_, p_[:], AF.Copy, scale=SCALE if isq else 1.0)
                        P.dma("sp", chq[tb % 2], k.qkT.ap()[nci, :, tb * 512:(tb + 1) * 512], st[:],
                              reads=[st], writes=[k.qkT])
            elif kind == "v":
                vview = k.vs.ap()[ti * 4:ti * 4 + 4].rearrange("h p s d -> p s h d")
                for s in range(32):
                    p_ = ps[pr[0] % 4]
                    pr[0] += 1
                    for c in range(KC):
                        mm(P, p_, p_[:], uT, uT[:, c, s * 128:(s + 1) * 128], w, w[:, c, :], c == 0, c == KC - 1)
                    st = vst[s % 2]
                    cp(P, "dve", st, st[:], p_, p_[:])
                    P.dma("sp", chv[s % 2], vview[:, s, :, :], st[:].rearrange("p (h d) -> p h d", h=4),
                          reads=[st], writes=[k.vs])
                if ti == 1:
                    for s in range(32):
                        for c in range(KC):
                            mm(P, psf, psf[:, 4 * s:4 * s + 4], uT, uT[:, c, s * 128:(s + 1) * 128], wft, wft[:, c, :],
                               c == 0, c == KC - 1)
                    vb = l * VEC_L + V_BF
                    tt(P, "dve", fsb, fsb[:], psf, psf[:], k.vec, k.vec[:, vb:vb + 128], ALU.add)
                    act(P, fsp, fsp[:], fsb, fsb[:], AF.Exp, scale=-1.0)
                    act(P, fsp, fsp[:], fsp, fsp[:], AF.Ln, extra=[k.epsb], bias=k.epsb[:, 1:2])
                    mm(P, psc, psc[:], k.cf, k.cf[:, C_UTRI:C_UTRI + 128], fsp, fsp[:], True, True)
                    mm(P, pst, pst[:], k.cf, k.cf[:, C_ONES:C_ONES + 128], fsp, fsp[:], True, True)
                    cp(P, "dve", tot, tot[:], pst, pst[:])
                    P.op("dve", lambda e: e.memset(pre[:, 0:4], 0.0), writes=[pre])
                    for s in range(1, 32):
                        tt(P, "dve", pre, pre[:, 4 * s:4 * s + 4], pre, pre[:, 4 * s - 4:4 * s],
                           tot, tot[:, 4 * s - 4:4 * s], ALU.add)
                    tt(P, "dve", k.csum, k.csum[:].rearrange("p h s -> p s h"),
                       psc, psc[:].rearrange("p (s h) -> p s h", h=4),
                       pre, pre[:].rearrange("p (s h) -> p s h", h=4), ALU.add)
                    for h in range(4):
                        mm(P, ps[h], ps[h][0:32, 0:128], k.csum, k.csum[:, h, :], k.cf, k.cf[:, C_ID:C_ID + 128],
                           True, True)
                        act(P, cT, cT[:, h, :], ps[h], ps[h][0:32, 0:128], AF.Copy, scale=-1.0)
                    P.dma("sp", chm, k.cdram.ap().rearrange("h (s p) -> s h p", p=128), cT[:],
                          reads=[cT], writes=[k.cdram])
            else:
                nj = 4 if ti < 6 else 3
                for j in range(nj):
                    nci = ti * 4 + j
                    M = 32 if nci == 26 else 128
                    mcol = l * VEC_L + V_MU + nci
                    for tb in range(8):
                        p_ = ps[pr[0] % 4]
                        pr[0] += 1
                        for c in range(KC):
                            mm(P, p_, p_[0:M, :], w, w[:, c, j * 128:j * 128 + M], uT, uT[:, c, tb * 512:(tb + 1) * 512],
                               c == 0, c == KC - 1)
                        zr = zraw[zi[0] % 2]
                        zp = zraw[(zi[0] + 1) % 2]
                        zi[0] += 1
                        if tb == 0:
                            P.op("dve", lambda e: e.memset(zr[0:M, 0:1], 0.0), writes=[zr])
                        else:
                            cp(P, "act", zr, zr[0:M, 0:1], zp, zp[0:M, 512:513])
                        cp(P, "act", zr, zr[0:M, 1:513], p_, p_[0:M, :])
                        tt(P, "dve", dt, dt[0:M, :], zr, zr[0:M, 0:512], zr, zr[0:M, 1:513], ALU.subtract)
                        st = zst[tb % 2]
                        stt(P, "dve", st, st[0:M, :], dt, dt[0:M, :],
                            k.vec[0:M, mcol:mcol + 1], zr, zr[0:M, 1:513], ALU.mult, ALU.add, extra=[k.vec])
                        P.dma("sp", chz[tb % 2], k.zsT.ap()[nci * 128:nci * 128 + M, tb * 512:(tb + 1) * 512],
                              st[0:M, :], reads=[st], writes=[k.zsT])


def phase_b2(P, k, l):
    with P.scope():
        qT = [P.sbuf("qT", [128, T], BF16) for _ in range(2)]
        kT = [P.sbuf("kT", [128, T], BF16) for _ in range(2)]
        vv = [P.sbuf("vv", [128, 32, 128], BF16) for _ in range(2)]
        cq = P.sbuf("cq", [128, T], F32)
        tts = [P.sbuf("tt", [128, 512], F32) for _ in range(3)]
        t2s3 = [P.sbuf("t2", [128, 512], F32) for _ in range(3)]
        e2s = [P.sbuf("e2", [128, 512], F32) for _ in range(3)]
        spb3 = [P.sbuf("spb", [128, 512], BF16) for _ in range(3)]
        pps = [P.sbuf("pp", [128, 512], BF16) for _ in range(3)]
        Rt = P.sbuf("Rt", [128, 512], F32)
        rec = P.sbuf("rec", [128, 512], F32)
        yst = [P.sbuf("yst", [128, 512], BF16) for _ in range(2)]
        pS = [P.psum("pS", [128, 512]) for _ in range(2)]
        pB = [P.psum("pB", [128, 512]) for _ in range(2)]
        pC = [P.psum("pC", [128, 512]) for _ in range(2)]
        pO = [P.psum("pO", [128, 512]) for _ in range(2)]
        chl = [P.chan("al") for _ in range(2)]
        chc = P.chan("ac")
        chy = [P.chan("ay") for _ in range(2)]
        heads = [("sb", h) for h in range(4)] + [("fox", h) for h in range(4)]

        def load(hi):
            kind, h = heads[hi]
            i = hi % 2
            base = 0 if kind == "sb" else 8
            P.dma("sp", chl[i], qT[i][:], k.qkT.ap()[base + h], reads=[k.qkT], writes=[qT[i]])
            P.dma("sp", chl[i], kT[i][:], k.qkT.ap()[base + 4 + h], reads=[k.qkT], writes=[kT[i]])
            P.dma("sp", chl[i], vv[i][:], k.vs.ap()[(0 if kind == "sb" else 4) + h], reads=[k.vs], writes=[vv[i]])
        load(0)
        it = [0]
        yi = [0]
        ones = k.cb[:, C_ONES:C_ONES + 128]
        for hi, (kind, h) in enumerate(heads):
            i = hi % 2
            if hi + 1 < len(heads):
                load(hi + 1)
            q_, k_, v_ = qT[i], kT[i], vv[i]
            if kind == "fox":
                P.dma("sp", chc, cq[:], k.cdram.ap()[h:h + 1, :].partition_broadcast(128), reads=[k.cdram], writes=[cq])
            for QB in range(8):
                nkb = 4 * QB + 4
                O = pO[QB % 2]
                qs = slice(QB * 512, (QB + 1) * 512)
                if kind == "fox":
                    DN = pC[QB % 2]
                    for kb in range(nkb):
                        n = it[0]
                        it[0] += 1
                        S = pS[n % 2]
                        mm(P, S, S[:], k_, k_[:, kb * 128:(kb + 1) * 128], q_, q_[:, qs], True, True)
                        t_ = tts[n % 3]
                        stt(P, "dve", t_, t_[:], S, S[:], k.csum[:, h, kb:kb + 1], cq, cq[:, qs], ALU.add, ALU.add,
                            extra=[k.csum])
                        if kb >= 4 * QB:
                            j = kb - 4 * QB
                            tt(P, "dve", t_, t_[:], t_, t_[:], k.cf, k.cf[:, C_FOXB + j * 512:C_FOXB + (j + 1) * 512],
                               ALU.add)
                        p_ = pps[n % 3]
                        act(P, p_, p_[:], t_, t_[:], AF.Exp)
                        mm(P, O, O[:], v_, v_[:, kb, :], p_, p_[:], kb == 0, kb == nkb - 1)
                        mm(P, DN, DN[:], k.cb, ones, p_, p_[:], kb == 0, kb == nkb - 1)
                    P.op("dve", lambda e: e.reciprocal(out=rec[:], in_=DN[:]), reads=[DN], writes=[rec])
                    ys = yst[yi[0] % 2]
                    tt(P, "dve", ys, ys[:], O, O[:], rec, rec[:], ALU.mult)
                    row0 = 512 + h * 128
                else:
                    P.op("dve", lambda e: e.memset(Rt[:], 0.0), writes=[Rt])
                    for kb in reversed(range(nkb)):
                        n = it[0]
                        it[0] += 1
                        S = pS[n % 2]
                        Bp = pB[n % 2]
                        Cp = pC[n % 2]
                        mm(P, S, S[:], k_, k_[:, kb * 128:(kb + 1) * 128], q_, q_[:, qs], True, True)
                        e1 = tts[n % 3]
                        e2 = e2s[n % 3]
                        sb_ = spb3[n % 3]
                        t2 = t2s3[n % 3]
                        act(P, e1, e1[:], S, S[:], AF.Exp)
                        act(P, e2, e2[:], S, S[:], AF.Exp, scale=-1.0)
                        act(P, sb_, sb_[:], e1, e1[:], AF.Ln, extra=[k.epsb], bias=k.epsb[:, 1:2])
                        act(P, e2, e2[:], e2, e2[:], AF.Ln, extra=[k.epsb], bias=k.epsb[:, 1:2])
                        diag = kb >= 4 * QB
                        if diag:
                            j = kb - 4 * QB
                            msk = k.sbm[:, j * 512:(j + 1) * 512]
                            tt(P, "dve", sb_, sb_[:], sb_, sb_[:], k.sbm, msk, ALU.mult)
                        mm(P, Bp, Bp[:], k.cb, k.cb[:, C_NEGT:C_NEGT + 128], sb_, sb_[:], True, True)
                        mm(P, Cp, Cp[:], k.cb, ones, sb_, sb_[:], True, True)
                        tt(P, "dve", t2, t2[:], Bp, Bp[:], Rt, Rt[:], ALU.add)
                        tt(P, "dve", t2, t2[:], t2, t2[:], e2, e2[:], ALU.subtract)
                        p_ = pps[n % 3]
                        act(P, p_, p_[:], t2, t2[:], AF.Exp)
                        if diag:
                            tt(P, "dve", p_, p_[:], p_, p_[:], k.sbm, msk, ALU.mult)
                        mm(P, O, O[:], v_, v_[:, kb, :], p_, p_[:], kb == nkb - 1, kb == 0)
                        tt(P, "dve", Rt, Rt[:], Rt, Rt[:], Cp, Cp[:], ALU.subtract)
                    ys = yst[yi[0] % 2]
                    cp(P, "act", ys, ys[:], O, O[:])
                    row0 = h * 128
                P.dma("sp", chy[yi[0] % 2], k.yT.ap()[row0:row0 + 128, qs], ys[:], reads=[ys], writes=[k.yT])
                yi[0] += 1


class _Stop(Exception):
    pass


def phase_b3(P, k, l):
    try:
        _phase_b3(P, k, l)
    except _Stop:
        pass


def _phase_b3(P, k, l):
    import os
    STOP = int(os.environ.get("B3_STOP", "99"))

    def stop(n):
        if STOP <= n:
            raise _Stop()
    with P.scope():
        vo = l * VEC_L
        NL = 6
        ld = {n: [P.sbuf("ld" + n, [128, 512], F32) for _ in range(2)] for n in ("r", "k", "v", "wa", "g0", "g1")}
        f32n = ("lw", "a", "kk", "tmp", "kp", "kka", "cl", "ex", "e4", "bv", "sq", "gT")
        f = {n: P.sbuf("f" + n, [128, 512], F32) for n in f32n}
        b16n = ("thad", "sg", "sg1", "rt", "kt", "bt", "kap", "Kp", "Bp", "vb")
        b = {n: P.sbuf("b" + n, [128, 512], BF16) for n in b16n}
        WL = P.sbuf("WL", [128, 4], F32)
        tm = {n: P.sbuf("tm" + n, [128, 4, 128], BF16) for n in ("V", "K", "B")}
        wa_t = P.sbuf("wa_t", [128, 128], BF16)
        g0_t = P.sbuf("g0_t", [128, 128], BF16)
        g1_t = P.sbuf("g1_t", [32, 128], BF16)
        def m16(n, cnt_):
            return [P.sbuf(n, [128, 128], BF16) for _ in range(cnt_)]
        A1T, B1T, B2T, TIV = m16("A1T", 8), m16("B1T", 8), m16("B2T", 8), m16("TIV", 8)
        Mp = [m16("Mp", 2) for _ in range(8)]
        Np = [m16("Np", 2) for _ in range(8)]
        R16 = [m16("R16", 2) for _ in range(8)]
        R32 = [P.sbuf("R32", [128, 128], F32) for _ in range(8)]
        S32 = P.sbuf("S32", [128, 128], F32)
        S16 = P.sbuf("S16", [128, 128], BF16)
        Gsb = P.sbuf("Gsb", [128, 128], BF16)
        nP = P.sbuf("nP", [128, 128], BF16)
        ysb4 = [P.sbuf("ysb", [128, 128], F32) for _ in range(4)]
        ysq = P.sbuf("ysq", [128, 128], F32)
        yn = P.sbuf("yn", [128, 128], F32)
        st1 = P.sbuf("st1", [128, 8], F32)
        o1 = P.sbuf("o1", [128, 128], F32)
        yst = [P.sbuf("yst3", [128, 512], BF16) for _ in range(2)]
        pbig = [P.psum("pbig", [128, 512]) for _ in range(2)]
        ptr_t = [P.psum("ptr", [128, 512]) for _ in range(2)]
        ptr = [Buf(ptr_t[i][:, 0:128], f"ptr{i}") for i in range(2)]
        pm_t = [P.psum("pm", [128, 512]) for _ in range(2)]
        pm = [Buf(pm_t[i][:, 0:128], f"pm{i}") for i in range(2)]
        p3_t = [P.psum("p3", [128, 512]) for _ in range(2)]
        p3 = [Buf(p3_t[i][:, 0:128], f"p3{i}") for i in range(2)]
        chl = [P.chan("rl") for _ in range(2)]
        chw = P.chan("rw")
        chy = [P.chan("ry") for _ in range(2)]
        cnt = {"big": 0, "tr": 0, "pm": 0, "p3": 0, "y": 0}

        def nxt(kind, lst):
            i = cnt[kind]
            cnt[kind] += 1
            return lst[i % len(lst)]
        idf = k.cf[:, C_ID:C_ID + 128]
        idb = k.cb[:, C_ID:C_ID + 128]
        blk = k.cf[:, C_BLK:C_BLK + 128]
        mup = k.cf[:, C_MUP:C_MUP + 128]
        mlo = k.cf[:, C_MLO:C_MLO + 128]
        mui = k.cf[:, C_UTRI:C_UTRI + 128]

        def load(idx):
            hp, tb = idx // 8, idx % 8
            i = idx % 2
            cs = slice(tb * 512, (tb + 1) * 512)
            for n, nci, M in (("r", hp, 128), ("k", 8 + hp, 128), ("v", 16 + hp, 128), ("wa", 24, 128),
                              ("g0", 25, 128), ("g1", 26, 32)):
                P.dma("sp", chl[i], ld[n][i][0:M, :], k.zsT.ap()[nci * 128:nci * 128 + M, cs],
                      reads=[k.zsT], writes=[ld[n][i]])
        load(0)
        import os
        NHP = int(os.environ.get("B3_HP", "8"))
        NTB = int(os.environ.get("B3_TB", "8"))
        for hp in range(NHP):
            hc = slice(hp * 128, (hp + 1) * 128)
            P.dma("pool", chw, wa_t[0:64, :], k.wup.ap()[l, :, hc], reads=[k.wup], writes=[wa_t])
            P.dma("pool", chw, wa_t[64:128, :], k.aup.ap()[l, :, hc], reads=[k.aup], writes=[wa_t])
            P.dma("pool", chw, g0_t[:], k.gup.ap()[l, 0:128, hc], reads=[k.gup], writes=[g0_t])
            P.dma("pool", chw, g1_t[:], k.gup.ap()[l, 128:160, hc], reads=[k.gup], writes=[g1_t])
            P.op("dve", lambda e: e.memset(S32[:], 0.0), writes=[S32])
            P.op("dve", lambda e: e.memset(S16[:], 0.0), writes=[S16])

            def vc(col):
                return k.vec[:, vo + col + hp:vo + col + hp + 1]
            for tb in range(NTB):
                idx = hp * 8 + tb
                i = idx % 2
                if tb + 1 < NTB or hp + 1 < NHP:
                    load(idx + 1 if tb + 1 < NTB else (hp + 1) * 8)
                r_, k_, v_, wa_, g0_, g1_ = (ld[n][i] for n in ("r", "k", "v", "wa", "g0", "g1"))
                act(P, b["thad"], b["thad"][0:64, :], wa_, wa_[0:64, :], AF.Tanh)
                cp(P, "act", b["thad"], b["thad"][64:128, :], wa_, wa_[64:128, :])
                act(P, b["sg"], b["sg"][:], g0_, g0_[:], AF.Sigmoid)
                act(P, b["sg1"], b["sg1"][0:32, :], g1_, g1_[0:32, :], AF.Sigmoid)
                pw = nxt("big", pbig)
                mm(P, pw, pw[:], wa_t, wa_t[0:64, :], b["thad"], b["thad"][0:64, :], True, True)
                act(P, f["lw"], f["lw"][:], pw, pw[:], AF.Sigmoid, extra=[k.vec], bias=vc(V_W0))
                pa = nxt("big", pbig)
                mm(P, pa, pa[:], wa_t, wa_t[64:128, :], b["thad"], b["thad"][64:128, :], True, True)
                act(P, f["a"], f["a"][:], pa, pa[:], AF.Sigmoid, extra=[k.vec], bias=vc(V_A0))
                pg = nxt("big", pbig)
                mm(P, pg, pg[:], g0_t, g0_t[:], b["sg"], b["sg"][:], True, False)
                mm(P, pg, pg[:], g1_t, g1_t[0:32, :], b["sg1"], b["sg1"][0:32, :], False, True)
                cp(P, "act", f["gT"], f["gT"][:], pg, pg[:])
                stop(1)
                P.op("dve", lambda e: e.tensor_scalar_mul(out=f["lw"][:], in0=f["lw"][:], scalar1=-EM05),
                     reads=[f["lw"]], writes=[f["lw"]])
                P.op("dve", lambda e: e.tensor_scalar_mul(out=f["kk"][:], in0=k_[:], scalar1=vc(V_KK)),
                     reads=[k_, k.vec], writes=[f["kk"]])
                ts(P, "dve", f["tmp"], f["tmp"][:], f["a"], f["a"][:], vc(V_KA), k.omka[:, l * 8 + hp:l * 8 + hp + 1],
                   ALU.mult, ALU.add, extra=[k.vec, k.omka])
                tt(P, "dve", f["kp"], f["kp"][:], k_, k_[:], f["tmp"], f["tmp"][:], ALU.mult)
                tt(P, "dve", f["sq"], f["sq"][:], f["kk"], f["kk"][:], f["kk"], f["kk"][:], ALU.mult)
                pq = nxt("big", pbig)
                mm(P, pq, pq[:], k.cf, blk, f["sq"], f["sq"][:], True, True)
                act(P, f["ex"], f["ex"][:], pq, pq[:], AF.Ln, extra=[k.epsb], bias=k.epsb[:, 3:4])
                act(P, f["ex"], f["ex"][:], f["ex"], f["ex"][:], AF.Exp, scale=-0.5)
                tt(P, "dve", f["kk"], f["kk"][:], f["kk"], f["kk"][:], f["ex"], f["ex"][:], ALU.mult)
                tt(P, "dve", f["kka"], f["kka"][:], f["kk"], f["kk"][:], f["a"], f["a"][:], ALU.mult)
                tt(P, "dve", f["tmp"], f["tmp"][:], r_, r_[:], f["kp"], f["kp"][:], ALU.mult)
                P.op("dve", lambda e: e.tensor_scalar_mul(out=f["sq"][:], in0=f["tmp"][:], scalar1=vc(V_RK)),
                     reads=[f["tmp"], k.vec], writes=[f["sq"]])
                pb_ = nxt("big", pbig)
                mm(P, pb_, pb_[:], k.cf, blk, f["sq"], f["sq"][:], True, True)
                tt(P, "dve", f["bv"], f["bv"][:], pb_, pb_[:], v_, v_[:], ALU.mult)
                cp(P, "act", b["vb"], b["vb"][:], v_, v_[:])
                stop(2)
                P.op("dve", lambda e: e.tensor_tensor_scan(out=f["cl"][:], data0=k.cf[:, C_RST:C_RST + 512],
                                                           data1=f["lw"][:], initial=0.0, op0=ALU.mult, op1=ALU.add),
                     reads=[k.cf, f["lw"]], writes=[f["cl"]])
                act(P, f["ex"], f["ex"][:], f["cl"], f["cl"][:], AF.Exp)
                tt(P, "dve", b["rt"], b["rt"][:], r_, r_[:], f["ex"], f["ex"][:], ALU.mult)
                act(P, f["ex"], f["ex"][:], f["cl"], f["cl"][:], AF.Exp, scale=-1.0)
                tt(P, "dve", b["kt"], b["kt"][:], f["kp"], f["kp"][:], f["ex"], f["ex"][:], ALU.mult)
                tt(P, "dve", b["bt"], b["bt"][:], f["kka"], f["kka"][:], f["ex"], f["ex"][:], ALU.mult)
                tt(P, "dve", f["tmp"], f["tmp"][:], f["cl"], f["cl"][:], f["lw"], f["lw"][:], ALU.subtract)
                act(P, f["ex"], f["ex"][:], f["tmp"], f["tmp"][:], AF.Exp)
                tt(P, "dve", b["kap"], b["kap"][:], f["kk"], f["kk"][:], f["ex"], f["ex"][:], ALU.mult)
                for c in range(4):
                    act(P, f["e4"], f["e4"][:, c * 128:(c + 1) * 128], f["cl"], f["cl"][:, c * 128:(c + 1) * 128],
                        AF.Exp, scale=-1.0, bias=f["cl"][:, c * 128 + 127:c * 128 + 128])
                    act(P, WL, WL[:, c:c + 1], f["cl"], f["cl"][:, c * 128 + 127:c * 128 + 128], AF.Exp)
                tt(P, "dve", b["Kp"], b["Kp"][:], f["kp"], f["kp"][:], f["e4"], f["e4"][:], ALU.mult)
                tt(P, "dve", b["Bp"], b["Bp"][:], f["kka"], f["kka"][:], f["e4"], f["e4"][:], ALU.mult)
                stop(3)
                TRN = os.environ.get("B3_TRN", "VKBe")
                for c in range(4):
                    for n, src_ in (("V", b["vb"]), ("K", b["Kp"]), ("B", b["Bp"])):
                        if n not in TRN:
                            continue
                        pt = nxt("tr", ptr)
                        mm(P, pt, pt[:], src_, src_[:, c * 128:(c + 1) * 128], k.cb, idb, True, True)
                        if "e" in TRN:
                            cp(P, "act" if n == "V" else "dve", tm[n], tm[n][:, c, :], pt, pt[:])
                stop(4)
                slots4 = pm + ptr

                def chain(ci, c, e):
                    cc = slice(c * 128, (c + 1) * 128)
                    er = slice(e * 64, (e + 1) * 64)
                    M_, N_, R16_, R32_ = Mp[ci], Np[ci], R16[ci], R32[ci]
                    p1 = nxt("pm", slots4)
                    mm(P, p1, p1[:], b["kt"], b["kt"][er, cc], b["kap"], b["kap"][er, cc], True, True)
                    tt(P, "dve", A1T[ci], A1T[ci][:], p1, p1[:], k.cf, mup, ALU.mult)
                    yield
                    p2 = nxt("pm", slots4)
                    mm(P, p2, p2[:], b["bt"], b["bt"][er, cc], b["kap"], b["kap"][er, cc], True, True)
                    stt(P, "dve", M_[0], M_[0][:], p2, p2[:], -1.0, k.cf, mup, ALU.mult, ALU.mult)
                    yield
                    p3_ = nxt("pm", slots4)
                    mm(P, p3_, p3_[:], b["kap"], b["kap"][er, cc], b["bt"], b["bt"][er, cc], True, True)
                    stt(P, "dve", N_[0], N_[0][:], p3_, p3_[:], -1.0, k.cf, mlo, ALU.mult, ALU.mult)
                    yield
                    p4 = nxt("pm", slots4)
                    mm(P, p4, p4[:], b["kt"], b["kt"][er, cc], b["rt"], b["rt"][er, cc], True, True)
                    tt(P, "dve", B1T[ci], B1T[ci][:], p4, p4[:], k.cf, mui, ALU.mult)
                    yield
                    p5 = nxt("pm", slots4)
                    mm(P, p5, p5[:], b["bt"], b["bt"][er, cc], b["rt"], b["rt"][er, cc], True, True)
                    tt(P, "dve", B2T[ci], B2T[ci][:], p5, p5[:], k.cf, mui, ALU.mult)
                    tt(P, "dve", R32_, R32_[:], M_[0], M_[0][:], k.cf, idf, ALU.add)
                    tt(P, "dve", R16_[0], R16_[0][:], M_[0], M_[0][:], k.cf, idf, ALU.add)
                    yield
                    cur = 0
                    for it_ in range(NL):
                        nx_ = 1 - cur
                        pn = nxt("pm", slots4)
                        mm(P, pn, pn[:], M_[cur], M_[cur][:], N_[cur], N_[cur][:], True, True)
                        cp(P, "act", N_[nx_], N_[nx_][:], pn, pn[:])
                        yield
                        if it_ < NL - 1:
                            pm_ = nxt("pm", slots4)
                            mm(P, pm_, pm_[:], N_[cur], N_[cur][:], M_[cur], M_[cur][:], True, True)
                            cp(P, "act", M_[nx_], M_[nx_][:], pm_, pm_[:])
                            yield
                        pr_ = nxt("pm", slots4)
                        mm(P, pr_, pr_[:], N_[nx_], N_[nx_][:], R16_[cur], R16_[cur][:], True, True)
                        tt(P, "dve", R32_, R32_[:], R32_, R32_[:], pr_, pr_[:], ALU.add)
                        last = it_ == NL - 1
                        dst = TIV[ci] if last else R16_[nx_]
                        cp(P, "act", dst, dst[:], R32_, R32_[:])
                        yield
                        cur = nx_
                gens = [chain(c * 2 + e, c, e) for c in range(4) for e in range(2)]
                while gens:
                    alive = []
                    for g_ in gens:
                        try:
                            next(g_)
                            alive.append(g_)
                        except StopIteration:
                            pass
                    gens = alive
                def chain_part(c):
                    cc = slice(c * 128, (c + 1) * 128)
                    stop(5)
                    G = nxt("p3", p3)
                    mm(P, G, G[:], b["kap"], b["kap"][:, cc], S16, S16[:], True, False)
                    for e in range(2):
                        ec = slice(e * 64, (e + 1) * 64)
                        mm(P, G, G[:, ec], A1T[c * 2 + e], A1T[c * 2 + e][:], tm["V"], tm["V"][:, c, ec], False, e == 1)
                    cp(P, "act", Gsb, Gsb[:], G, G[:])
                    Pp = nxt("p3", p3)
                    for e in range(2):
                        ec = slice(e * 64, (e + 1) * 64)
                        mm(P, Pp, Pp[:, ec], TIV[c * 2 + e], TIV[c * 2 + e][:], Gsb, Gsb[:, ec], True, True)
                    act(P, nP, nP[:], Pp, Pp[:], AF.Copy, scale=-1.0)
                    Y = nxt("p3", p3)
                    mm(P, Y, Y[:], b["rt"], b["rt"][:, cc], S16, S16[:], True, False)
                    for e in range(2):
                        ec = slice(e * 64, (e + 1) * 64)
                        mm(P, Y, Y[:, ec], B1T[c * 2 + e], B1T[c * 2 + e][:], tm["V"], tm["V"][:, c, ec], False, False)
                        mm(P, Y, Y[:, ec], B2T[c * 2 + e], B2T[c * 2 + e][:], nP, nP[:, ec], False, e == 1)
                    cp(P, "act", ysb4[c], ysb4[c][:], Y, Y[:])
                    U = nxt("p3", p3)
                    mm(P, U, U[:], tm["K"], tm["K"][:, c, :], tm["V"], tm["V"][:, c, :], True, False)
                    mm(P, U, U[:], tm["B"], tm["B"][:, c, :], nP, nP[:], False, True)
                    for e in range(2):
                        er = slice(e * 64, (e + 1) * 64)
                        stt(P, "dve", S32, S32[er, er], S32, S32[er, er], WL[er, c:c + 1], U, U[er, er],
                            ALU.mult, ALU.add, extra=[WL])
                        cp(P, "act", S16, S16[er, er], S32, S32[er, er])

                def out_part(c):
                    cc = slice(c * 128, (c + 1) * 128)
                    ys = yst[cnt["y"] % 2]
                    stop(6)
                    ysb = ysb4[c]
                    y3 = ysb[:].rearrange("p (e v) -> p e v", e=2)
                    P.op("dve", lambda e_: e_.reduce_sum(out=st1[:, 0:2], in_=y3, axis=AX.X), reads=[ysb], writes=[st1])
                    tt(P, "dve", ysq, ysq[:], ysb, ysb[:], ysb, ysb[:], ALU.mult)
                    P.op("dve", lambda e_: e_.reduce_sum(out=st1[:, 2:4], in_=ysq[:].rearrange("p (e v) -> p e v", e=2),
                                                         axis=AX.X), reads=[ysq], writes=[st1])
                    P.op("dve", lambda e_: e_.tensor_scalar_mul(out=st1[:, 0:2], in0=st1[:, 0:2], scalar1=1.0 / 64),
                         reads=[st1], writes=[st1])
                    tt(P, "dve", st1, st1[:, 4:6], st1, st1[:, 0:2], st1, st1[:, 0:2], ALU.mult)
                    stt(P, "dve", st1, st1[:, 6:8], st1, st1[:, 2:4], 1.0 / 64, st1, st1[:, 4:6], ALU.mult, ALU.subtract)
                    act(P, st1, st1[:, 6:8], st1, st1[:, 6:8], AF.Ln, extra=[k.epsb], bias=k.epsb[:, 2:3])
                    act(P, st1, st1[:, 6:8], st1, st1[:, 6:8], AF.Exp, scale=-0.5)
                    for e in range(2):
                        ec = slice(e * 64, (e + 1) * 64)
                        ts(P, "dve", yn, yn[:, ec], ysb, ysb[:, ec], st1[:, e:e + 1], st1[:, 6 + e:7 + e],
                           ALU.subtract, ALU.mult, extra=[st1])
                    YT = nxt("tr", ptr)
                    mm(P, YT, YT[:], yn, yn[:], k.cf, idf, True, True)
                    stt(P, "dve", o1, o1[:], YT, YT[:], vc(V_LNW), f["bv"], f["bv"][:, cc], ALU.mult, ALU.add,
                        extra=[k.vec])
                    ys = yst[cnt["y"] % 2]
                    stt(P, "dve", ys, ys[:, cc], o1, o1[:], vc(V_LNB), f["gT"], f["gT"][:, cc], ALU.add, ALU.mult,
                        extra=[k.vec])

                chain_part(0)
                for c in range(1, 4):
                    chain_part(c)
                    out_part(c - 1)
                out_part(3)
                ys = yst[cnt["y"] % 2]
                P.dma("sp", chy[cnt["y"] % 2], k.yT.ap()[1024 + hp * 128:1024 + (hp + 1) * 128, tb * 512:(tb + 1) * 512],
                      ys[:], reads=[ys], writes=[k.yT])
                cnt["y"] += 1


def phase_c(P, k, l, xin, xout):
    with P.scope():
        vo = l * VEC_L + V_NRM
        xs = P.sbuf("cx", [128, KC, 512], F32)
        acc = P.sbuf("cacc", [128, KC, 512], F32)
        uT = P.sbuf("cu", [128, KC, 512], BF16)
        mT = P.sbuf("cm", [128, KC, 512], BF16)
        hd = P.sbuf("chd", [128, KC, 512], BF16)
        yb = P.sbuf("cy", [128, KC, 512], BF16)
        wt = [P.sbuf("cw", [128, KC, 512], BF16) for _ in range(2)]
        gs = [[P.sbuf("cg", [128, 512], BF16) for _ in range(4)] for _ in range(3)]
        rs = P.sbuf("crs", [128, 512], F32)
        tmp = [P.sbuf("ctmp", [128, 512], F32) for _ in range(2)]
        pg = [P.psum("cpg", [128, 512]) for _ in range(3)]
        pbr = [P.psum("cpb", [128, 512]) for _ in range(3)]
        pss = P.psum("cpss", [128, 512])
        chw = [P.chan("cw") for _ in range(2)]
        chx = P.chan("cx")
        chu = P.chan("cu")
        chyy = P.chan("cy")
        cho = P.chan("co")
        order = []
        for ng in range(4):
            order += [(k.wg, i * 4 + ng) for i in range(3)] + [(k.wbr, ng)]
        order += [(k.wout, i) for i in range(4)]
        for q in range(4):
            order += [(k.wup_mlp, q * 4 + i) for i in range(4)] + [(k.wdn, q * 4 + i) for i in range(4)]
        NT = len(order)
        seq = [0]

        def loadw(gidx):
            src, ti = order[gidx % NT]
            i = gidx % 2
            P.dma("pool", chw[i], wt[i][:].rearrange("p c n -> p (c n)"), src.ap()[l, ti], reads=[src], writes=[wt[i]])

        def nextw():
            g = seq[0]
            seq[0] += 1
            if g + 1 < NT * 8:
                loadw(g + 1)
            return wt[g % 2]
        loadw(0)
        xv = xin.ap().rearrange("(c p) t -> p c t", p=128)
        ov = xout.ap().rearrange("(c p) t -> p c t", p=128)
        uv = k.uT.ap().rearrange("(c p) t -> p c t", p=128)
        yv = k.yT.ap().rearrange("(c p) t -> p c t", p=128)
        gi = [0]

        def gemm(w, j, rhs_b, out_ps):
            for c in range(KC):
                mm(P, out_ps, out_ps[:], w, w[:, c, j * 128:(j + 1) * 128], rhs_b, rhs_b[:, c, :], c == 0, c == KC - 1)

        def post_norm(gcol):
            rms_stats(P, k, acc, hd, pss, rs)
            for c in range(KC):
                stt(P, "dve", acc, acc[:, c, :], acc, acc[:, c, :], k.vec[:, gcol + c:gcol + c + 1], rs, rs[:],
                    ALU.mult, ALU.mult, extra=[k.vec])
                tt(P, "dve", xs, xs[:, c, :], xs, xs[:, c, :], acc, acc[:, c, :], ALU.add)
        for tb in range(8):
            cs = slice(tb * 512, (tb + 1) * 512)
            for g in range(4):
                gsl = slice(4 * g, 4 * g + 4)
                P.dma("sp", chx, xs[:, gsl, :], xv[:, gsl, cs], reads=[xin], writes=[xs])
                P.dma("sp", chu, uT[:, gsl, :], uv[:, gsl, cs], reads=[k.uT], writes=[uT])
                P.dma("sp", chyy, yb[:, gsl, :], yv[:, gsl, cs], reads=[k.yT], writes=[yb])
            for ng in range(4):
                for i in range(3):
                    w = nextw()
                    for j in range(4):
                        p_ = pg[gi[0] % 3]
                        gi[0] += 1
                        gemm(w, j, uT, p_)
                        act(P, gs[i][j], gs[i][j][:], p_, p_[:], AF.Sigmoid)
                w = nextw()
                for j in range(4):
                    n = ng * 4 + j
                    for i, (k0, nk) in enumerate(((0, 4), (4, 4), (8, 8))):
                        for c in range(nk):
                            mm(P, pbr[i], pbr[i][:], w, w[:, k0 + c, j * 128:(j + 1) * 128], yb, yb[:, k0 + c, :],
                               c == 0, c == nk - 1)
                    tt(P, "dve", tmp[0], tmp[0][:], pbr[0], pbr[0][:], gs[0][j], gs[0][j][:], ALU.mult)
                    tt(P, "dve", tmp[1], tmp[1][:], pbr[1], pbr[1][:], gs[1][j], gs[1][j][:], ALU.mult)
                    tt(P, "dve", tmp[0], tmp[0][:], tmp[0], tmp[0][:], tmp[1], tmp[1][:], ALU.add)
                    tt(P, "dve", tmp[1], tmp[1][:], pbr[2], pbr[2][:], gs[2][j], gs[2][j][:], ALU.mult)
                    tt(P, "dve", mT, mT[:, n, :], tmp[0], tmp[0][:], tmp[1], tmp[1][:], ALU.add)
            for t_ in range(4):
                w = nextw()
                for j in range(4):
                    p_ = pg[gi[0] % 3]
                    gi[0] += 1
                    gemm(w, j, mT, p_)
                    cp(P, "act", acc, acc[:, t_ * 4 + j, :], p_, p_[:])
            post_norm(vo + 16)
            rms_stats(P, k, xs, hd, pss, rs)
            for c in range(KC):
                stt(P, "dve", mT, mT[:, c, :], xs, xs[:, c, :], k.vec[:, vo + 32 + c:vo + 32 + c + 1], rs, rs[:],
                    ALU.mult, ALU.mult, extra=[k.vec])
            for q in range(4):
                for t_ in range(4):
                    w = nextw()
                    for j in range(4):
                        p_ = pg[gi[0] % 3]
                        gi[0] += 1
                        gemm(w, j, mT, p_)
                        tq = tmp[gi[0] % 2]
                        P.op("dve", lambda e: e.tensor_scalar_max(out=tq[:], in0=p_[:], scalar1=0.0), reads=[p_], writes=[tq])
                        tt(P, "dve", hd, hd[:, t_ * 4 + j, :], tq, tq[:], tq, tq[:], ALU.mult)
                for cg in range(4):
                    w = nextw()
                    for j in range(4):
                        p_ = pg[gi[0] % 3]
                        gi[0] += 1
                        gemm(w, j, hd, p_)
                        n = cg * 4 + j
                        if q == 0:
                            cp(P, "act", acc, acc[:, n, :], p_, p_[:])
                        else:
                            tt(P, "dve", acc, acc[:, n, :], acc, acc[:, n, :], p_, p_[:], ALU.add)
            post_norm(vo + 48)
            for g in range(4):
                gsl = slice(4 * g, 4 * g + 4)
                P.dma("sp", cho, ov[:, gsl, cs], xs[:, gsl, :], reads=[xs], writes=[xout])


def build(dbg=(), stages="abcdC", nl=L):
    nc = bass.Bass("TRN2", target_bir_lowering=False)
    P = Prog(nc)
    k = K()

    def dk(name):
        return "ExternalOutput" if name in dbg else "Internal"

    def inp(name, shape):
        return P.dram(name, shape, F32, kind="ExternalInput")
    k.xT = inp("xT", [D, T])
    k.consts = inp("consts", [128, NCONST])
    k.vecs = inp("vecs", [128, L * VEC_L])
    k.wqk = inp("wqk", [L, 4, 128, KC * 512])
    k.wv = inp("wv", [L, 2, 128, KC * 512])
    k.wf = inp("wf", [L, 128, KC * 4])
    k.wrw = inp("wrw", [L, 7, 128, KC * 512])
    k.wup = inp("wup", [L, 64, 1024])
    k.aup = inp("aup", [L, 64, 1024])
    k.gup = inp("gup", [L, 160, 1024])
    if "C" in stages:
        k.wg = inp("wg", [L, 12, 128, KC * 512])
        k.wbr = inp("wbr", [L, 4, 128, KC * 512])
        k.wout = inp("wout", [L, 4, 128, KC * 512])
        k.wup_mlp = inp("wmup", [L, 16, 128, KC * 512])
        k.wdn = inp("wmdn", [L, 16, 128, KC * 512])
    k.uT = P.dram("uT", [D, T], BF16, kind=dk("uT"))
    k.qkT = P.dram("qkT", [NQK, 128, T], BF16, kind=dk("qkT"))
    k.vs = P.dram("vs", [8, 128, 32, 128], BF16, kind=dk("vs"))
    k.zsT = P.dram("zsT", [NRW * 128, T], F32, kind=dk("zsT"))
    k.cdram = P.dram("cdram", [4, T], F32, kind=dk("cdram"))
    k.yT = P.dram("yT", [D, T], BF16, kind=dk("yT"))
    k.x1T = P.dram("x1T", [D, T], F32, kind=dk("x1T"))
    k.out = P.dram("outT", [D, T], F32, kind="ExternalOutput")
    k.csum = P.sbuf("csum", [128, 4, 32], F32)
    load_consts(P, k)
    for l in range(nl):
        xin = k.xT if l == 0 else k.x1T
        xout = k.x1T if l == 0 and nl == 2 else k.out
        if "a" in stages:
            phase_norm(P, k, xin, l * VEC_L + V_NRM + 0, k.uT)
        if "b" in stages:
            phase_b1(P, k, l)
        if "c" in stages:
            phase_b2(P, k, l)
        if "d" in stages:
            phase_b3(P, k, l)
        if "C" in stages:
            phase_c(P, k, l, xin, xout)
    P.close()
    print("instructions:", P.n_inst)
    return nc


QA0, KA0, VA0, QB0, KB0, VB0, FB0, RW0, GT0 = 0, 512, 1024, 1536, 2048, 2560, 3072, 3076, 6436


def _tile(w):
    kk, n = w.shape
    out = np.zeros((kk // 128, 128, 512), np.float32)
    out[:, :, :n] = w.reshape(kk // 128, 128, n)
    return np.ascontiguousarray(out.transpose(1, 0, 2)).reshape(128, -1)


def make_consts():
    c = np.zeros((128, NCONST), np.float32)
    i = np.arange(128)
    c[:, C_ID:C_ID + 128] = np.eye(128)
    c[:, C_ONES:C_ONES + 128] = 1.0
    c[:, C_UTRI:C_UTRI + 128] = (i[:, None] <= i[None, :])
    c[:, C_NEGT:C_NEGT + 128] = -1.0 * (i[:, None] > i[None, :])
    c[:, C_MUP:C_MUP + 128] = (i[:, None] < i[None, :])
    c[:, C_MLO:C_MLO + 128] = (i[:, None] > i[None, :])
    c[:, C_BLK:C_BLK + 128] = ((i[:, None] // 64) == (i[None, :] // 64))
    q = np.arange(512)
    c[:, C_RST:C_RST + 512] = (q % 128 != 0)[None, :]
    for j in range(4):
        kk = j * 128 + i
        c[:, C_FOXB + j * 512:C_FOXB + (j + 1) * 512] = np.where(kk[:, None] <= q[None, :], 0.0, -30000.0)
        c[:, C_SBM + j * 512:C_SBM + (j + 1) * 512] = (kk[:, None] < q[None, :])
    return c


def prep_shared(inp, with_c=True):
    m = {}
    m["consts"] = make_consts()
    vec = np.zeros((128, L * VEC_L), np.float32)
    wqk = np.zeros((L, 4, 128, KC * 512), np.float32)
    wv = np.zeros((L, 2, 128, KC * 512), np.float32)
    wf = np.zeros((L, 128, KC * 4), np.float32)
    wrw = np.zeros((L, 7, 128, KC * 512), np.float32)
    for l in range(L):
        o = l * VEC_L
        for wi, nm in enumerate(("norm_mix_pre", "norm_mix_post", "norm_mlp_pre", "norm_mlp_post")):
            vec[:, o + V_NRM + wi * 16:o + V_NRM + wi * 16 + 16] = inp[nm][l].reshape(16, 128).T
        mu_p = np.zeros(NRW * 128, np.float32)
        mu_p[:3360] = inp["rwkv_mu"][l]
        vec[:, o + V_MU:o + V_MU + NRW] = mu_p.reshape(NRW, 128).T
        for col, nm in ((V_W0, "rwkv_w0"), (V_A0, "rwkv_a0"), (V_KK, "rwkv_k_k"), (V_KA, "rwkv_k_a"),
                        (V_RK, "rwkv_r_k"), (V_LNW, "rwkv_ln_w"), (V_LNB, "rwkv_ln_b")):
            vec[:, o + col:o + col + 8] = inp[nm][l].reshape(8, 128).T
        vec[:, o + V_BF:o + V_BF + 128] = np.tile(inp["b_forget"][l], 32)[None, :]
        w = inp["w_in"][l]
        for ti, c0 in enumerate((QA0, KA0, QB0, KB0)):
            wqk[l, ti] = _tile(w[:, c0:c0 + 512])
        wv[l, 0] = _tile(w[:, VA0:VA0 + 512])
        wv[l, 1] = _tile(w[:, VB0:VB0 + 512])
        wf[l] = np.ascontiguousarray(w[:, FB0:FB0 + 4].reshape(16, 128, 4).transpose(1, 0, 2)).reshape(128, 64)
        for ti in range(7):
            wrw[l, ti] = _tile(w[:, RW0 + ti * 512:min(RW0 + (ti + 1) * 512, RW0 + 3360)])
    m["vecs"] = vec
    m["wqk"], m["wv"], m["wf"], m["wrw"] = wqk, wv, wf, wrw
    m["wup"] = np.ascontiguousarray(inp["rwkv_w_up"])
    m["aup"] = np.ascontiguousarray(inp["rwkv_a_up"])
    m["gup"] = np.ascontiguousarray(inp["rwkv_g_up"])
    if with_c:
        wg = np.zeros((L, 12, 128, KC * 512), np.float32)
        wbr = np.zeros((L, 4, 128, KC * 512), np.float32)
        wout = np.zeros((L, 4, 128, KC * 512), np.float32)
        wmup = np.zeros((L, 16, 128, KC * 512), np.float32)
        wmdn = np.zeros((L, 16, 128, KC * 512), np.float32)
        for l in range(L):
            w = inp["w_in"][l]
            for t in range(12):
                wg[l, t] = _tile(w[:, GT0 + t * 512:GT0 + (t + 1) * 512])
            br = np.concatenate([inp["w_branch_a"][l], inp["w_branch_b"][l], inp["w_branch_c"][l]], 0)
            for t in range(4):
                wbr[l, t] = _tile(br[:, t * 512:(t + 1) * 512])
                wout[l, t] = _tile(inp["w_out"][l][:, t * 512:(t + 1) * 512])
            for t in range(16):
                wmup[l, t] = _tile(inp["w_mlp_up"][l][:, t * 512:(t + 1) * 512])
                q, cg = t // 4, t % 4
                wmdn[l, t] = _tile(inp["w_mlp_down"][l][q * 2048:(q + 1) * 2048, cg * 512:(cg + 1) * 512])
        m["wg"], m["wbr"], m["wout"], m["wmup"], m["wmdn"] = wg, wbr, wout, wmup, wmdn
    return m


_CACHE = {}


def kernel(**inputs):
    inp = {k_: np.asarray(v, dtype=np.float32) for k_, v in inputs.items()}
    if "nc" not in _CACHE:
        _CACHE["nc"] = build()
    nc = _CACHE["nc"]
    shared = prep_shared(inp)
    maps = []
    for c in range(NCORES):
        m = dict(shared)
        m["xT"] = np.ascontiguousarray(inp["x"][c].T)
        maps.append(m)
    res = run_bass_kernel_spmd(nc, maps, core_ids=list(range(NCORES)))
    out = np.stack([np.ascontiguousarray(np.asarray(res.results[c]["outT"]).T) for c in range(NCORES)], 0)
    return out.astype(np.float32)
```

```python
import contextlib
import numpy as np
import concourse.bass as bass
import concourse.mybir as mybir
from concourse.bass_utils import run_bass_kernel_spmd

F32 = mybir.dt.float32
BF16 = mybir.dt.bfloat16
AF = mybir.ActivationFunctionType
ALU = mybir.AluOpType
AX = mybir.AxisListType


class Counter:
    def __init__(self, prog, name, step, epoch):
        self.prog, self.name, self.step, self.epoch = prog, name, step, epoch
        self.n = 0
        self.sems = []

    def next(self):
        self.n += 1
        ep = (self.n - 1) // self.epoch
        while len(self.sems) <= ep:
            self.sems.append(self.prog.new_sem(f"{self.name}_{len(self.sems)}"))
        return self.n

    def sem_val(self, n):
        ep = (n - 1) // self.epoch
        return self.sems[ep], ((n - 1) % self.epoch + 1) * self.step


class Buf:
    def __init__(self, t, name=""):
        self.t = t
        self.name = name
        self.last_write = None
        self.reads = {}

    def __getitem__(self, idx):
        return self.t[idx]

    def ap(self):
        return self.t.ap()


class Prog:
    ENGS = ("pe", "act", "dve", "pool", "sp")

    def __init__(self, nc):
        self.nc = nc
        self.root = contextlib.ExitStack()
        self.stacks = [self.root]
        self.eng = {"pe": nc.tensor, "act": nc.scalar, "dve": nc.vector,
                    "pool": nc.gpsimd, "sp": nc.sync}
        self.cnt = {e: Counter(self, "c" + e, 1, 30000) for e in self.ENGS}
        self.observed = {e: {} for e in self.ENGS}
        self.all_counters = list(self.cnt.values())
        self.n_inst = 0
        self.uid = 0

    def new_sem(self, name):
        return self.root.enter_context(self.nc.semaphore(name))

    def chan(self, name, step=16):
        self.uid += 1
        c = Counter(self, f"d{name}{self.uid}", step, 1800 if step == 16 else 30000)
        self.all_counters.append(c)
        return c

    @contextlib.contextmanager
    def scope(self):
        st = contextlib.ExitStack()
        self.stacks.append(st)
        try:
            yield
        finally:
            self.barrier()
            self.stacks.pop()
            st.close()

    def sbuf(self, name, shape, dtype):
        self.uid += 1
        t = self.stacks[-1].enter_context(
            self.nc.sbuf_tensor(f"{name}_{self.uid}", list(shape), dtype))
        return Buf(t, name)

    def psum(self, name, shape, dtype=F32):
        self.uid += 1
        t = self.stacks[-1].enter_context(
            self.nc.psum_tensor(f"{name}_{self.uid}", list(shape), dtype))
        return Buf(t, name)

    def dram(self, name, shape, dtype, kind="Internal"):
        return Buf(self.nc.dram_tensor(name, list(shape), dtype, kind=kind), name)

    def _need(self, engine, tok, waits):
        if tok is None:
            return
        c, n, teng = tok
        if teng == "pe" and engine == "pe":
            return
        if teng == "dma":
            n = c.n
        ob = self.observed[engine]
        if ob.get(id(c), 0) >= n:
            return
        ob[id(c)] = n
        waits.append((c, n))

    def op(self, engine, fn, reads=(), writes=(), chan=None):
        waits = []
        for b in reads:
            self._need(engine, b.last_write, waits)
        for b in writes:
            self._need(engine, b.last_write, waits)
            for t in b.reads.values():
                self._need(engine, t, waits)
        e = self.eng[engine]
        for c, n in waits:
            s, v = c.sem_val(n)
            e.wait_ge(s, v)
        ins = fn(e)
        c = chan if chan is not None else self.cnt[engine]
        n = c.next()
        s, _ = c.sem_val(n)
        ins.then_inc(s, c.step)
        tok = (c, n, engine if chan is None else "dma")
        for b in reads:
            b.reads[id(c)] = tok
        for b in writes:
            b.last_write = tok
            b.reads = {}
        self.n_inst += 1
        return tok

    def dma(self, queue, chan, out, in_, reads=(), writes=(), **kw):
        return self.op(queue, lambda e: e.dma_start(out=out, in_=in_, **kw),
                       reads=reads, writes=writes, chan=chan)

    def barrier(self):
        toks = [(c, c.n, "x") for c in self.all_counters if c.n > 0]
        for engine in self.ENGS:
            waits = []
            for t in toks:
                self._need(engine, t, waits)
            e = self.eng[engine]
            for c, n in waits:
                s, v = c.sem_val(n)
                e.wait_ge(s, v)

    def close(self):
        self.barrier()
        self.root.close()


D = 2048
T = 4096
KC = 16
L = 2
NCORES = 4
SCALE = 128.0 ** -0.5
EPS = 1e-6
NQK = 16
NRW = 27
V_NRM = 0
V_MU = 64
V_W0, V_A0, V_KK, V_KA, V_RK, V_LNW, V_LNB = 96, 104, 112, 120, 128, 136, 144
V_BF = 152
VEC_L = 288
C_ID = 0
C_ONES = 128
C_UTRI = 256
C_NEGT = 384
C_MUP = 512
C_MLO = 640
C_BLK = 768
C_RST = 896
C_FOXB = 1408
NCF = 1408 + 2048
C_SBM = NCF
NCONST = NCF + 2048
NCB = 896
EM05 = float(np.exp(-0.5))


class K:
    pass


def mm(P, ps, out_ap, lb, lhsT, rb, rhs, start, stop):
    P.op("pe", lambda e: e.matmul(out_ap, lhsT, rhs, start=start, stop=stop),
         reads=[lb, rb], writes=[ps])


def act(P, out_b, out_ap, in_b, in_ap, func, extra=(), **kw):
    P.op("act", lambda e: e.activation(out=out_ap, in_=in_ap, func=func, **kw),
         reads=[in_b] + list(extra), writes=[out_b])


def tt(P, eng, out_b, out_ap, a_b, a_ap, b_b, b_ap, op):
    P.op(eng, lambda e: e.tensor_tensor(out=out_ap, in0=a_ap, in1=b_ap, op=op),
         reads=[a_b, b_b], writes=[out_b])


def stt(P, eng, out_b, out_ap, a_b, a_ap, scalar, b_b, b_ap, op0, op1, extra=()):
    P.op(eng, lambda e: e.scalar_tensor_tensor(out=out_ap, in0=a_ap, scalar=scalar, in1=b_ap, op0=op0, op1=op1),
         reads=[a_b, b_b] + list(extra), writes=[out_b])


def ts(P, eng, out_b, out_ap, a_b, a_ap, s1, s2, op0, op1, extra=()):
    P.op(eng, lambda e: e.tensor_scalar(out=out_ap, in0=a_ap, scalar1=s1, scalar2=s2, op0=op0, op1=op1),
         reads=[a_b] + list(extra), writes=[out_b])


def cp(P, eng, out_b, out_ap, in_b, in_ap):
    if eng == "act":
        P.op("act", lambda e: e.copy(out=out_ap, in_=in_ap), reads=[in_b], writes=[out_b])
    else:
        P.op(eng, lambda e: e.tensor_copy(out=out_ap, in_=in_ap), reads=[in_b], writes=[out_b])


def load_consts(P, k):
    k.cf = P.sbuf("cf", [128, NCF], F32)
    k.cb = P.sbuf("cb", [128, NCB], BF16)
    k.sbm = P.sbuf("sbm", [128, 2048], BF16)
    k.vec = P.sbuf("vec", [128, L * VEC_L], F32)
    k.epsb = P.sbuf("epsb", [128, 4], F32)
    k.omka = P.sbuf("omka", [128, L * 8], F32)
    for i, v in enumerate((EPS, 1.0, 64e-5, 1e-24)):
        P.op("dve", lambda e: e.memset(k.epsb[:, i:i + 1], v), writes=[k.epsb])
    ch = P.chan("const")
    P.dma("sp", ch, k.cf[:], k.consts.ap()[:, 0:NCF], reads=[k.consts], writes=[k.cf])
    P.dma("pool", ch, k.cb[:], k.consts.ap()[:, 0:NCB], reads=[k.consts], writes=[k.cb])
    P.dma("pool", ch, k.sbm[:], k.consts.ap()[:, C_SBM:C_SBM + 2048], reads=[k.consts], writes=[k.sbm])
    P.dma("sp", ch, k.vec[:], k.vecs.ap(), reads=[k.vecs], writes=[k.vec])
    for l in range(L):
        o = l * VEC_L + V_KA
        ts(P, "dve", k.omka, k.omka[:, l * 8:l * 8 + 8], k.vec, k.vec[:, o:o + 8], -1.0, 1.0, ALU.mult, ALU.add)


def rms_stats(P, k, src, sq, ps, rs):
    for g in range(4):
        act(P, sq, sq[:, 4 * g:4 * g + 4, :], src, src[:, 4 * g:4 * g + 4, :], AF.Square)
    for c in range(KC):
        mm(P, ps, ps[:], k.cb, k.cb[:, C_ONES:C_ONES + 128], sq, sq[:, c, :], c == 0, c == KC - 1)
    act(P, rs, rs[:], ps, ps[:], AF.Ln, extra=[k.epsb], scale=1.0 / D, bias=k.epsb[:, 0:1])
    act(P, rs, rs[:], rs, rs[:], AF.Exp, scale=-0.5)


def phase_norm(P, k, src, gain_col, dst):
    with P.scope():
        xs = [P.sbuf("nx", [128, KC, 512], F32) for _ in range(2)]
        sq = [P.sbuf("nsq", [128, KC, 512], BF16) for _ in range(2)]
        us = [P.sbuf("nu", [128, KC, 512], BF16) for _ in range(2)]
        rs = [P.sbuf("nr", [128, 512], F32) for _ in range(2)]
        ps = [P.psum("nps", [128, 512]) for _ in range(2)]
        chl = [P.chan("nl") for _ in range(2)]
        chs = [P.chan("ns") for _ in range(2)]
        sv = src.ap().rearrange("(c p) t -> p c t", p=128)
        dv = dst.ap().rearrange("(c p) t -> p c t", p=128)
        nb = T // 512

        def load(tb):
            i = tb % 2
            for g in range(4):
                P.dma("sp", chl[i], xs[i][:, 4 * g:4 * g + 4, :],
                      sv[:, 4 * g:4 * g + 4, tb * 512:(tb + 1) * 512], reads=[src], writes=[xs[i]])
        load(0)
        for tb in range(nb):
            i = tb % 2
            if tb + 1 < nb:
                load(tb + 1)
            rms_stats(P, k, xs[i], sq[i], ps[i], rs[i])
            for c in range(KC):
                stt(P, "dve", us[i], us[i][:, c, :], xs[i], xs[i][:, c, :], k.vec[:, gain_col + c:gain_col + c + 1],
                    rs[i], rs[i][:], ALU.mult, ALU.mult, extra=[k.vec])
            for g in range(4):
                P.dma("sp", chs[i], dv[:, 4 * g:4 * g + 4, tb * 512:(tb + 1) * 512],
                      us[i][:, 4 * g:4 * g + 4, :], reads=[us[i]], writes=[dst])


def phase_b1(P, k, l):
    with P.scope():
        uT = P.sbuf("uT", [128, KC, T], BF16)
        wt = [P.sbuf("wt", [128, KC, 512], BF16) for _ in range(2)]
        wft = P.sbuf("wft", [128, KC, 4], BF16)
        qst = [P.sbuf("qst", [128, 512], BF16) for _ in range(2)]
        zst = [P.sbuf("zst", [128, 512], F32) for _ in range(2)]
        vst = [P.sbuf("vst", [128, 512], BF16) for _ in range(2)]
        zraw = [P.sbuf("zraw", [128, 513], F32) for _ in range(2)]
        dt = P.sbuf("dt", [128, 512], F32)
        fsb = P.sbuf("fsb", [128, 128], F32)
        fsp = P.sbuf("fsp", [128, 128], F32)
        tot = P.sbuf("tot", [128, 128], F32)
        pre = P.sbuf("pre", [128, 128], F32)
        cT = P.sbuf("cT", [32, 4, 128], F32)
        ps = [P.psum("b1ps", [128, 512]) for _ in range(4)]
        psf = P.psum("psf", [128, 128])
        psc = P.psum("psc", [128, 128])
        pst = P.psum("pst", [128, 128])
        chu = P.chan("u")
        chw = [P.chan("w") for _ in range(2)]
        chq = [P.chan("q") for _ in range(2)]
        chz = [P.chan("z") for _ in range(2)]
        chv = [P.chan("v") for _ in range(2)]
        chm = P.chan("m")

        uv = k.uT.ap().rearrange("(c p) t -> p c t", p=128)
        for hh in range(2):
            for g in range(4):
                P.dma("sp", chu, uT[:, 4 * g:4 * g + 4, hh * 2048:(hh + 1) * 2048],
                      uv[:, 4 * g:4 * g + 4, hh * 2048:(hh + 1) * 2048], reads=[k.uT], writes=[uT])
        P.dma("pool", chm, wft[:].rearrange("p c n -> p (c n)"), k.wf.ap()[l], reads=[k.wf], writes=[wft])

        tiles = [("qk", k.wqk, i) for i in range(4)] + [("v", k.wv, i) for i in range(2)] + \
                [("rw", k.wrw, i) for i in range(7)]

        def loadw(idx):
            kind, src, ti = tiles[idx]
            i = idx % 2
            P.dma("pool", chw[i], wt[i][:].rearrange("p c n -> p (c n)"), src.ap()[l, ti], reads=[src], writes=[wt[i]])
        loadw(0)
        pr = [0]
        zi = [0]
        for idx, (kind, src, ti) in enumerate(tiles):
            w = wt[idx % 2]
            if idx + 1 < len(tiles):
                loadw(idx + 1)
            if kind == "qk":
                for j in range(4):
                    nci = ti * 4 + j
                    isq = ti in (0, 2)
                    for tb in range(8):
                        p_ = ps[pr[0] % 4]
                        pr[0] += 1
                        for c in range(KC):
                            mm(P, p_, p_[:], w, w[:, c, j * 128:(j + 1) * 128], uT, uT[:, c, tb * 512:(tb + 1) * 512],
                               c == 0, c == KC - 1)
                        st = qst[tb % 2]
                        act(P, st, st[:], p_, p_[:], AF.Copy, scale=SCALE if isq else 1.0)
                        P.dma("sp", chq[tb % 2], k.qkT.ap()[nci, :, tb * 512:(tb + 1) * 512], st[:],
                              reads=[st], writes=[k.qkT])
            elif kind == "v":
                vview = k.vs.ap()[ti * 4:ti * 4 + 4].rearrange("h p s d -> p s h d")
                for s in range(32):
                    p_ = ps[pr[0] % 4]
                    pr[0] += 1
                    for c in range(KC):
                        mm(P, p_, p_[:], uT, uT[:, c, s * 128:(s + 1) * 128], w, w[:, c, :], c == 0, c == KC - 1)
                    st = vst[s % 2]
                    cp(P, "dve", st, st[:], p_, p_[:])
                    P.dma("sp", chv[s % 2], vview[:, s, :, :], st[:].rearrange("p (h d) -> p h d", h=4),
                          reads=[st], writes=[k.vs])
                if ti == 1:
                    for s in range(32):
                        for c in range(KC):
                            mm(P, psf, psf[:, 4 * s:4 * s + 4], uT, uT[:, c, s * 128:(s + 1) * 128], wft, wft[:, c, :],
                               c == 0, c == KC - 1)
                    vb = l * VEC_L + V_BF
                    tt(P, "dve", fsb, fsb[:], psf, psf[:], k.vec, k.vec[:, vb:vb + 128], ALU.add)
                    act(P, fsp, fsp[:], fsb, fsb[:], AF.Exp, scale=-1.0)
                    act(P, fsp, fsp[:], fsp, fsp[:], AF.Ln, extra=[k.epsb], bias=k.epsb[:, 1:2])
                    mm(P, psc, psc[:], k.cf, k.cf[:, C_UTRI:C_UTRI + 128], fsp, fsp[:], True, True)
                    mm(P, pst, pst[:], k.cf, k.cf[:, C_ONES:C_ONES + 128], fsp, fsp[:], True, True)
                    cp(P, "dve", tot, tot[:], pst, pst[:])
                    P.op("dve", lambda e: e.memset(pre[:, 0:4], 0.0), writes=[pre])
                    for s in range(1, 32):
                        tt(P, "dve", pre, pre[:, 4 * s:4 * s + 4], pre, pre[:, 4 * s - 4:4 * s],
                           tot, tot[:, 4 * s - 4:4 * s], ALU.add)
                    tt(P, "dve", k.csum, k.csum[:].rearrange("p h s -> p s h"),
                       psc, psc[:].rearrange("p (s h) -> p s h", h=4),
                       pre, pre[:].rearrange("p (s h) -> p s h", h=4), ALU.add)
                    for h in range(4):
                        mm(P, ps[h], ps[h][0:32, 0:128], k.csum, k.csum[:, h, :], k.cf, k.cf[:, C_ID:C_ID + 128],
                           True, True)
                        act(P, cT, cT[:, h, :], ps[h], ps[h][0:32, 0:128], AF.Copy, scale=-1.0)
                    P.dma("sp", chm, k.cdram.ap().rearrange("h (s p) -> s h p", p=128), cT[:],
                          reads=[cT], writes=[k.cdram])
            else:
                nj = 4 if ti < 6 else 3
                for j in range(nj):
                    nci = ti * 4 + j
                    M = 32 if nci == 26 else 128
                    mcol = l * VEC_L + V_MU + nci
                    for tb in range(8):
                        p_ = ps[pr[0] % 4]
                        pr[0] += 1
                        for c in range(KC):
                            mm(P, p_, p_[0:M, :], w, w[:, c, j * 128:j * 128 + M], uT, uT[:, c, tb * 512:(tb + 1) * 512],
                               c == 0, c == KC - 1)
                        zr = zraw[zi[0] % 2]
                        zp = zraw[(zi[0] + 1) % 2]
                        zi[0] += 1
                        if tb == 0:
                            P.op("dve", lambda e: e.memset(zr[0:M, 0:1], 0.0), writes=[zr])
                        else:
                            cp(P, "act", zr, zr[0:M, 0:1], zp, zp[0:M, 512:513])
                        cp(P, "act", zr, zr[0:M, 1:513], p_, p_[0:M, :])
                        tt(P, "dve", dt, dt[0:M, :], zr, zr[0:M, 0:512], zr, zr[0:M, 1:513], ALU.subtract)
                        st = zst[tb % 2]
                        stt(P, "dve", st, st[0:M, :], dt, dt[0:M, :],
                            k.vec[0:M, mcol:mcol + 1], zr, zr[0:M, 1:513], ALU.mult, ALU.add, extra=[k.vec])
                        P.dma("sp", chz[tb % 2], k.zsT.ap()[nci * 128:nci * 128 + M, tb * 512:(tb + 1) * 512],
                              st[0:M, :], reads=[st], writes=[k.zsT])


def phase_b2(P, k, l):
    with P.scope():
        qT = [P.sbuf("qT", [128, T], BF16) for _ in range(2)]
        kT = [P.sbuf("kT", [128, T], BF16) for _ in range(2)]
        vv = [P.sbuf("vv", [128, 32, 128], BF16) for _ in range(2)]
        cq = P.sbuf("cq", [128, T], F32)
        tts = [P.sbuf("tt", [128, 512], F32) for _ in range(3)]
        t2s3 = [P.sbuf("t2", [128, 512], F32) for _ in range(3)]
        e2s = [P.sbuf("e2", [128, 512], F32) for _ in range(3)]
        spb3 = [P.sbuf("spb", [128, 512], BF16) for _ in range(3)]
        pps = [P.sbuf("pp", [128, 512], BF16) for _ in range(3)]
        Rt = P.sbuf("Rt", [128, 512], F32)
        rec = P.sbuf("rec", [128, 512], F32)
        yst = [P.sbuf("yst", [128, 512], BF16) for _ in range(2)]
        pS = [P.psum("pS", [128, 512]) for _ in range(2)]
        pB = [P.psum("pB", [128, 512]) for _ in range(2)]
        pC = [P.psum("pC", [128, 512]) for _ in range(2)]
        pO = [P.psum("pO", [128, 512]) for _ in range(2)]
        chl = [P.chan("al") for _ in range(2)]
        chc = P.chan("ac")
        chy = [P.chan("ay") for _ in range(2)]
        heads = [("sb", h) for h in range(4)] + [("fox", h) for h in range(4)]

        def load(hi):
            kind, h = heads[hi]
            i = hi % 2
            base = 0 if kind == "sb" else 8
            P.dma("sp", chl[i], qT[i][:], k.qkT.ap()[base + h], reads=[k.qkT], writes=[qT[i]])
            P.dma("sp", chl[i], kT[i][:], k.qkT.ap()[base + 4 + h], reads=[k.qkT], writes=[kT[i]])
            P.dma("sp", chl[i], vv[i][:], k.vs.ap()[(0 if kind == "sb" else 4) + h], reads=[k.vs], writes=[vv[i]])
        load(0)
        it = [0]
        yi = [0]
        ones = k.cb[:, C_ONES:C_ONES + 128]
        for hi, (kind, h) in enumerate(heads):
            i = hi % 2
            if hi + 1 < len(heads):
                load(hi + 1)
            q_, k_, v_ = qT[i], kT[i], vv[i]
            if kind == "fox":
                P.dma("sp", chc, cq[:], k.cdram.ap()[h:h + 1, :].partition_broadcast(128), reads=[k.cdram], writes=[cq])
            for QB in range(8):
                nkb = 4 * QB + 4
                O = pO[QB % 2]
                qs = slice(QB * 512, (QB + 1) * 512)
                if kind == "fox":
                    DN = pC[QB % 2]
                    for kb in range(nkb):
                        n = it[0]
                        it[0] += 1
                        S = pS[n % 2]
                        mm(P, S, S[:], k_, k_[:, kb * 128:(kb + 1) * 128], q_, q_[:, qs], True, True)
                        t_ = tts[n % 3]
                        stt(P, "dve", t_, t_[:], S, S[:], k.csum[:, h, kb:kb + 1], cq, cq[:, qs], ALU.add, ALU.add,
                            extra=[k.csum])
                        if kb >= 4 * QB:
                            j = kb - 4 * QB
                            tt(P, "dve", t_, t_[:], t_, t_[:], k.cf, k.cf[:, C_FOXB + j * 512:C_FOXB + (j + 1) * 512],
                               ALU.add)
                        p_ = pps[n % 3]
                        act(P, p_, p_[:], t_, t_[:], AF.Exp)
                        mm(P, O, O[:], v_, v_[:, kb, :], p_, p_[:], kb == 0, kb == nkb - 1)
                        mm(P, DN, DN[:], k.cb, ones, p_, p_[:], kb == 0, kb == nkb - 1)
                    P.op("dve", lambda e: e.reciprocal(out=rec[:], in_=DN[:]), reads=[DN], writes=[rec])
                    ys = yst[yi[0] % 2]
                    tt(P, "dve", ys, ys[:], O, O[:], rec, rec[:], ALU.mult)
                    row0 = 512 + h * 128
                else:
                    P.op("dve", lambda e: e.memset(Rt[:], 0.0), writes=[Rt])
                    def front(kb):
                        n = it[0]
                        it[0] += 1
                        S = pS[n % 2]
                        Bp = pB[n % 2]
                        Cp = pC[n % 2]
                        mm(P, S, S[:], k_, k_[:, kb * 128:(kb + 1) * 128], q_, q_[:, qs], True, True)
                        e1 = tts[n % 3]
                        e2 = e2s[n % 3]
                        sb_ = spb3[n % 3]
                        act(P, e1, e1[:], S, S[:], AF.Exp)
                        act(P, e2, e2[:], S, S[:], AF.Exp, scale=-1.0)
                        act(P, sb_, sb_[:], e1, e1[:], AF.Ln, extra=[k.epsb], bias=k.epsb[:, 1:2])
                        act(P, e2, e2[:], e2, e2[:], AF.Ln, extra=[k.epsb], bias=k.epsb[:, 1:2])
                        diag = kb >= 4 * QB
                        msk = None
                        if diag:
                            j = kb - 4 * QB
                            msk = k.sbm[:, j * 512:(j + 1) * 512]
                            tt(P, "dve", sb_, sb_[:], sb_, sb_[:], k.sbm, msk, ALU.mult)
                        mm(P, Bp, Bp[:], k.cb, k.cb[:, C_NEGT:C_NEGT + 128], sb_, sb_[:], True, True)
                        mm(P, Cp, Cp[:], k.cb, ones, sb_, sb_[:], True, True)
                        return (n, kb, Bp, Cp, e2, msk)

                    def back(st_):
                        n, kb, Bp, Cp, e2, msk = st_
                        t2 = t2s3[n % 3]
                        tt(P, "dve", t2, t2[:], Bp, Bp[:], Rt, Rt[:], ALU.add)
                        tt(P, "dve", Rt, Rt[:], Rt, Rt[:], Cp, Cp[:], ALU.subtract)
                        tt(P, "dve", t2, t2[:], t2, t2[:], e2, e2[:], ALU.subtract)
                        p_ = pps[n % 3]
                        act(P, p_, p_[:], t2, t2[:], AF.Exp)
                        if msk is not None:
                            tt(P, "dve", p_, p_[:], p_, p_[:], k.sbm, msk, ALU.mult)
                        mm(P, O, O[:], v_, v_[:, kb, :], p_, p_[:], kb == nkb - 1, kb == 0)
                    prev = None
                    for kb in reversed(range(nkb)):
                        cur_ = front(kb)
                        if prev is not None:
                            back(prev)
                        prev = cur_
                    back(prev)
                    ys = yst[yi[0] % 2]
                    cp(P, "act", ys, ys[:], O, O[:])
                    row0 = h * 128
                P.dma("sp", chy[yi[0] % 2], k.yT.ap()[row0:row0 + 128, qs], ys[:], reads=[ys], writes=[k.yT])
                yi[0] += 1


class _Stop(Exception):
    pass


def phase_b3(P, k, l):
    try:
        _phase_b3(P, k, l)
    except _Stop:
        pass


def _phase_b3(P, k, l):
    import os
    STOP = int(os.environ.get("B3_STOP", "99"))

    def stop(n):
        if STOP <= n:
            raise _Stop()
    with P.scope():
        vo = l * VEC_L
        NL = 6
        ld = {n: [P.sbuf("ld" + n, [128, 512], F32) for _ in range(2)] for n in ("r", "k", "v", "wa", "g0", "g1")}
        f32n = ("lw", "a", "kk", "tmp", "kp", "kka", "cl", "ex", "e4", "bv", "sq", "gT")
        f = {n: P.sbuf("f" + n, [128, 512], F32) for n in f32n}
        b16n = ("thad", "sg", "sg1", "rt", "kt", "bt", "kap", "Kp", "Bp", "vb")
        b = {n: P.sbuf("b" + n, [128, 512], BF16) for n in b16n}
        WL = P.sbuf("WL", [128, 4], F32)
        tm = {n: P.sbuf("tm" + n, [128, 4, 128], BF16) for n in ("V", "K", "B")}
        wa_t = P.sbuf("wa_t", [128, 128], BF16)
        g0_t = P.sbuf("g0_t", [128, 128], BF16)
        g1_t = P.sbuf("g1_t", [32, 128], BF16)
        def m16(n, cnt_):
            return [P.sbuf(n, [128, 128], BF16) for _ in range(cnt_)]
        A1T, B1T, B2T, TIV = m16("A1T", 8), m16("B1T", 8), m16("B2T", 8), m16("TIV", 8)
        Mp = [m16("Mp", 2) for _ in range(8)]
        Np = [m16("Np", 2) for _ in range(8)]
        R16 = [m16("R16", 2) for _ in range(8)]
        R32 = [P.sbuf("R32", [128, 128], F32) for _ in range(8)]
        S32 = P.sbuf("S32", [128, 128], F32)
        S16 = P.sbuf("S16", [128, 128], BF16)
        Gsb = P.sbuf("Gsb", [128, 128], BF16)
        nP = P.sbuf("nP", [128, 128], BF16)
        ysb4 = [P.sbuf("ysb", [128, 128], F32) for _ in range(4)]
        ysq = P.sbuf("ysq", [128, 128], F32)
        yn = P.sbuf("yn", [128, 128], F32)
        st1 = P.sbuf("st1", [128, 8], F32)
        o1 = P.sbuf("o1", [128, 128], F32)
        yst = [P.sbuf("yst3", [128, 512], BF16) for _ in range(2)]
        pbig = [P.psum("pbig", [128, 512]) for _ in range(2)]
        ptr_t = [P.psum("ptr", [128, 512]) for _ in range(2)]
        ptr = [Buf(ptr_t[i][:, 0:128], f"ptr{i}") for i in range(2)]
        pm_t = [P.psum("pm", [128, 512]) for _ in range(2)]
        pm = [Buf(pm_t[i][:, 0:128], f"pm{i}") for i in range(2)]
        p3_t = [P.psum("p3", [128, 512]) for _ in range(2)]
        p3 = [Buf(p3_t[i][:, 0:128], f"p3{i}") for i in range(2)]
        chl = [P.chan("rl") for _ in range(2)]
        chw = P.chan("rw")
        chy = [P.chan("ry") for _ in range(2)]
        cnt = {"big": 0, "tr": 0, "pm": 0, "p3": 0, "y": 0}

        def nxt(kind, lst):
            i = cnt[kind]
            cnt[kind] += 1
            return lst[i % len(lst)]
        idf = k.cf[:, C_ID:C_ID + 128]
        idb = k.cb[:, C_ID:C_ID + 128]
        blk = k.cf[:, C_BLK:C_BLK + 128]
        mup = k.cf[:, C_MUP:C_MUP + 128]
        mlo = k.cf[:, C_MLO:C_MLO + 128]
        mui = k.cf[:, C_UTRI:C_UTRI + 128]

        def load(idx):
            hp, tb = idx // 8, idx % 8
            i = idx % 2
            cs = slice(tb * 512, (tb + 1) * 512)
            for n, nci, M in (("r", hp, 128), ("k", 8 + hp, 128), ("v", 16 + hp, 128), ("wa", 24, 128),
                              ("g0", 25, 128), ("g1", 26, 32)):
                P.dma("sp", chl[i], ld[n][i][0:M, :], k.zsT.ap()[nci * 128:nci * 128 + M, cs],
                      reads=[k.zsT], writes=[ld[n][i]])
        load(0)
        import os
        NHP = int(os.environ.get("B3_HP", "8"))
        NTB = int(os.environ.get("B3_TB", "8"))
        for hp in range(NHP):
            hc = slice(hp * 128, (hp + 1) * 128)
            P.dma("pool", chw, wa_t[0:64, :], k.wup.ap()[l, :, hc], reads=[k.wup], writes=[wa_t])
            P.dma("pool", chw, wa_t[64:128, :], k.aup.ap()[l, :, hc], reads=[k.aup], writes=[wa_t])
            P.dma("pool", chw, g0_t[:], k.gup.ap()[l, 0:128, hc], reads=[k.gup], writes=[g0_t])
            P.dma("pool", chw, g1_t[:], k.gup.ap()[l, 128:160, hc], reads=[k.gup], writes=[g1_t])
            P.op("dve", lambda e: e.memset(S32[:], 0.0), writes=[S32])
            P.op("dve", lambda e: e.memset(S16[:], 0.0), writes=[S16])

            def vc(col):
                return k.vec[:, vo + col + hp:vo + col + hp + 1]
            for tb in range(NTB):
                idx = hp * 8 + tb
                i = idx % 2
                if tb + 1 < NTB or hp + 1 < NHP:
                    load(idx + 1 if tb + 1 < NTB else (hp + 1) * 8)
                r_, k_, v_, wa_, g0_, g1_ = (ld[n][i] for n in ("r", "k", "v", "wa", "g0", "g1"))
                act(P, b["thad"], b["thad"][0:64, :], wa_, wa_[0:64, :], AF.Tanh)
                cp(P, "act", b["thad"], b["thad"][64:128, :], wa_, wa_[64:128, :])
                act(P, b["sg"], b["sg"][:], g0_, g0_[:], AF.Sigmoid)
                act(P, b["sg1"], b["sg1"][0:32, :], g1_, g1_[0:32, :], AF.Sigmoid)
                pw = nxt("big", pbig)
                mm(P, pw, pw[:], wa_t, wa_t[0:64, :], b["thad"], b["thad"][0:64, :], True, True)
                act(P, f["lw"], f["lw"][:], pw, pw[:], AF.Sigmoid, extra=[k.vec], bias=vc(V_W0))
                pa = nxt("big", pbig)
                mm(P, pa, pa[:], wa_t, wa_t[64:128, :], b["thad"], b["thad"][64:128, :], True, True)
                act(P, f["a"], f["a"][:], pa, pa[:], AF.Sigmoid, extra=[k.vec], bias=vc(V_A0))
                pg = nxt("big", pbig)
                mm(P, pg, pg[:], g0_t, g0_t[:], b["sg"], b["sg"][:], True, False)
                mm(P, pg, pg[:], g1_t, g1_t[0:32, :], b["sg1"], b["sg1"][0:32, :], False, True)
                cp(P, "act", f["gT"], f["gT"][:], pg, pg[:])
                stop(1)
                P.op("dve", lambda e: e.tensor_scalar_mul(out=f["lw"][:], in0=f["lw"][:], scalar1=-EM05),
                     reads=[f["lw"]], writes=[f["lw"]])
                P.op("dve", lambda e: e.tensor_scalar_mul(out=f["kk"][:], in0=k_[:], scalar1=vc(V_KK)),
                     reads=[k_, k.vec], writes=[f["kk"]])
                ts(P, "dve", f["tmp"], f["tmp"][:], f["a"], f["a"][:], vc(V_KA), k.omka[:, l * 8 + hp:l * 8 + hp + 1],
                   ALU.mult, ALU.add, extra=[k.vec, k.omka])
                tt(P, "dve", f["kp"], f["kp"][:], k_, k_[:], f["tmp"], f["tmp"][:], ALU.mult)
                tt(P, "dve", f["sq"], f["sq"][:], f["kk"], f["kk"][:], f["kk"], f["kk"][:], ALU.mult)
                pq = nxt("big", pbig)
                mm(P, pq, pq[:], k.cf, blk, f["sq"], f["sq"][:], True, True)
                act(P, f["ex"], f["ex"][:], pq, pq[:], AF.Ln, extra=[k.epsb], bias=k.epsb[:, 3:4])
                act(P, f["ex"], f["ex"][:], f["ex"], f["ex"][:], AF.Exp, scale=-0.5)
                tt(P, "dve", f["kk"], f["kk"][:], f["kk"], f["kk"][:], f["ex"], f["ex"][:], ALU.mult)
                tt(P, "dve", f["kka"], f["kka"][:], f["kk"], f["kk"][:], f["a"], f["a"][:], ALU.mult)
                tt(P, "dve", f["tmp"], f["tmp"][:], r_, r_[:], f["kp"], f["kp"][:], ALU.mult)
                P.op("dve", lambda e: e.tensor_scalar_mul(out=f["sq"][:], in0=f["tmp"][:], scalar1=vc(V_RK)),
                     reads=[f["tmp"], k.vec], writes=[f["sq"]])
                pb_ = nxt("big", pbig)
                mm(P, pb_, pb_[:], k.cf, blk, f["sq"], f["sq"][:], True, True)
                tt(P, "dve", f["bv"], f["bv"][:], pb_, pb_[:], v_, v_[:], ALU.mult)
                cp(P, "act", b["vb"], b["vb"][:], v_, v_[:])
                stop(2)
                P.op("dve", lambda e: e.tensor_tensor_scan(out=f["cl"][:], data0=k.cf[:, C_RST:C_RST + 512],
                                                           data1=f["lw"][:], initial=0.0, op0=ALU.mult, op1=ALU.add),
                     reads=[k.cf, f["lw"]], writes=[f["cl"]])
                act(P, f["ex"], f["ex"][:], f["cl"], f["cl"][:], AF.Exp)
                tt(P, "dve", b["rt"], b["rt"][:], r_, r_[:], f["ex"], f["ex"][:], ALU.mult)
                act(P, f["ex"], f["ex"][:], f["cl"], f["cl"][:], AF.Exp, scale=-1.0)
                tt(P, "dve", b["kt"], b["kt"][:], f["kp"], f["kp"][:], f["ex"], f["ex"][:], ALU.mult)
                tt(P, "dve", b["bt"], b["bt"][:], f["kka"], f["kka"][:], f["ex"], f["ex"][:], ALU.mult)
                tt(P, "dve", f["tmp"], f["tmp"][:], f["cl"], f["cl"][:], f["lw"], f["lw"][:], ALU.subtract)
                act(P, f["ex"], f["ex"][:], f["tmp"], f["tmp"][:], AF.Exp)
                tt(P, "dve", b["kap"], b["kap"][:], f["kk"], f["kk"][:], f["ex"], f["ex"][:], ALU.mult)
                for c in range(4):
                    act(P, f["e4"], f["e4"][:, c * 128:(c + 1) * 128], f["cl"], f["cl"][:, c * 128:(c + 1) * 128],
                        AF.Exp, scale=-1.0, bias=f["cl"][:, c * 128 + 127:c * 128 + 128])
                    act(P, WL, WL[:, c:c + 1], f["cl"], f["cl"][:, c * 128 + 127:c * 128 + 128], AF.Exp)
                tt(P, "dve", b["Kp"], b["Kp"][:], f["kp"], f["kp"][:], f["e4"], f["e4"][:], ALU.mult)
                tt(P, "dve", b["Bp"], b["Bp"][:], f["kka"], f["kka"][:], f["e4"], f["e4"][:], ALU.mult)
                stop(3)
                TRN = os.environ.get("B3_TRN", "VKBe")
                for c in range(4):
                    for n, src_ in (("V", b["vb"]), ("K", b["Kp"]), ("B", b["Bp"])):
                        if n not in TRN:
                            continue
                        pt = nxt("tr", ptr)
                        mm(P, pt, pt[:], src_, src_[:, c * 128:(c + 1) * 128], k.cb, idb, True, True)
                        if "e" in TRN:
                            cp(P, "act" if n == "V" else "dve", tm[n], tm[n][:, c, :], pt, pt[:])
                stop(4)
                slots4 = pm + ptr

                def chain(ci, c, e):
                    cc = slice(c * 128, (c + 1) * 128)
                    er = slice(e * 64, (e + 1) * 64)
                    M_, N_, R16_, R32_ = Mp[ci], Np[ci], R16[ci], R32[ci]
                    p1 = nxt("pm", slots4)
                    mm(P, p1, p1[:], b["kt"], b["kt"][er, cc], b["kap"], b["kap"][er, cc], True, True)
                    tt(P, "dve", A1T[ci], A1T[ci][:], p1, p1[:], k.cf, mup, ALU.mult)
                    yield
                    p2 = nxt("pm", slots4)
                    mm(P, p2, p2[:], b["bt"], b["bt"][er, cc], b["kap"], b["kap"][er, cc], True, True)
                    stt(P, "dve", M_[0], M_[0][:], p2, p2[:], -1.0, k.cf, mup, ALU.mult, ALU.mult)
                    yield
                    p3_ = nxt("pm", slots4)
                    mm(P, p3_, p3_[:], b["kap"], b["kap"][er, cc], b["bt"], b["bt"][er, cc], True, True)
                    stt(P, "dve", N_[0], N_[0][:], p3_, p3_[:], -1.0, k.cf, mlo, ALU.mult, ALU.mult)
                    yield
                    p4 = nxt("pm", slots4)
                    mm(P, p4, p4[:], b["kt"], b["kt"][er, cc], b["rt"], b["rt"][er, cc], True, True)
                    tt(P, "dve", B1T[ci], B1T[ci][:], p4, p4[:], k.cf, mui, ALU.mult)
                    yield
                    p5 = nxt("pm", slots4)
                    mm(P, p5, p5[:], b["bt"], b["bt"][er, cc], b["rt"], b["rt"][er, cc], True, True)
                    tt(P, "dve", B2T[ci], B2T[ci][:], p5, p5[:], k.cf, mui, ALU.mult)
                    tt(P, "dve", R32_, R32_[:], M_[0], M_[0][:], k.cf, idf, ALU.add)
                    tt(P, "dve", R16_[0], R16_[0][:], M_[0], M_[0][:], k.cf, idf, ALU.add)
                    yield
                    cur = 0
                    for it_ in range(NL):
                        nx_ = 1 - cur
                        pn = nxt("pm", slots4)
                        mm(P, pn, pn[:], M_[cur], M_[cur][:], N_[cur], N_[cur][:], True, True)
                        cp(P, "act", N_[nx_], N_[nx_][:], pn, pn[:])
                        yield
                        if it_ < NL - 1:
                            pm_ = nxt("pm", slots4)
                            mm(P, pm_, pm_[:], N_[cur], N_[cur][:], M_[cur], M_[cur][:], True, True)
                            cp(P, "act", M_[nx_], M_[nx_][:], pm_, pm_[:])
                            yield
                        pr_ = nxt("pm", slots4)
                        mm(P, pr_, pr_[:], N_[nx_], N_[nx_][:], R16_[cur], R16_[cur][:], True, True)
                        tt(P, "dve", R32_, R32_[:], R32_, R32_[:], pr_, pr_[:], ALU.add)
                        last = it_ == NL - 1
                        dst = TIV[ci] if last else R16_[nx_]
                        cp(P, "act", dst, dst[:], R32_, R32_[:])
                        yield
                        cur = nx_
                gens = [chain(c * 2 + e, c, e) for c in range(4) for e in range(2)]
                while gens:
                    alive = []
                    for g_ in gens:
                        try:
                            next(g_)
                            alive.append(g_)
                        except StopIteration:
                            pass
                    gens = alive
                def chain_part(c):
                    cc = slice(c * 128, (c + 1) * 128)
                    stop(5)
                    G = nxt("p3", p3)
                    mm(P, G, G[:], b["kap"], b["kap"][:, cc], S16, S16[:], True, False)
                    for e in range(2):
                        ec = slice(e * 64, (e + 1) * 64)
                        mm(P, G, G[:, ec], A1T[c * 2 + e], A1T[c * 2 + e][:], tm["V"], tm["V"][:, c, ec], False, e == 1)
                    cp(P, "act", Gsb, Gsb[:], G, G[:])
                    Pp = nxt("p3", p3)
                    for e in range(2):
                        ec = slice(e * 64, (e + 1) * 64)
                        mm(P, Pp, Pp[:, ec], TIV[c * 2 + e], TIV[c * 2 + e][:], Gsb, Gsb[:, ec], True, True)
                    act(P, nP, nP[:], Pp, Pp[:], AF.Copy, scale=-1.0)
                    Y = nxt("p3", p3)
                    mm(P, Y, Y[:], b["rt"], b["rt"][:, cc], S16, S16[:], True, False)
                    for e in range(2):
                        ec = slice(e * 64, (e + 1) * 64)
                        mm(P, Y, Y[:, ec], B1T[c * 2 + e], B1T[c * 2 + e][:], tm["V"], tm["V"][:, c, ec], False, False)
                        mm(P, Y, Y[:, ec], B2T[c * 2 + e], B2T[c * 2 + e][:], nP, nP[:, ec], False, e == 1)
                    cp(P, "act", ysb4[c], ysb4[c][:], Y, Y[:])
                    U = nxt("p3", p3)
                    mm(P, U, U[:], tm["K"], tm["K"][:, c, :], tm["V"], tm["V"][:, c, :], True, False)
                    mm(P, U, U[:], tm["B"], tm["B"][:, c, :], nP, nP[:], False, True)
                    for e in range(2):
                        er = slice(e * 64, (e + 1) * 64)
                        stt(P, "dve", S32, S32[er, er], S32, S32[er, er], WL[er, c:c + 1], U, U[er, er],
                            ALU.mult, ALU.add, extra=[WL])
                        cp(P, "act", S16, S16[er, er], S32, S32[er, er])

                def out_part(c):
                    cc = slice(c * 128, (c + 1) * 128)
                    ys = yst[cnt["y"] % 2]
                    stop(6)
                    ysb = ysb4[c]
                    y3 = ysb[:].rearrange("p (e v) -> p e v", e=2)
                    P.op("dve", lambda e_: e_.reduce_sum(out=st1[:, 0:2], in_=y3, axis=AX.X), reads=[ysb], writes=[st1])
                    tt(P, "dve", ysq, ysq[:], ysb, ysb[:], ysb, ysb[:], ALU.mult)
                    P.op("dve", lambda e_: e_.reduce_sum(out=st1[:, 2:4], in_=ysq[:].rearrange("p (e v) -> p e v", e=2),
                                                         axis=AX.X), reads=[ysq], writes=[st1])
                    P.op("dve", lambda e_: e_.tensor_scalar_mul(out=st1[:, 0:2], in0=st1[:, 0:2], scalar1=1.0 / 64),
                         reads=[st1], writes=[st1])
                    tt(P, "dve", st1, st1[:, 4:6], st1, st1[:, 0:2], st1, st1[:, 0:2], ALU.mult)
                    stt(P, "dve", st1, st1[:, 6:8], st1, st1[:, 2:4], 1.0 / 64, st1, st1[:, 4:6], ALU.mult, ALU.subtract)
                    act(P, st1, st1[:, 6:8], st1, st1[:, 6:8], AF.Ln, extra=[k.epsb], bias=k.epsb[:, 2:3])
                    act(P, st1, st1[:, 6:8], st1, st1[:, 6:8], AF.Exp, scale=-0.5)
                    for e in range(2):
                        ec = slice(e * 64, (e + 1) * 64)
                        ts(P, "dve", yn, yn[:, ec], ysb, ysb[:, ec], st1[:, e:e + 1], st1[:, 6 + e:7 + e],
                           ALU.subtract, ALU.mult, extra=[st1])
                    YT = nxt("tr", ptr)
                    mm(P, YT, YT[:], yn, yn[:], k.cf, idf, True, True)
                    stt(P, "dve", o1, o1[:], YT, YT[:], vc(V_LNW), f["bv"], f["bv"][:, cc], ALU.mult, ALU.add,
                        extra=[k.vec])
                    ys = yst[cnt["y"] % 2]
                    stt(P, "dve", ys, ys[:, cc], o1, o1[:], vc(V_LNB), f["gT"], f["gT"][:, cc], ALU.add, ALU.mult,
                        extra=[k.vec])

                chain_part(0)
                for c in range(1, 4):
                    chain_part(c)
                    out_part(c - 1)
                out_part(3)
                ys = yst[cnt["y"] % 2]
                P.dma("sp", chy[cnt["y"] % 2], k.yT.ap()[1024 + hp * 128:1024 + (hp + 1) * 128, tb * 512:(tb + 1) * 512],
                      ys[:], reads=[ys], writes=[k.yT])
                cnt["y"] += 1


def phase_c(P, k, l, xin, xout):
    with P.scope():
        vo = l * VEC_L + V_NRM
        xs = P.sbuf("cx", [128, KC, 512], F32)
        acc = P.sbuf("cacc", [128, KC, 512], F32)
        uT = P.sbuf("cu", [128, KC, 512], BF16)
        mT = P.sbuf("cm", [128, KC, 512], BF16)
        hd = P.sbuf("chd", [128, KC, 512], BF16)
        yb = P.sbuf("cy", [128, KC, 512], BF16)
        wt = [P.sbuf("cw", [128, KC, 512], BF16) for _ in range(2)]
        gs = [[P.sbuf("cg", [128, 512], BF16) for _ in range(4)] for _ in range(3)]
        rs = P.sbuf("crs", [128, 512], F32)
        tmp = [P.sbuf("ctmp", [128, 512], F32) for _ in range(2)]
        pg = [P.psum("cpg", [128, 512]) for _ in range(3)]
        pbr = [P.psum("cpb", [128, 512]) for _ in range(3)]
        pss = P.psum("cpss", [128, 512])
        chw = [P.chan("cw") for _ in range(2)]
        chx = P.chan("cx")
        chu = P.chan("cu")
        chyy = P.chan("cy")
        cho = P.chan("co")
        order = []
        for ng in range(4):
            order += [(k.wg, i * 4 + ng) for i in range(3)] + [(k.wbr, ng)]
        order += [(k.wout, i) for i in range(4)]
        for q in range(4):
            order += [(k.wup_mlp, q * 4 + i) for i in range(4)] + [(k.wdn, q * 4 + i) for i in range(4)]
        NT = len(order)
        seq = [0]

        def loadw(gidx):
            src, ti = order[gidx % NT]
            i = gidx % 2
            P.dma("pool", chw[i], wt[i][:].rearrange("p c n -> p (c n)"), src.ap()[l, ti], reads=[src], writes=[wt[i]])

        def nextw():
            g = seq[0]
            seq[0] += 1
            if g + 1 < NT * 8:
                loadw(g + 1)
            return wt[g % 2]
        loadw(0)
        xv = xin.ap().rearrange("(c p) t -> p c t", p=128)
        ov = xout.ap().rearrange("(c p) t -> p c t", p=128)
        uv = k.uT.ap().rearrange("(c p) t -> p c t", p=128)
        yv = k.yT.ap().rearrange("(c p) t -> p c t", p=128)
        gi = [0]

        def gemm(w, j, rhs_b, out_ps):
            for c in range(KC):
                mm(P, out_ps, out_ps[:], w, w[:, c, j * 128:(j + 1) * 128], rhs_b, rhs_b[:, c, :], c == 0, c == KC - 1)

        def post_norm(gcol):
            rms_stats(P, k, acc, hd, pss, rs)
            for c in range(KC):
                stt(P, "dve", acc, acc[:, c, :], acc, acc[:, c, :], k.vec[:, gcol + c:gcol + c + 1], rs, rs[:],
                    ALU.mult, ALU.mult, extra=[k.vec])
                tt(P, "dve", xs, xs[:, c, :], xs, xs[:, c, :], acc, acc[:, c, :], ALU.add)
        for tb in range(8):
            cs = slice(tb * 512, (tb + 1) * 512)
            for g in range(4):
                gsl = slice(4 * g, 4 * g + 4)
                P.dma("sp", chx, xs[:, gsl, :], xv[:, gsl, cs], reads=[xin], writes=[xs])
                P.dma("sp", chu, uT[:, gsl, :], uv[:, gsl, cs], reads=[k.uT], writes=[uT])
                P.dma("sp", chyy, yb[:, gsl, :], yv[:, gsl, cs], reads=[k.yT], writes=[yb])
            for ng in range(4):
                for i in range(3):
                    w = nextw()
                    for j in range(4):
                        p_ = pg[gi[0] % 3]
                        gi[0] += 1
                        gemm(w, j, uT, p_)
                        act(P, gs[i][j], gs[i][j][:], p_, p_[:], AF.Sigmoid)
                w = nextw()
                for j in range(4):
                    n = ng * 4 + j
                    for i, (k0, nk) in enumerate(((0, 4), (4, 4), (8, 8))):
                        for c in range(nk):
                            mm(P, pbr[i], pbr[i][:], w, w[:, k0 + c, j * 128:(j + 1) * 128], yb, yb[:, k0 + c, :],
                               c == 0, c == nk - 1)
                    tt(P, "dve", tmp[0], tmp[0][:], pbr[0], pbr[0][:], gs[0][j], gs[0][j][:], ALU.mult)
                    tt(P, "dve", tmp[1], tmp[1][:], pbr[1], pbr[1][:], gs[1][j], gs[1][j][:], ALU.mult)
                    tt(P, "dve", tmp[0], tmp[0][:], tmp[0], tmp[0][:], tmp[1], tmp[1][:], ALU.add)
                    tt(P, "dve", tmp[1], tmp[1][:], pbr[2], pbr[2][:], gs[2][j], gs[2][j][:], ALU.mult)
                    tt(P, "dve", mT, mT[:, n, :], tmp[0], tmp[0][:], tmp[1], tmp[1][:], ALU.add)
            for t_ in range(4):
                w = nextw()
                for j in range(4):
                    p_ = pg[gi[0] % 3]
                    gi[0] += 1
                    gemm(w, j, mT, p_)
                    cp(P, "act", acc, acc[:, t_ * 4 + j, :], p_, p_[:])
            post_norm(vo + 16)
            rms_stats(P, k, xs, hd, pss, rs)
            for c in range(KC):
                stt(P, "dve", mT, mT[:, c, :], xs, xs[:, c, :], k.vec[:, vo + 32 + c:vo + 32 + c + 1], rs, rs[:],
                    ALU.mult, ALU.mult, extra=[k.vec])
            for q in range(4):
                for t_ in range(4):
                    w = nextw()
                    for j in range(4):
                        p_ = pg[gi[0] % 3]
                        gi[0] += 1
                        gemm(w, j, mT, p_)
                        tq = tmp[gi[0] % 2]
                        P.op("dve", lambda e: e.tensor_scalar_max(out=tq[:], in0=p_[:], scalar1=0.0), reads=[p_], writes=[tq])
                        tt(P, "dve", hd, hd[:, t_ * 4 + j, :], tq, tq[:], tq, tq[:], ALU.mult)
                for cg in range(4):
                    w = nextw()
                    for j in range(4):
                        p_ = pg[gi[0] % 3]
                        gi[0] += 1
                        gemm(w, j, hd, p_)
                        n = cg * 4 + j
                        if q == 0:
                            cp(P, "act", acc, acc[:, n, :], p_, p_[:])
                        else:
                            tt(P, "dve", acc, acc[:, n, :], acc, acc[:, n, :], p_, p_[:], ALU.add)
            post_norm(vo + 48)
            for g in range(4):
                gsl = slice(4 * g, 4 * g + 4)
                P.dma("sp", cho, ov[:, gsl, cs], xs[:, gsl, :], reads=[xs], writes=[xout])


def build(dbg=(), stages="abcdC", nl=L):
    nc = bass.Bass("TRN2", target_bir_lowering=False)
    P = Prog(nc)
    k = K()

    def dk(name):
        return "ExternalOutput" if name in dbg else "Internal"

    def inp(name, shape):
        return P.dram(name, shape, F32, kind="ExternalInput")
    k.xT = inp("xT", [D, T])
    k.consts = inp("consts", [128, NCONST])
    k.vecs = inp("vecs", [128, L * VEC_L])
    k.wqk = inp("wqk", [L, 4, 128, KC * 512])
    k.wv = inp("wv", [L, 2, 128, KC * 512])
    k.wf = inp("wf", [L, 128, KC * 4])
    k.wrw = inp("wrw", [L, 7, 128, KC * 512])
    k.wup = inp("wup", [L, 64, 1024])
    k.aup = inp("aup", [L, 64, 1024])
    k.gup = inp("gup", [L, 160, 1024])
    if "C" in stages:
        k.wg = inp("wg", [L, 12, 128, KC * 512])
        k.wbr = inp("wbr", [L, 4, 128, KC * 512])
        k.wout = inp("wout", [L, 4, 128, KC * 512])
        k.wup_mlp = inp("wmup", [L, 16, 128, KC * 512])
        k.wdn = inp("wmdn", [L, 16, 128, KC * 512])
    k.uT = P.dram("uT", [D, T], BF16, kind=dk("uT"))
    k.qkT = P.dram("qkT", [NQK, 128, T], BF16, kind=dk("qkT"))
    k.vs = P.dram("vs", [8, 128, 32, 128], BF16, kind=dk("vs"))
    k.zsT = P.dram("zsT", [NRW * 128, T], F32, kind=dk("zsT"))
    k.cdram = P.dram("cdram", [4, T], F32, kind=dk("cdram"))
    k.yT = P.dram("yT", [D, T], BF16, kind=dk("yT"))
    k.x1T = P.dram("x1T", [D, T], F32, kind=dk("x1T"))
    k.out = P.dram("outT", [D, T], F32, kind="ExternalOutput")
    k.csum = P.sbuf("csum", [128, 4, 32], F32)
    load_consts(P, k)
    for l in range(nl):
        xin = k.xT if l == 0 else k.x1T
        xout = k.x1T if l == 0 and nl == 2 else k.out
        if "a" in stages:
            phase_norm(P, k, xin, l * VEC_L + V_NRM + 0, k.uT)
        if "b" in stages:
            phase_b1(P, k, l)
        if "c" in stages:
            phase_b2(P, k, l)
        if "d" in stages:
            phase_b3(P, k, l)
        if "C" in stages:
            phase_c(P, k, l, xin, xout)
    P.close()
    print("instructions:", P.n_inst)
    return nc


QA0, KA0, VA0, QB0, KB0, VB0, FB0, RW0, GT0 = 0, 512, 1024, 1536, 2048, 2560, 3072, 3076, 6436


def _tile(w):
    kk, n = w.shape
    out = np.zeros((kk // 128, 128, 512), np.float32)
    out[:, :, :n] = w.reshape(kk // 128, 128, n)
    return np.ascontiguousarray(out.transpose(1, 0, 2)).reshape(128, -1)


def make_consts():
    c = np.zeros((128, NCONST), np.float32)
    i = np.arange(128)
    c[:, C_ID:C_ID + 128] = np.eye(128)
    c[:, C_ONES:C_ONES + 128] = 1.0
    c[:, C_UTRI:C_UTRI + 128] = (i[:, None] <= i[None, :])
    c[:, C_NEGT:C_NEGT + 128] = -1.0 * (i[:, None] > i[None, :])
    c[:, C_MUP:C_MUP + 128] = (i[:, None] < i[None, :])
    c[:, C_MLO:C_MLO + 128] = (i[:, None] > i[None, :])
    c[:, C_BLK:C_BLK + 128] = ((i[:, None] // 64) == (i[None, :] // 64))
    q = np.arange(512)
    c[:, C_RST:C_RST + 512] = (q % 128 != 0)[None, :]
    for j in range(4):
        kk = j * 128 + i
        c[:, C_FOXB + j * 512:C_FOXB + (j + 1) * 512] = np.where(kk[:, None] <= q[None, :], 0.0, -30000.0)
        c[:, C_SBM + j * 512:C_SBM + (j + 1) * 512] = (kk[:, None] < q[None, :])
    return c


def prep_shared(inp, with_c=True):
    m = {}
    m["consts"] = make_consts()
    vec = np.zeros((128, L * VEC_L), np.float32)
    wqk = np.zeros((L, 4, 128, KC * 512), np.float32)
    wv = np.zeros((L, 2, 128, KC * 512), np.float32)
    wf = np.zeros((L, 128, KC * 4), np.float32)
    wrw = np.zeros((L, 7, 128, KC * 512), np.float32)
    for l in range(L):
        o = l * VEC_L
        for wi, nm in enumerate(("norm_mix_pre", "norm_mix_post", "norm_mlp_pre", "norm_mlp_post")):
            vec[:, o + V_NRM + wi * 16:o + V_NRM + wi * 16 + 16] = inp[nm][l].reshape(16, 128).T
        mu_p = np.zeros(NRW * 128, np.float32)
        mu_p[:3360] = inp["rwkv_mu"][l]
        vec[:, o + V_MU:o + V_MU + NRW] = mu_p.reshape(NRW, 128).T
        for col, nm in ((V_W0, "rwkv_w0"), (V_A0, "rwkv_a0"), (V_KK, "rwkv_k_k"), (V_KA, "rwkv_k_a"),
                        (V_RK, "rwkv_r_k"), (V_LNW, "rwkv_ln_w"), (V_LNB, "rwkv_ln_b")):
            vec[:, o + col:o + col + 8] = inp[nm][l].reshape(8, 128).T
        vec[:, o + V_BF:o + V_BF + 128] = np.tile(inp["b_forget"][l], 32)[None, :]
        w = inp["w_in"][l]
        for ti, c0 in enumerate((QA0, KA0, QB0, KB0)):
            wqk[l, ti] = _tile(w[:, c0:c0 + 512])
        wv[l, 0] = _tile(w[:, VA0:VA0 + 512])
        wv[l, 1] = _tile(w[:, VB0:VB0 + 512])
        wf[l] = np.ascontiguousarray(w[:, FB0:FB0 + 4].reshape(16, 128, 4).transpose(1, 0, 2)).reshape(128, 64)
        for ti in range(7):
            wrw[l, ti] = _tile(w[:, RW0 + ti * 512:min(RW0 + (ti + 1) * 512, RW0 + 3360)])
    m["vecs"] = vec
    m["wqk"], m["wv"], m["wf"], m["wrw"] = wqk, wv, wf, wrw
    m["wup"] = np.ascontiguousarray(inp["rwkv_w_up"])
    m["aup"] = np.ascontiguousarray(inp["rwkv_a_up"])
    m["gup"] = np.ascontiguousarray(inp["rwkv_g_up"])
    if with_c:
        wg = np.zeros((L, 12, 128, KC * 512), np.float32)
        wbr = np.zeros((L, 4, 128, KC * 512), np.float32)
        wout = np.zeros((L, 4, 128, KC * 512), np.float32)
        wmup = np.zeros((L, 16, 128, KC * 512), np.float32)
        wmdn = np.zeros((L, 16, 128, KC * 512), np.float32)
        for l in range(L):
            w = inp["w_in"][l]
            for t in range(12):
                wg[l, t] = _tile(w[:, GT0 + t * 512:GT0 + (t + 1) * 512])
            br = np.concatenate([inp["w_branch_a"][l], inp["w_branch_b"][l], inp["w_branch_c"][l]], 0)
            for t in range(4):
                wbr[l, t] = _tile(br[:, t * 512:(t + 1) * 512])
                wout[l, t] = _tile(inp["w_out"][l][:, t * 512:(t + 1) * 512])
            for t in range(16):
                wmup[l, t] = _tile(inp["w_mlp_up"][l][:, t * 512:(t + 1) * 512])
                q, cg = t // 4, t % 4
                wmdn[l, t] = _tile(inp["w_mlp_down"][l][q * 2048:(q + 1) * 2048, cg * 512:(cg + 1) * 512])
        m["wg"], m["wbr"], m["wout"], m["wmup"], m["wmdn"] = wg, wbr, wout, wmup, wmdn
    return m


_CACHE = {}


def kernel(**inputs):
    inp = {k_: np.asarray(v, dtype=np.float32) for k_, v in inputs.items()}
    if "nc" not in _CACHE:
        _CACHE["nc"] = build()
    nc = _CACHE["nc"]
    shared = prep_shared(inp)
    maps = []
    for c in range(NCORES):
        m = dict(shared)
        m["xT"] = np.ascontiguousarray(inp["x"][c].T)
        maps.append(m)
    res = run_bass_kernel_spmd(nc, maps, core_ids=list(range(NCORES)))
    out = np.stack([np.ascontiguousarray(np.asarray(res.results[c]["outT"]).T) for c in range(NCORES)], 0)
    return out.astype(np.float32)
```

```python
import contextlib
import numpy as np
import concourse.bass as bass
import concourse.mybir as mybir
from concourse.bass_utils import run_bass_kernel_spmd

F32 = mybir.dt.float32
BF16 = mybir.dt.bfloat16
AF = mybir.ActivationFunctionType
ALU = mybir.AluOpType
AX = mybir.AxisListType


class Counter:
    def __init__(self, prog, name, step, epoch):
        self.prog, self.name, self.step, self.epoch = prog, name, step, epoch
        self.n = 0
        self.sems = []

    def next(self):
        self.n += 1
        ep = (self.n - 1) // self.epoch
        while len(self.sems) <= ep:
            self.sems.append(self.prog.new_sem(f"{self.name}_{len(self.sems)}"))
        return self.n

    def sem_val(self, n):
        ep = (n - 1) // self.epoch
        return self.sems[ep], ((n - 1) % self.epoch + 1) * self.step


class Buf:
    def __init__(self, t, name=""):
        self.t = t
        self.name = name
        self.last_write = None
        self.reads = {}

    def __getitem__(self, idx):
        return self.t[idx]

    def ap(self):
        return self.t.ap()


class Prog:
    ENGS = ("pe", "act", "dve", "pool", "sp")

    def __init__(self, nc):
        self.nc = nc
        self.root = contextlib.ExitStack()
        self.stacks = [self.root]
        self.eng = {"pe": nc.tensor, "act": nc.scalar, "dve": nc.vector,
                    "pool": nc.gpsimd, "sp": nc.sync}
        self.cnt = {e: Counter(self, "c" + e, 1, 30000) for e in self.ENGS}
        self.observed = {e: {} for e in self.ENGS}
        self.all_counters = list(self.cnt.values())
        self.n_inst = 0
        self.uid = 0

    def new_sem(self, name):
        return self.root.enter_context(self.nc.semaphore(name))

    def chan(self, name, step=16):
        self.uid += 1
        c = Counter(self, f"d{name}{self.uid}", step, 1800 if step == 16 else 30000)
        self.all_counters.append(c)
        return c

    @contextlib.contextmanager
    def scope(self):
        st = contextlib.ExitStack()
        self.stacks.append(st)
        try:
            yield
        finally:
            self.barrier()
            self.stacks.pop()
            st.close()

    def sbuf(self, name, shape, dtype):
        self.uid += 1
        t = self.stacks[-1].enter_context(
            self.nc.sbuf_tensor(f"{name}_{self.uid}", list(shape), dtype))
        return Buf(t, name)

    def psum(self, name, shape, dtype=F32):
        self.uid += 1
        t = self.stacks[-1].enter_context(
            self.nc.psum_tensor(f"{name}_{self.uid}", list(shape), dtype))
        return Buf(t, name)

    def dram(self, name, shape, dtype, kind="Internal"):
        return Buf(self.nc.dram_tensor(name, list(shape), dtype, kind=kind), name)

    def _need(self, engine, tok, waits):
        if tok is None:
            return
        c, n, teng = tok
        if teng == "pe" and engine == "pe":
            return
        if teng == "dma":
            n = c.n
        ob = self.observed[engine]
        if ob.get(id(c), 0) >= n:
            return
        ob[id(c)] = n
        waits.append((c, n))

    def op(self, engine, fn, reads=(), writes=(), chan=None):
        waits = []
        for b in reads:
            self._need(engine, b.last_write, waits)
        for b in writes:
            self._need(engine, b.last_write, waits)
            for t in b.reads.values():
                self._need(engine, t, waits)
        e = self.eng[engine]
        for c, n in waits:
            s, v = c.sem_val(n)
            e.wait_ge(s, v)
        ins = fn(e)
        c = chan if chan is not None else self.cnt[engine]
        n = c.next()
        s, _ = c.sem_val(n)
        ins.then_inc(s, c.step)
        tok = (c, n, engine if chan is None else "dma")
        for b in reads:
            b.reads[id(c)] = tok
        for b in writes:
            b.last_write = tok
            b.reads = {}
        self.n_inst += 1
        return tok

    def dma(self, queue, chan, out, in_, reads=(), writes=(), **kw):
        return self.op(queue, lambda e: e.dma_start(out=out, in_=in_, **kw),
                       reads=reads, writes=writes, chan=chan)

    def barrier(self):
        toks = [(c, c.n, "x") for c in self.all_counters if c.n > 0]
        for engine in self.ENGS:
            waits = []
            for t in toks:
                self._need(engine, t, waits)
            e = self.eng[engine]
            for c, n in waits:
                s, v = c.sem_val(n)
                e.wait_ge(s, v)

    def close(self):
        self.barrier()
        self.root.close()


D = 2048
T = 4096
KC = 16
L = 2
NCORES = 4
SCALE = 128.0 ** -0.5
EPS = 1e-6
NQK = 16
NRW = 27
V_NRM = 0
V_MU = 64
V_W0, V_A0, V_KK, V_KA, V_RK, V_LNW, V_LNB = 96, 104, 112, 120, 128, 136, 144
V_BF = 152
VEC_L = 288
C_ID = 0
C_ONES = 128
C_UTRI = 256
C_NEGT = 384
C_MUP = 512
C_MLO = 640
C_BLK = 768
C_RST = 896
C_FOXB = 1408
NCF = 1408 + 2048
C_SBM = NCF
NCONST = NCF + 2048
NCB = 896
EM05 = float(np.exp(-0.5))


class K:
    pass


def mm(P, ps, out_ap, lb, lhsT, rb, rhs, start, stop):
    P.op("pe", lambda e: e.matmul(out_ap, lhsT, rhs, start=start, stop=stop),
         reads=[lb, rb], writes=[ps])


def act(P, out_b, out_ap, in_b, in_ap, func, extra=(), **kw):
    P.op("act", lambda e: e.activation(out=out_ap, in_=in_ap, func=func, **kw),
         reads=[in_b] + list(extra), writes=[out_b])


def tt(P, eng, out_b, out_ap, a_b, a_ap, b_b, b_ap, op):
    P.op(eng, lambda e: e.tensor_tensor(out=out_ap, in0=a_ap, in1=b_ap, op=op),
         reads=[a_b, b_b], writes=[out_b])


def stt(P, eng, out_b, out_ap, a_b, a_ap, scalar, b_b, b_ap, op0, op1, extra=()):
    P.op(eng, lambda e: e.scalar_tensor_tensor(out=out_ap, in0=a_ap, scalar=scalar, in1=b_ap, op0=op0, op1=op1),
         reads=[a_b, b_b] + list(extra), writes=[out_b])


def ts(P, eng, out_b, out_ap, a_b, a_ap, s1, s2, op0, op1, extra=()):
    P.op(eng, lambda e: e.tensor_scalar(out=out_ap, in0=a_ap, scalar1=s1, scalar2=s2, op0=op0, op1=op1),
         reads=[a_b] + list(extra), writes=[out_b])


def cp(P, eng, out_b, out_ap, in_b, in_ap):
    if eng == "act":
        P.op("act", lambda e: e.copy(out=out_ap, in_=in_ap), reads=[in_b], writes=[out_b])
    else:
        P.op(eng, lambda e: e.tensor_copy(out=out_ap, in_=in_ap), reads=[in_b], writes=[out_b])


def load_consts(P, k):
    k.cf = P.sbuf("cf", [128, NCF], F32)
    k.cb = P.sbuf("cb", [128, NCB], BF16)
    k.sbm = P.sbuf("sbm", [128, 2048], BF16)
    k.vec = P.sbuf("vec", [128, L * VEC_L], F32)
    k.epsb = P.sbuf("epsb", [128, 4], F32)
    k.omka = P.sbuf("omka", [128, L * 8], F32)
    for i, v in enumerate((EPS, 1.0, 64e-5, 1e-24)):
        P.op("dve", lambda e: e.memset(k.epsb[:, i:i + 1], v), writes=[k.epsb])
    ch = P.chan("const")
    P.dma("sp", ch, k.cf[:], k.consts.ap()[:, 0:NCF], reads=[k.consts], writes=[k.cf])
    P.dma("pool", ch, k.cb[:], k.consts.ap()[:, 0:NCB], reads=[k.consts], writes=[k.cb])
    P.dma("pool", ch, k.sbm[:], k.consts.ap()[:, C_SBM:C_SBM + 2048], reads=[k.consts], writes=[k.sbm])
    P.dma("sp", ch, k.vec[:], k.vecs.ap(), reads=[k.vecs], writes=[k.vec])
    for l in range(L):
        o = l * VEC_L + V_KA
        ts(P, "dve", k.omka, k.omka[:, l * 8:l * 8 + 8], k.vec, k.vec[:, o:o + 8], -1.0, 1.0, ALU.mult, ALU.add)


def rms_stats(P, k, src, sq, ps, rs):
    for g in range(4):
        act(P, sq, sq[:, 4 * g:4 * g + 4, :], src, src[:, 4 * g:4 * g + 4, :], AF.Square)
    for c in range(KC):
        mm(P, ps, ps[:], k.cb, k.cb[:, C_ONES:C_ONES + 128], sq, sq[:, c, :], c == 0, c == KC - 1)
    act(P, rs, rs[:], ps, ps[:], AF.Ln, extra=[k.epsb], scale=1.0 / D, bias=k.epsb[:, 0:1])
    act(P, rs, rs[:], rs, rs[:], AF.Exp, scale=-0.5)


def phase_norm(P, k, src, gain_col, dst):
    with P.scope():
        xs = [P.sbuf("nx", [128, KC, 512], F32) for _ in range(2)]
        sq = [P.sbuf("nsq", [128, KC, 512], BF16) for _ in range(2)]
        us = [P.sbuf("nu", [128, KC, 512], BF16) for _ in range(2)]
        rs = [P.sbuf("nr", [128, 512], F32) for _ in range(2)]
        ps = [P.psum("nps", [128, 512]) for _ in range(2)]
        chl = [P.chan("nl") for _ in range(2)]
        chs = [P.chan("ns") for _ in range(2)]
        sv = src.ap().rearrange("(c p) t -> p c t", p=128)
        dv = dst.ap().rearrange("(c p) t -> p c t", p=128)
        nb = T // 512

        def load(tb):
            i = tb % 2
            for g in range(4):
                P.dma("sp", chl[i], xs[i][:, 4 * g:4 * g + 4, :],
                      sv[:, 4 * g:4 * g + 4, tb * 512:(tb + 1) * 512], reads=[src], writes=[xs[i]])
        load(0)
        for tb in range(nb):
            i = tb % 2
            if tb + 1 < nb:
                load(tb + 1)
            rms_stats(P, k, xs[i], sq[i], ps[i], rs[i])
            for c in range(KC):
                stt(P, "dve", us[i], us[i][:, c, :], xs[i], xs[i][:, c, :], k.vec[:, gain_col + c:gain_col + c + 1],
                    rs[i], rs[i][:], ALU.mult, ALU.mult, extra=[k.vec])
            for g in range(4):
                P.dma("sp", chs[i], dv[:, 4 * g:4 * g + 4, tb * 512:(tb + 1) * 512],
                      us[i][:, 4 * g:4 * g + 4, :], reads=[us[i]], writes=[dst])


def phase_b1(P, k, l):
    with P.scope():
        uT = P.sbuf("uT", [128, KC, T], BF16)
        wt = [P.sbuf("wt", [128, KC, 512], BF16) for _ in range(2)]
        wft = P.sbuf("wft", [128, KC, 4], BF16)
        qst = [P.sbuf("qst", [128, 512], BF16) for _ in range(2)]
        zst = [P.sbuf("zst", [128, 512], F32) for _ in range(2)]
        vst = [P.sbuf("vst", [128, 512], BF16) for _ in range(2)]
        zraw = [P.sbuf("zraw", [128, 513], F32) for _ in range(2)]
        dt = P.sbuf("dt", [128, 512], F32)
        fsb = P.sbuf("fsb", [128, 128], F32)
        fsp = P.sbuf("fsp", [128, 128], F32)
        tot = P.sbuf("tot", [128, 128], F32)
        pre = P.sbuf("pre", [128, 128], F32)
        cT = P.sbuf("cT", [32, 4, 128], F32)
        ps = [P.psum("b1ps", [128, 512]) for _ in range(4)]
        psf = P.psum("psf", [128, 128])
        psc = P.psum("psc", [128, 128])
        pst = P.psum("pst", [128, 128])
        chu = P.chan("u")
        chw = [P.chan("w") for _ in range(2)]
        chq = [P.chan("q") for _ in range(2)]
        chz = [P.chan("z") for _ in range(2)]
        chv = [P.chan("v") for _ in range(2)]
        chm = P.chan("m")

        uv = k.uT.ap().rearrange("(c p) t -> p c t", p=128)
        for hh in range(2):
            for g in range(4):
                P.dma("sp", chu, uT[:, 4 * g:4 * g + 4, hh * 2048:(hh + 1) * 2048],
                      uv[:, 4 * g:4 * g + 4, hh * 2048:(hh + 1) * 2048], reads=[k.uT], writes=[uT])
        P.dma("pool", chm, wft[:].rearrange("p c n -> p (c n)"), k.wf.ap()[l], reads=[k.wf], writes=[wft])

        tiles = [("qk", k.wqk, i) for i in range(4)] + [("v", k.wv, i) for i in range(2)] + \
                [("rw", k.wrw, i) for i in range(7)]

        def loadw(idx):
            kind, src, ti = tiles[idx]
            i = idx % 2
            P.dma("pool", chw[i], wt[i][:].rearrange("p c n -> p (c n)"), src.ap()[l, ti], reads=[src], writes=[wt[i]])
        loadw(0)
        pr = [0]
        zi = [0]
        for idx, (kind, src, ti) in enumerate(tiles):
            w = wt[idx % 2]
            if idx + 1 < len(tiles):
                loadw(idx + 1)
            if kind == "qk":
                for j in range(4):
                    nci = ti * 4 + j
                    isq = ti in (0, 2)
                    for tb in range(8):
                        p_ = ps[pr[0] % 4]
                        pr[0] += 1
                        for c in range(KC):
                            mm(P, p_, p_[:], w, w[:, c, j * 128:(j + 1) * 128], uT, uT[:, c, tb * 512:(tb + 1) * 512],
                               c == 0, c == KC - 1)
                        st = qst[tb % 2]
                        act(P, st, st[:], p_, p_[:], AF.Copy, scale=SCALE if isq else 1.0)
                        P.dma("sp", chq[tb % 2], k.qkT.ap()[nci, :, tb * 512:(tb + 1) * 512], st[:],
                              reads=[st], writes=[k.qkT])
            elif kind == "v":
                vview = k.vs.ap()[ti * 4:ti * 4 + 4].rearrange("h p s d -> p s h d")
                for s in range(32):
                    p_ = ps[pr[0] % 4]
                    pr[0] += 1
                    for c in range(KC):
                        mm(P, p_, p_[:], uT, uT[:, c, s * 128:(s + 1) * 128], w, w[:, c, :], c == 0, c == KC - 1)
                    st = vst[s % 2]
                    cp(P, "dve", st, st[:], p_, p_[:])
                    P.dma("sp", chv[s % 2], vview[:, s, :, :], st[:].rearrange("p (h d) -> p h d", h=4),
                          reads=[st], writes=[k.vs])
                if ti == 1:
                    for s in range(32):
                        for c in range(KC):
                            mm(P, psf, psf[:, 4 * s:4 * s + 4], uT, uT[:, c, s * 128:(s + 1) * 128], wft, wft[:, c, :],
                               c == 0, c == KC - 1)
                    vb = l * VEC_L + V_BF
                    tt(P, "dve", fsb, fsb[:], psf, psf[:], k.vec, k.vec[:, vb:vb + 128], ALU.add)
                    act(P, fsp, fsp[:], fsb, fsb[:], AF.Exp, scale=-1.0)
                    act(P, fsp, fsp[:], fsp, fsp[:], AF.Ln, extra=[k.epsb], bias=k.epsb[:, 1:2])
                    mm(P, psc, psc[:], k.cf, k.cf[:, C_UTRI:C_UTRI + 128], fsp, fsp[:], True, True)
                    mm(P, pst, pst[:], k.cf, k.cf[:, C_ONES:C_ONES + 128], fsp, fsp[:], True, True)
                    cp(P, "dve", tot, tot[:], pst, pst[:])
                    P.op("dve", lambda e: e.memset(pre[:, 0:4], 0.0), writes=[pre])
                    for s in range(1, 32):
                        tt(P, "dve", pre, pre[:, 4 * s:4 * s + 4], pre, pre[:, 4 * s - 4:4 * s],
                           tot, tot[:, 4 * s - 4:4 * s], ALU.add)
                    tt(P, "dve", k.csum, k.csum[:].rearrange("p h s -> p s h"),
                       psc, psc[:].rearrange("p (s h) -> p s h", h=4),
                       pre, pre[:].rearrange("p (s h) -> p s h", h=4), ALU.add)
                    for h in range(4):
                        mm(P, ps[h], ps[h][0:32, 0:128], k.csum, k.csum[:, h, :], k.cf, k.cf[:, C_ID:C_ID + 128],
                           True, True)
                        act(P, cT, cT[:, h, :], ps[h], ps[h][0:32, 0:128], AF.Copy, scale=-1.0)
                    P.dma("sp", chm, k.cdram.ap().rearrange("h (s p) -> s h p", p=128), cT[:],
                          reads=[cT], writes=[k.cdram])
            else:
                nj = 4 if ti < 6 else 3
                for j in range(nj):
                    nci = ti * 4 + j
                    M = 32 if nci == 26 else 128
                    mcol = l * VEC_L + V_MU + nci
                    for tb in range(8):
                        p_ = ps[pr[0] % 4]
                        pr[0] += 1
                        for c in range(KC):
                            mm(P, p_, p_[0:M, :], w, w[:, c, j * 128:j * 128 + M], uT, uT[:, c, tb * 512:(tb + 1) * 512],
                               c == 0, c == KC - 1)
                        zr = zraw[zi[0] % 2]
                        zp = zraw[(zi[0] + 1) % 2]
                        zi[0] += 1
                        if tb == 0:
                            P.op("dve", lambda e: e.memset(zr[0:M, 0:1], 0.0), writes=[zr])
                        else:
                            cp(P, "act", zr, zr[0:M, 0:1], zp, zp[0:M, 512:513])
                        cp(P, "act", zr, zr[0:M, 1:513], p_, p_[0:M, :])
                        tt(P, "dve", dt, dt[0:M, :], zr, zr[0:M, 0:512], zr, zr[0:M, 1:513], ALU.subtract)
                        st = zst[tb % 2]
                        stt(P, "dve", st, st[0:M, :], dt, dt[0:M, :],
                            k.vec[0:M, mcol:mcol + 1], zr, zr[0:M, 1:513], ALU.mult, ALU.add, extra=[k.vec])
                        P.dma("sp", chz[tb % 2], k.zsT.ap()[nci * 128:nci * 128 + M, tb * 512:(tb + 1) * 512],
                              st[0:M, :], reads=[st], writes=[k.zsT])


def phase_b2(P, k, l):
    with P.scope():
        qT = [P.sbuf("qT", [128, T], BF16) for _ in range(2)]
        kT = [P.sbuf("kT", [128, T], BF16) for _ in range(2)]
        vv = [P.sbuf("vv", [128, 32, 128], BF16) for _ in range(2)]
        cq = P.sbuf("cq", [128, T], F32)
        tts = [P.sbuf("tt", [128, 512], F32) for _ in range(3)]
        t2s3 = [P.sbuf("t2", [128, 512], F32) for _ in range(3)]
        e2s = [P.sbuf("e2", [128, 512], F32) for _ in range(3)]
        spb3 = [P.sbuf("spb", [128, 512], BF16) for _ in range(3)]
        pps = [P.sbuf("pp", [128, 512], BF16) for _ in range(3)]
        Rt = P.sbuf("Rt", [128, 512], F32)
        rec = P.sbuf("rec", [128, 512], F32)
        yst = [P.sbuf("yst", [128, 512], BF16) for _ in range(2)]
        pS = [P.psum("pS", [128, 512]) for _ in range(2)]
        pB = [P.psum("pB", [128, 512]) for _ in range(2)]
        pC = [P.psum("pC", [128, 512]) for _ in range(2)]
        pO = [P.psum("pO", [128, 512]) for _ in range(2)]
        chl = [P.chan("al") for _ in range(2)]
        chc = P.chan("ac")
        chy = [P.chan("ay") for _ in range(2)]
        heads = [("sb", h) for h in range(4)] + [("fox", h) for h in range(4)]

        def load(hi):
            kind, h = heads[hi]
            i = hi % 2
            base = 0 if kind == "sb" else 8
            P.dma("sp", chl[i], qT[i][:], k.qkT.ap()[base + h], reads=[k.qkT], writes=[qT[i]])
            P.dma("sp", chl[i], kT[i][:], k.qkT.ap()[base + 4 + h], reads=[k.qkT], writes=[kT[i]])
            P.dma("sp", chl[i], vv[i][:], k.vs.ap()[(0 if kind == "sb" else 4) + h], reads=[k.vs], writes=[vv[i]])
        load(0)
        it = [0]
        yi = [0]
        ones = k.cb[:, C_ONES:C_ONES + 128]
        for hi, (kind, h) in enumerate(heads):
            i = hi % 2
            if hi + 1 < len(heads):
                load(hi + 1)
            q_, k_, v_ = qT[i], kT[i], vv[i]
            if kind == "fox":
                P.dma("sp", chc, cq[:], k.cdram.ap()[h:h + 1, :].partition_broadcast(128), reads=[k.cdram], writes=[cq])
            for QB in range(8):
                nkb = 4 * QB + 4
                O = pO[QB % 2]
                qs = slice(QB * 512, (QB + 1) * 512)
                if kind == "fox":
                    DN = pC[QB % 2]
                    for kb in range(nkb):
                        n = it[0]
                        it[0] += 1
                        S = pS[n % 2]
                        mm(P, S, S[:], k_, k_[:, kb * 128:(kb + 1) * 128], q_, q_[:, qs], True, True)
                        t_ = tts[n % 3]
                        stt(P, "dve", t_, t_[:], S, S[:], k.csum[:, h, kb:kb + 1], cq, cq[:, qs], ALU.add, ALU.add,
                            extra=[k.csum])
                        if kb >= 4 * QB:
                            j = kb - 4 * QB
                            tt(P, "dve", t_, t_[:], t_, t_[:], k.cf, k.cf[:, C_FOXB + j * 512:C_FOXB + (j + 1) * 512],
                               ALU.add)
                        p_ = pps[n % 3]
                        act(P, p_, p_[:], t_, t_[:], AF.Exp)
                        mm(P, O, O[:], v_, v_[:, kb, :], p_, p_[:], kb == 0, kb == nkb - 1)
                        mm(P, DN, DN[:], k.cb, ones, p_, p_[:], kb == 0, kb == nkb - 1)
                    P.op("dve", lambda e: e.reciprocal(out=rec[:], in_=DN[:]), reads=[DN], writes=[rec])
                    ys = yst[yi[0] % 2]
                    tt(P, "dve", ys, ys[:], O, O[:], rec, rec[:], ALU.mult)
                    row0 = 512 + h * 128
                else:
                    P.op("dve", lambda e: e.memset(Rt[:], 0.0), writes=[Rt])
                    def front(kb):
                        n = it[0]
                        it[0] += 1
                        S = pS[n % 2]
                        Bp = pB[n % 2]
                        Cp = pC[n % 2]
                        mm(P, S, S[:], k_, k_[:, kb * 128:(kb + 1) * 128], q_, q_[:, qs], True, True)
                        e1 = tts[n % 3]
                        e2 = e2s[n % 3]
                        sb_ = spb3[n % 3]
                        act(P, e1, e1[:], S, S[:], AF.Exp)
                        act(P, e2, e2[:], S, S[:], AF.Exp, scale=-1.0)
                        act(P, sb_, sb_[:], e1, e1[:], AF.Ln, extra=[k.epsb], bias=k.epsb[:, 1:2])
                        act(P, e2, e2[:], e2, e2[:], AF.Ln, extra=[k.epsb], bias=k.epsb[:, 1:2])
                        diag = kb >= 4 * QB
                        msk = None
                        if diag:
                            j = kb - 4 * QB
                            msk = k.sbm[:, j * 512:(j + 1) * 512]
                            tt(P, "dve", sb_, sb_[:], sb_, sb_[:], k.sbm, msk, ALU.mult)
                        mm(P, Bp, Bp[:], k.cb, k.cb[:, C_NEGT:C_NEGT + 128], sb_, sb_[:], True, True)
                        mm(P, Cp, Cp[:], k.cb, ones, sb_, sb_[:], True, True)
                        return (n, kb, Bp, Cp, e2, msk)

                    def back(st_):
                        n, kb, Bp, Cp, e2, msk = st_
                        t2 = t2s3[n % 3]
                        tt(P, "dve", t2, t2[:], Bp, Bp[:], Rt, Rt[:], ALU.add)
                        tt(P, "dve", Rt, Rt[:], Rt, Rt[:], Cp, Cp[:], ALU.subtract)
                        tt(P, "dve", t2, t2[:], t2, t2[:], e2, e2[:], ALU.subtract)
                        p_ = pps[n % 3]
                        act(P, p_, p_[:], t2, t2[:], AF.Exp)
                        if msk is not None:
                            tt(P, "dve", p_, p_[:], p_, p_[:], k.sbm, msk, ALU.mult)
                        mm(P, O, O[:], v_, v_[:, kb, :], p_, p_[:], kb == nkb - 1, kb == 0)
                    prev = None
                    for kb in reversed(range(nkb)):
                        cur_ = front(kb)
                        if prev is not None:
                            back(prev)
                        prev = cur_
                    back(prev)
                    ys = yst[yi[0] % 2]
                    cp(P, "act", ys, ys[:], O, O[:])
                    row0 = h * 128
                P.dma("sp", chy[yi[0] % 2], k.yT.ap()[row0:row0 + 128, qs], ys[:], reads=[ys], writes=[k.yT])
                yi[0] += 1


class _Stop(Exception):
    pass


def phase_b3(P, k, l):
    try:
        _phase_b3(P, k, l)
    except _Stop:
        pass


def _phase_b3(P, k, l):
    import os
    STOP = int(os.environ.get("B3_STOP", "99"))

    def stop(n):
        if STOP <= n:
            raise _Stop()
    with P.scope():
        vo = l * VEC_L
        NL = 6
        ld = {n: [P.sbuf("ld" + n, [128, 512], F32) for _ in range(2)] for n in ("r", "k", "v", "wa", "g0", "g1")}
        f32n = ("lw", "a", "kk", "tmp", "kp", "kka", "cl", "ex", "e4", "bv", "sq", "gT")
        f = {n: P.sbuf("f" + n, [128, 512], F32) for n in f32n}
        b16n = ("thad", "sg", "sg1", "rt", "kt", "bt", "kap", "Kp", "Bp", "vb")
        b = {n: P.sbuf("b" + n, [128, 512], BF16) for n in b16n}
        WL = P.sbuf("WL", [128, 4], F32)
        tm = {n: P.sbuf("tm" + n, [128, 4, 128], BF16) for n in ("V", "K", "B")}
        wa_t = P.sbuf("wa_t", [128, 128], BF16)
        g0_t = P.sbuf("g0_t", [128, 128], BF16)
        g1_t = P.sbuf("g1_t", [32, 128], BF16)
        def m16(n, cnt_):
            return [P.sbuf(n, [128, 128], BF16) for _ in range(cnt_)]
        A1T, B1T, B2T, TIV = m16("A1T", 8), m16("B1T", 8), m16("B2T", 8), m16("TIV", 8)
        Mp = [m16("Mp", 2) for _ in range(8)]
        Np = [m16("Np", 2) for _ in range(8)]
        R16 = [m16("R16", 2) for _ in range(8)]
        R32 = [P.sbuf("R32", [128, 128], F32) for _ in range(8)]
        S32 = P.sbuf("S32", [128, 128], F32)
        S16 = P.sbuf("S16", [128, 128], BF16)
        Gsb = P.sbuf("Gsb", [128, 128], BF16)
        nP = P.sbuf("nP", [128, 128], BF16)
        ysb4 = [P.sbuf("ysb", [128, 128], F32) for _ in range(4)]
        ysq = P.sbuf("ysq", [128, 128], F32)
        yn = P.sbuf("yn", [128, 128], F32)
        st1 = P.sbuf("st1", [128, 8], F32)
        o1 = P.sbuf("o1", [128, 128], F32)
        yst = [P.sbuf("yst3", [128, 512], BF16) for _ in range(2)]
        pbig = [P.psum("pbig", [128, 512]) for _ in range(2)]
        ptr_t = [P.psum("ptr", [128, 512]) for _ in range(2)]
        ptr = [Buf(ptr_t[i][:, 0:128], f"ptr{i}") for i in range(2)]
        pm_t = [P.psum("pm", [128, 512]) for _ in range(2)]
        pm = [Buf(pm_t[i][:, 0:128], f"pm{i}") for i in range(2)]
        p3_t = [P.psum("p3", [128, 512]) for _ in range(2)]
        p3 = [Buf(p3_t[i][:, 0:128], f"p3{i}") for i in range(2)]
        chl = [P.chan("rl") for _ in range(2)]
        chw = P.chan("rw")
        chy = [P.chan("ry") for _ in range(2)]
        cnt = {"big": 0, "tr": 0, "pm": 0, "p3": 0, "y": 0}

        def nxt(kind, lst):
            i = cnt[kind]
            cnt[kind] += 1
            return lst[i % len(lst)]
        idf = k.cf[:, C_ID:C_ID + 128]
        idb = k.cb[:, C_ID:C_ID + 128]
        blk = k.cf[:, C_BLK:C_BLK + 128]
        mup = k.cf[:, C_MUP:C_MUP + 128]
        mlo = k.cf[:, C_MLO:C_MLO + 128]
        mui = k.cf[:, C_UTRI:C_UTRI + 128]

        def load(idx):
            hp, tb = idx // 8, idx % 8
            i = idx % 2
            cs = slice(tb * 512, (tb + 1) * 512)
            for n, nci, M in (("r", hp, 128), ("k", 8 + hp, 128), ("v", 16 + hp, 128), ("wa", 24, 128),
                              ("g0", 25, 128), ("g1", 26, 32)):
                P.dma("sp", chl[i], ld[n][i][0:M, :], k.zsT.ap()[nci * 128:nci * 128 + M, cs],
                      reads=[k.zsT], writes=[ld[n][i]])
        load(0)
        import os
        NHP = int(os.environ.get("B3_HP", "8"))
        NTB = int(os.environ.get("B3_TB", "8"))
        for hp in range(NHP):
            hc = slice(hp * 128, (hp + 1) * 128)
            P.dma("pool", chw, wa_t[0:64, :], k.wup.ap()[l, :, hc], reads=[k.wup], writes=[wa_t])
            P.dma("pool", chw, wa_t[64:128, :], k.aup.ap()[l, :, hc], reads=[k.aup], writes=[wa_t])
            P.dma("pool", chw, g0_t[:], k.gup.ap()[l, 0:128, hc], reads=[k.gup], writes=[g0_t])
            P.dma("pool", chw, g1_t[:], k.gup.ap()[l, 128:160, hc], reads=[k.gup], writes=[g1_t])
            P.op("dve", lambda e: e.memset(S32[:], 0.0), writes=[S32])
            P.op("dve", lambda e: e.memset(S16[:], 0.0), writes=[S16])

            def vc(col):
                return k.vec[:, vo + col + hp:vo + col + hp + 1]
            for tb in range(NTB):
                idx = hp * 8 + tb
                i = idx % 2
                if tb + 1 < NTB or hp + 1 < NHP:
                    load(idx + 1 if tb + 1 < NTB else (hp + 1) * 8)
                r_, k_, v_, wa_, g0_, g1_ = (ld[n][i] for n in ("r", "k", "v", "wa", "g0", "g1"))
                act(P, b["thad"], b["thad"][0:64, :], wa_, wa_[0:64, :], AF.Tanh)
                cp(P, "act", b["thad"], b["thad"][64:128, :], wa_, wa_[64:128, :])
                act(P, b["sg"], b["sg"][:], g0_, g0_[:], AF.Sigmoid)
                act(P, b["sg1"], b["sg1"][0:32, :], g1_, g1_[0:32, :], AF.Sigmoid)
                pw = nxt("big", pbig)
                mm(P, pw, pw[:], wa_t, wa_t[0:64, :], b["thad"], b["thad"][0:64, :], True, True)
                act(P, f["lw"], f["lw"][:], pw, pw[:], AF.Sigmoid, extra=[k.vec], bias=vc(V_W0))
                pa = nxt("big", pbig)
                mm(P, pa, pa[:], wa_t, wa_t[64:128, :], b["thad"], b["thad"][64:128, :], True, True)
                act(P, f["a"], f["a"][:], pa, pa[:], AF.Sigmoid, extra=[k.vec], bias=vc(V_A0))
                pg = nxt("big", pbig)
                mm(P, pg, pg[:], g0_t, g0_t[:], b["sg"], b["sg"][:], True, False)
                mm(P, pg, pg[:], g1_t, g1_t[0:32, :], b["sg1"], b["sg1"][0:32, :], False, True)
                cp(P, "act", f["gT"], f["gT"][:], pg, pg[:])
                stop(1)
                P.op("dve", lambda e: e.tensor_scalar_mul(out=f["lw"][:], in0=f["lw"][:], scalar1=-EM05),
                     reads=[f["lw"]], writes=[f["lw"]])
                P.op("dve", lambda e: e.tensor_scalar_mul(out=f["kk"][:], in0=k_[:], scalar1=vc(V_KK)),
                     reads=[k_, k.vec], writes=[f["kk"]])
                ts(P, "dve", f["tmp"], f["tmp"][:], f["a"], f["a"][:], vc(V_KA), k.omka[:, l * 8 + hp:l * 8 + hp + 1],
                   ALU.mult, ALU.add, extra=[k.vec, k.omka])
                tt(P, "dve", f["kp"], f["kp"][:], k_, k_[:], f["tmp"], f["tmp"][:], ALU.mult)
                tt(P, "dve", f["sq"], f["sq"][:], f["kk"], f["kk"][:], f["kk"], f["kk"][:], ALU.mult)
                pq = nxt("big", pbig)
                mm(P, pq, pq[:], k.cf, blk, f["sq"], f["sq"][:], True, True)
                act(P, f["ex"], f["ex"][:], pq, pq[:], AF.Ln, extra=[k.epsb], bias=k.epsb[:, 3:4])
                act(P, f["ex"], f["ex"][:], f["ex"], f["ex"][:], AF.Exp, scale=-0.5)
                tt(P, "dve", f["kk"], f["kk"][:], f["kk"], f["kk"][:], f["ex"], f["ex"][:], ALU.mult)
                tt(P, "dve", f["kka"], f["kka"][:], f["kk"], f["kk"][:], f["a"], f["a"][:], ALU.mult)
                tt(P, "dve", f["tmp"], f["tmp"][:], r_, r_[:], f["kp"], f["kp"][:], ALU.mult)
                P.op("dve", lambda e: e.tensor_scalar_mul(out=f["sq"][:], in0=f["tmp"][:], scalar1=vc(V_RK)),
                     reads=[f["tmp"], k.vec], writes=[f["sq"]])
                pb_ = nxt("big", pbig)
                mm(P, pb_, pb_[:], k.cf, blk, f["sq"], f["sq"][:], True, True)
                tt(P, "dve", f["bv"], f["bv"][:], pb_, pb_[:], v_, v_[:], ALU.mult)
                cp(P, "act", b["vb"], b["vb"][:], v_, v_[:])
                stop(2)
                P.op("dve", lambda e: e.tensor_tensor_scan(out=f["cl"][:], data0=k.cf[:, C_RST:C_RST + 512],
                                                           data1=f["lw"][:], initial=0.0, op0=ALU.mult, op1=ALU.add),
                     reads=[k.cf, f["lw"]], writes=[f["cl"]])
                act(P, f["ex"], f["ex"][:], f["cl"], f["cl"][:], AF.Exp)
                tt(P, "dve", b["rt"], b["rt"][:], r_, r_[:], f["ex"], f["ex"][:], ALU.mult)
                act(P, f["ex"], f["ex"][:], f["cl"], f["cl"][:], AF.Exp, scale=-1.0)
                tt(P, "dve", b["kt"], b["kt"][:], f["kp"], f["kp"][:], f["ex"], f["ex"][:], ALU.mult)
                tt(P, "dve", b["bt"], b["bt"][:], f["kka"], f["kka"][:], f["ex"], f["ex"][:], ALU.mult)
                tt(P, "dve", f["tmp"], f["tmp"][:], f["cl"], f["cl"][:], f["lw"], f["lw"][:], ALU.subtract)
                act(P, f["ex"], f["ex"][:], f["tmp"], f["tmp"][:], AF.Exp)
                tt(P, "dve", b["kap"], b["kap"][:], f["kk"], f["kk"][:], f["ex"], f["ex"][:], ALU.mult)
                for c in range(4):
                    act(P, f["e4"], f["e4"][:, c * 128:(c + 1) * 128], f["cl"], f["cl"][:, c * 128:(c + 1) * 128],
                        AF.Exp, scale=-1.0, bias=f["cl"][:, c * 128 + 127:c * 128 + 128])
                    act(P, WL, WL[:, c:c + 1], f["cl"], f["cl"][:, c * 128 + 127:c * 128 + 128], AF.Exp)
                tt(P, "dve", b["Kp"], b["Kp"][:], f["kp"], f["kp"][:], f["e4"], f["e4"][:], ALU.mult)
                tt(P, "dve", b["Bp"], b["Bp"][:], f["kka"], f["kka"][:], f["e4"], f["e4"][:], ALU.mult)
                stop(3)
                TRN = os.environ.get("B3_TRN", "VKBe")
                for c in range(4):
                    for n, src_ in (("V", b["vb"]), ("K", b["Kp"]), ("B", b["Bp"])):
                        if n not in TRN:
                            continue
                        pt = nxt("tr", ptr)
                        mm(P, pt, pt[:], src_, src_[:, c * 128:(c + 1) * 128], k.cb, idb, True, True)
                        if "e" in TRN:
                            cp(P, "act" if n == "V" else "dve", tm[n], tm[n][:, c, :], pt, pt[:])
                stop(4)
                slots4 = pm + ptr

                def chain(ci, c, e):
                    cc = slice(c * 128, (c + 1) * 128)
                    er = slice(e * 64, (e + 1) * 64)
                    M_, N_, R16_, R32_ = Mp[ci], Np[ci], R16[ci], R32[ci]
                    p1 = nxt("pm", slots4)
                    mm(P, p1, p1[:], b["kt"], b["kt"][er, cc], b["kap"], b["kap"][er, cc], True, True)
                    tt(P, "dve", A1T[ci], A1T[ci][:], p1, p1[:], k.cf, mup, ALU.mult)
                    yield
                    p2 = nxt("pm", slots4)
                    mm(P, p2, p2[:], b["bt"], b["bt"][er, cc], b["kap"], b["kap"][er, cc], True, True)
                    stt(P, "dve", M_[0], M_[0][:], p2, p2[:], -1.0, k.cf, mup, ALU.mult, ALU.mult)
                    yield
                    p3_ = nxt("pm", slots4)
                    mm(P, p3_, p3_[:], b["kap"], b["kap"][er, cc], b["bt"], b["bt"][er, cc], True, True)
                    stt(P, "dve", N_[0], N_[0][:], p3_, p3_[:], -1.0, k.cf, mlo, ALU.mult, ALU.mult)
                    yield
                    p4 = nxt("pm", slots4)
                    mm(P, p4, p4[:], b["kt"], b["kt"][er, cc], b["rt"], b["rt"][er, cc], True, True)
                    tt(P, "dve", B1T[ci], B1T[ci][:], p4, p4[:], k.cf, mui, ALU.mult)
                    yield
                    p5 = nxt("pm", slots4)
                    mm(P, p5, p5[:], b["bt"], b["bt"][er, cc], b["rt"], b["rt"][er, cc], True, True)
                    tt(P, "dve", B2T[ci], B2T[ci][:], p5, p5[:], k.cf, mui, ALU.mult)
                    tt(P, "dve", R32_, R32_[:], M_[0], M_[0][:], k.cf, idf, ALU.add)
                    tt(P, "dve", R16_[0], R16_[0][:], M_[0], M_[0][:], k.cf, idf, ALU.add)
                    yield
                    cur = 0
                    for it_ in range(NL):
                        nx_ = 1 - cur
                        pn = nxt("pm", slots4)
                        mm(P, pn, pn[:], M_[cur], M_[cur][:], N_[cur], N_[cur][:], True, True)
                        cp(P, "act", N_[nx_], N_[nx_][:], pn, pn[:])
                        yield
                        if it_ < NL - 1:
                            pm_ = nxt("pm", slots4)
                            mm(P, pm_, pm_[:], N_[cur], N_[cur][:], M_[cur], M_[cur][:], True, True)
                            cp(P, "act", M_[nx_], M_[nx_][:], pm_, pm_[:])
                            yield
                        pr_ = nxt("pm", slots4)
                        mm(P, pr_, pr_[:], N_[nx_], N_[nx_][:], R16_[cur], R16_[cur][:], True, True)
                        tt(P, "dve", R32_, R32_[:], R32_, R32_[:], pr_, pr_[:], ALU.add)
                        last = it_ == NL - 1
                        dst = TIV[ci] if last else R16_[nx_]
                        cp(P, "act", dst, dst[:], R32_, R32_[:])
                        yield
                        cur = nx_
                gens = [chain(c * 2 + e, c, e) for c in range(4) for e in range(2)]
                while gens:
                    alive = []
                    for g_ in gens:
                        try:
                            next(g_)
                            alive.append(g_)
                        except StopIteration:
                            pass
                    gens = alive
                def chain_part(c):
                    cc = slice(c * 128, (c + 1) * 128)
                    stop(5)
                    G = nxt("p3", p3)
                    mm(P, G, G[:], b["kap"], b["kap"][:, cc], S16, S16[:], True, False)
                    for e in range(2):
                        ec = slice(e * 64, (e + 1) * 64)
                        mm(P, G, G[:, ec], A1T[c * 2 + e], A1T[c * 2 + e][:], tm["V"], tm["V"][:, c, ec], False, e == 1)
                    cp(P, "act", Gsb, Gsb[:], G, G[:])
                    Pp = nxt("p3", p3)
                    for e in range(2):
                        ec = slice(e * 64, (e + 1) * 64)
                        mm(P, Pp, Pp[:, ec], TIV[c * 2 + e], TIV[c * 2 + e][:], Gsb, Gsb[:, ec], True, True)
                    act(P, nP, nP[:], Pp, Pp[:], AF.Copy, scale=-1.0)
                    Y = nxt("p3", p3)
                    mm(P, Y, Y[:], b["rt"], b["rt"][:, cc], S16, S16[:], True, False)
                    for e in range(2):
                        ec = slice(e * 64, (e + 1) * 64)
                        mm(P, Y, Y[:, ec], B1T[c * 2 + e], B1T[c * 2 + e][:], tm["V"], tm["V"][:, c, ec], False, False)
                        mm(P, Y, Y[:, ec], B2T[c * 2 + e], B2T[c * 2 + e][:], nP, nP[:, ec], False, e == 1)
                    cp(P, "act", ysb4[c], ysb4[c][:], Y, Y[:])
                    U = nxt("p3", p3)
                    mm(P, U, U[:], tm["K"], tm["K"][:, c, :], tm["V"], tm["V"][:, c, :], True, False)
                    mm(P, U, U[:], tm["B"], tm["B"][:, c, :], nP, nP[:], False, True)
                    for e in range(2):
                        er = slice(e * 64, (e + 1) * 64)
                        stt(P, "dve", S32, S32[er, er], S32, S32[er, er], WL[er, c:c + 1], U, U[er, er],
                            ALU.mult, ALU.add, extra=[WL])
                        cp(P, "act", S16, S16[er, er], S32, S32[er, er])

                def out_part(c):
                    cc = slice(c * 128, (c + 1) * 128)
                    ys = yst[cnt["y"] % 2]
                    stop(6)
                    ysb = ysb4[c]
                    y3 = ysb[:].rearrange("p (e v) -> p e v", e=2)
                    P.op("dve", lambda e_: e_.reduce_sum(out=st1[:, 0:2], in_=y3, axis=AX.X), reads=[ysb], writes=[st1])
                    tt(P, "dve", ysq, ysq[:], ysb, ysb[:], ysb, ysb[:], ALU.mult)
                    P.op("dve", lambda e_: e_.reduce_sum(out=st1[:, 2:4], in_=ysq[:].rearrange("p (e v) -> p e v", e=2),
                                                         axis=AX.X), reads=[ysq], writes=[st1])
                    P.op("dve", lambda e_: e_.tensor_scalar_mul(out=st1[:, 0:2], in0=st1[:, 0:2], scalar1=1.0 / 64),
                         reads=[st1], writes=[st1])
                    tt(P, "dve", st1, st1[:, 4:6], st1, st1[:, 0:2], st1, st1[:, 0:2], ALU.mult)
                    stt(P, "dve", st1, st1[:, 6:8], st1, st1[:, 2:4], 1.0 / 64, st1, st1[:, 4:6], ALU.mult, ALU.subtract)
                    act(P, st1, st1[:, 6:8], st1, st1[:, 6:8], AF.Ln, extra=[k.epsb], bias=k.epsb[:, 2:3])
                    act(P, st1, st1[:, 6:8], st1, st1[:, 6:8], AF.Exp, scale=-0.5)
                    for e in range(2):
                        ec = slice(e * 64, (e + 1) * 64)
                        ts(P, "dve", yn, yn[:, ec], ysb, ysb[:, ec], st1[:, e:e + 1], st1[:, 6 + e:7 + e],
                           ALU.subtract, ALU.mult, extra=[st1])
                    YT = nxt("tr", ptr)
                    mm(P, YT, YT[:], yn, yn[:], k.cf, idf, True, True)
                    stt(P, "dve", o1, o1[:], YT, YT[:], vc(V_LNW), f["bv"], f["bv"][:, cc], ALU.mult, ALU.add,
                        extra=[k.vec])
                    ys = yst[cnt["y"] % 2]
                    stt(P, "dve", ys, ys[:, cc], o1, o1[:], vc(V_LNB), f["gT"], f["gT"][:, cc], ALU.add, ALU.mult,
                        extra=[k.vec])

                chain_part(0)
                for c in range(1, 4):
                    chain_part(c)
                    out_part(c - 1)
                out_part(3)
                ys = yst[cnt["y"] % 2]
                P.dma("sp", chy[cnt["y"] % 2], k.yT.ap()[1024 + hp * 128:1024 + (hp + 1) * 128, tb * 512:(tb + 1) * 512],
                      ys[:], reads=[ys], writes=[k.yT])
                cnt["y"] += 1


def c_order(k):
    order = []
    for ng in range(4):
        order += [(k.wg, i * 4 + ng) for i in range(3)] + [(k.wbr, ng)]
    order += [(k.wout, i) for i in range(4)]
    for q in range(4):
        order += [(k.wup_mlp, q * 4 + i) for i in range(4)] + [(k.wdn, q * 4 + i) for i in range(4)]
    return order


def convert_c_weights(P, k, l):
    ch = P.chan("cv")
    for t, (src, ti) in enumerate(c_order(k)):
        P.dma("pool", ch, k.wcb.ap()[l, t], src.ap()[l, ti], reads=[src], writes=[k.wcb])


def phase_c(P, k, l, xin, xout):
    with P.scope():
        vo = l * VEC_L + V_NRM
        xs = P.sbuf("cx", [128, KC, 512], F32)
        acc = P.sbuf("cacc", [128, KC, 512], F32)
        uT = P.sbuf("cu", [128, KC, 512], BF16)
        mT = P.sbuf("cm", [128, KC, 512], BF16)
        hd = P.sbuf("chd", [128, KC, 512], BF16)
        yb = P.sbuf("cy", [128, KC, 512], BF16)
        wt = [P.sbuf("cw", [128, KC, 512], BF16) for _ in range(2)]
        gs = [[P.sbuf("cg", [128, 512], BF16) for _ in range(4)] for _ in range(3)]
        rs = P.sbuf("crs", [128, 512], F32)
        tmp = [P.sbuf("ctmp", [128, 512], F32) for _ in range(2)]
        pg = [P.psum("cpg", [128, 512]) for _ in range(3)]
        pbr = [P.psum("cpb", [128, 512]) for _ in range(3)]
        pss = P.psum("cpss", [128, 512])
        chw = [P.chan("cw") for _ in range(2)]
        chx = P.chan("cx")
        chu = P.chan("cu")
        chyy = P.chan("cy")
        cho = P.chan("co")
        order = c_order(k)
        NT = len(order)
        seq = [0]

        def loadw(gidx):
            i = gidx % 2
            P.dma("sp", chw[i], wt[i][:].rearrange("p c n -> p (c n)"), k.wcb.ap()[l, gidx % NT],
                  reads=[k.wcb], writes=[wt[i]])

        def nextw():
            g = seq[0]
            seq[0] += 1
            if g + 1 < NT * 8:
                loadw(g + 1)
            return wt[g % 2]
        loadw(0)
        xv = xin.ap().rearrange("(c p) t -> p c t", p=128)
        ov = xout.ap().rearrange("(c p) t -> p c t", p=128)
        uv = k.uT.ap().rearrange("(c p) t -> p c t", p=128)
        yv = k.yT.ap().rearrange("(c p) t -> p c t", p=128)
        gi = [0]

        def gemm(w, j, rhs_b, out_ps):
            for c in range(KC):
                mm(P, out_ps, out_ps[:], w, w[:, c, j * 128:(j + 1) * 128], rhs_b, rhs_b[:, c, :], c == 0, c == KC - 1)

        def post_norm(gcol):
            rms_stats(P, k, acc, hd, pss, rs)
            for c in range(KC):
                stt(P, "dve", acc, acc[:, c, :], acc, acc[:, c, :], k.vec[:, gcol + c:gcol + c + 1], rs, rs[:],
                    ALU.mult, ALU.mult, extra=[k.vec])
                tt(P, "dve", xs, xs[:, c, :], xs, xs[:, c, :], acc, acc[:, c, :], ALU.add)
        for tb in range(8):
            cs = slice(tb * 512, (tb + 1) * 512)
            for g in range(4):
                gsl = slice(4 * g, 4 * g + 4)
                P.dma("sp", chx, xs[:, gsl, :], xv[:, gsl, cs], reads=[xin], writes=[xs])
                P.dma("sp", chu, uT[:, gsl, :], uv[:, gsl, cs], reads=[k.uT], writes=[uT])
                P.dma("sp", chyy, yb[:, gsl, :], yv[:, gsl, cs], reads=[k.yT], writes=[yb])
            for ng in range(4):
                for i in range(3):
                    w = nextw()
                    for j in range(4):
                        p_ = pg[gi[0] % 3]
                        gi[0] += 1
                        gemm(w, j, uT, p_)
                        act(P, gs[i][j], gs[i][j][:], p_, p_[:], AF.Sigmoid)
                w = nextw()
                for j in range(4):
                    n = ng * 4 + j
                    for i, (k0, nk) in enumerate(((0, 4), (4, 4), (8, 8))):
                        for c in range(nk):
                            mm(P, pbr[i], pbr[i][:], w, w[:, k0 + c, j * 128:(j + 1) * 128], yb, yb[:, k0 + c, :],
                               c == 0, c == nk - 1)
                    tt(P, "dve", tmp[0], tmp[0][:], pbr[0], pbr[0][:], gs[0][j], gs[0][j][:], ALU.mult)
                    tt(P, "dve", tmp[1], tmp[1][:], pbr[1], pbr[1][:], gs[1][j], gs[1][j][:], ALU.mult)
                    tt(P, "dve", tmp[0], tmp[0][:], tmp[0], tmp[0][:], tmp[1], tmp[1][:], ALU.add)
                    tt(P, "dve", tmp[1], tmp[1][:], pbr[2], pbr[2][:], gs[2][j], gs[2][j][:], ALU.mult)
                    tt(P, "dve", mT, mT[:, n, :], tmp[0], tmp[0][:], tmp[1], tmp[1][:], ALU.add)
            for t_ in range(4):
                w = nextw()
                for j in range(4):
                    p_ = pg[gi[0] % 3]
                    gi[0] += 1
                    gemm(w, j, mT, p_)
                    cp(P, "act", acc, acc[:, t_ * 4 + j, :], p_, p_[:])
            post_norm(vo + 16)
            rms_stats(P, k, xs, hd, pss, rs)
            for c in range(KC):
                stt(P, "dve", mT, mT[:, c, :], xs, xs[:, c, :], k.vec[:, vo + 32 + c:vo + 32 + c + 1], rs, rs[:],
                    ALU.mult, ALU.mult, extra=[k.vec])
            for q in range(4):
                for t_ in range(4):
                    w = nextw()
                    for j in range(4):
                        p_ = pg[gi[0] % 3]
                        gi[0] += 1
                        gemm(w, j, mT, p_)
                        tq = tmp[gi[0] % 2]
                        P.op("dve", lambda e: e.tensor_scalar_max(out=tq[:], in0=p_[:], scalar1=0.0), reads=[p_], writes=[tq])
                        tt(P, "dve", hd, hd[:, t_ * 4 + j, :], tq, tq[:], tq, tq[:], ALU.mult)
                for cg in range(4):
                    w = nextw()
                    for j in range(4):
                        p_ = pg[gi[0] % 3]
                        gi[0] += 1
                        gemm(w, j, hd, p_)
                        n = cg * 4 + j
                        if q == 0:
                            cp(P, "act", acc, acc[:, n, :], p_, p_[:])
                        else:
                            tt(P, "dve", acc, acc[:, n, :], acc, acc[:, n, :], p_, p_[:], ALU.add)
            post_norm(vo + 48)
            for g in range(4):
                gsl = slice(4 * g, 4 * g + 4)
                P.dma("sp", cho, ov[:, gsl, cs], xs[:, gsl, :], reads=[xs], writes=[xout])


def build(dbg=(), stages="abcdC", nl=L):
    nc = bass.Bass("TRN2", target_bir_lowering=False)
    P = Prog(nc)
    k = K()

    def dk(name):
        return "ExternalOutput" if name in dbg else "Internal"

    def inp(name, shape):
        return P.dram(name, shape, F32, kind="ExternalInput")
    k.xT = inp("xT", [D, T])
    k.consts = inp("consts", [128, NCONST])
    k.vecs = inp("vecs", [128, L * VEC_L])
    k.wqk = inp("wqk", [L, 4, 128, KC * 512])
    k.wv = inp("wv", [L, 2, 128, KC * 512])
    k.wf = inp("wf", [L, 128, KC * 4])
    k.wrw = inp("wrw", [L, 7, 128, KC * 512])
    k.wup = inp("wup", [L, 64, 1024])
    k.aup = inp("aup", [L, 64, 1024])
    k.gup = inp("gup", [L, 160, 1024])
    if "C" in stages:
        k.wg = inp("wg", [L, 12, 128, KC * 512])
        k.wbr = inp("wbr", [L, 4, 128, KC * 512])
        k.wout = inp("wout", [L, 4, 128, KC * 512])
        k.wup_mlp = inp("wmup", [L, 16, 128, KC * 512])
        k.wdn = inp("wmdn", [L, 16, 128, KC * 512])
    k.uT = P.dram("uT", [D, T], BF16, kind=dk("uT"))
    k.qkT = P.dram("qkT", [NQK, 128, T], BF16, kind=dk("qkT"))
    k.vs = P.dram("vs", [8, 128, 32, 128], BF16, kind=dk("vs"))
    k.zsT = P.dram("zsT", [NRW * 128, T], F32, kind=dk("zsT"))
    k.cdram = P.dram("cdram", [4, T], F32, kind=dk("cdram"))
    k.yT = P.dram("yT", [D, T], BF16, kind=dk("yT"))
    k.x1T = P.dram("x1T", [D, T], F32, kind=dk("x1T"))
    k.out = P.dram("outT", [D, T], F32, kind="ExternalOutput")
    if "C" in stages:
        k.wcb = P.dram("wcb", [L, 52, 128, KC * 512], BF16)
    k.csum = P.sbuf("csum", [128, 4, 32], F32)
    load_consts(P, k)
    for l in range(nl):
        xin = k.xT if l == 0 else k.x1T
        xout = k.x1T if l == 0 and nl == 2 else k.out
        if "a" in stages:
            phase_norm(P, k, xin, l * VEC_L + V_NRM + 0, k.uT)
        if "b" in stages:
            phase_b1(P, k, l)
        if "C" in stages:
            convert_c_weights(P, k, l)
        if "c" in stages:
            phase_b2(P, k, l)
        if "d" in stages:
            phase_b3(P, k, l)
        if "C" in stages:
            phase_c(P, k, l, xin, xout)
    P.close()
    print("instructions:", P.n_inst)
    return nc


QA0, KA0, VA0, QB0, KB0, VB0, FB0, RW0, GT0 = 0, 512, 1024, 1536, 2048, 2560, 3072, 3076, 6436


def _tile(w):
    kk, n = w.shape
    out = np.zeros((kk // 128, 128, 512), np.float32)
    out[:, :, :n] = w.reshape(kk // 128, 128, n)
    return np.ascontiguousarray(out.transpose(1, 0, 2)).reshape(128, -1)


def make_consts():
    c = np.zeros((128, NCONST), np.float32)
    i = np.arange(128)
    c[:, C_ID:C_ID + 128] = np.eye(128)
    c[:, C_ONES:C_ONES + 128] = 1.0
    c[:, C_UTRI:C_UTRI + 128] = (i[:, None] <= i[None, :])
    c[:, C_NEGT:C_NEGT + 128] = -1.0 * (i[:, None] > i[None, :])
    c[:, C_MUP:C_MUP + 128] = (i[:, None] < i[None, :])
    c[:, C_MLO:C_MLO + 128] = (i[:, None] > i[None, :])
    c[:, C_BLK:C_BLK + 128] = ((i[:, None] // 64) == (i[None, :] // 64))
    q = np.arange(512)
    c[:, C_RST:C_RST + 512] = (q % 128 != 0)[None, :]
    for j in range(4):
        kk = j * 128 + i
        c[:, C_FOXB + j * 512:C_FOXB + (j + 1) * 512] = np.where(kk[:, None] <= q[None, :], 0.0, -30000.0)
        c[:, C_SBM + j * 512:C_SBM + (j + 1) * 512] = (kk[:, None] < q[None, :])
    return c


def prep_shared(inp, with_c=True):
    m = {}
    m["consts"] = make_consts()
    vec = np.zeros((128, L * VEC_L), np.float32)
    wqk = np.zeros((L, 4, 128, KC * 512), np.float32)
    wv = np.zeros((L, 2, 128, KC * 512), np.float32)
    wf = np.zeros((L, 128, KC * 4), np.float32)
    wrw = np.zeros((L, 7, 128, KC * 512), np.float32)
    for l in range(L):
        o = l * VEC_L
        for wi, nm in enumerate(("norm_mix_pre", "norm_mix_post", "norm_mlp_pre", "norm_mlp_post")):
            vec[:, o + V_NRM + wi * 16:o + V_NRM + wi * 16 + 16] = inp[nm][l].reshape(16, 128).T
        mu_p = np.zeros(NRW * 128, np.float32)
        mu_p[:3360] = inp["rwkv_mu"][l]
        vec[:, o + V_MU:o + V_MU + NRW] = mu_p.reshape(NRW, 128).T
        for col, nm in ((V_W0, "rwkv_w0"), (V_A0, "rwkv_a0"), (V_KK, "rwkv_k_k"), (V_KA, "rwkv_k_a"),
                        (V_RK, "rwkv_r_k"), (V_LNW, "rwkv_ln_w"), (V_LNB, "rwkv_ln_b")):
            vec[:, o + col:o + col + 8] = inp[nm][l].reshape(8, 128).T
        vec[:, o + V_BF:o + V_BF + 128] = np.tile(inp["b_forget"][l], 32)[None, :]
        w = inp["w_in"][l]
        for ti, c0 in enumerate((QA0, KA0, QB0, KB0)):
            wqk[l, ti] = _tile(w[:, c0:c0 + 512])
        wv[l, 0] = _tile(w[:, VA0:VA0 + 512])
        wv[l, 1] = _tile(w[:, VB0:VB0 + 512])
        wf[l] = np.ascontiguousarray(w[:, FB0:FB0 + 4].reshape(16, 128, 4).transpose(1, 0, 2)).reshape(128, 64)
        for ti in range(7):
            wrw[l, ti] = _tile(w[:, RW0 + ti * 512:min(RW0 + (ti + 1) * 512, RW0 + 3360)])
    m["vecs"] = vec
    m["wqk"], m["wv"], m["wf"], m["wrw"] = wqk, wv, wf, wrw
    m["wup"] = np.ascontiguousarray(inp["rwkv_w_up"])
    m["aup"] = np.ascontiguousarray(inp["rwkv_a_up"])
    m["gup"] = np.ascontiguousarray(inp["rwkv_g_up"])
    if with_c:
        wg = np.zeros((L, 12, 128, KC * 512), np.float32)
        wbr = np.zeros((L, 4, 128, KC * 512), np.float32)
        wout = np.zeros((L, 4, 128, KC * 512), np.float32)
        wmup = np.zeros((L, 16, 128, KC * 512), np.float32)
        wmdn = np.zeros((L, 16, 128, KC * 512), np.float32)
        for l in range(L):
            w = inp["w_in"][l]
            for t in range(12):
                wg[l, t] = _tile(w[:, GT0 + t * 512:GT0 + (t + 1) * 512])
            br = np.concatenate([inp["w_branch_a"][l], inp["w_branch_b"][l], inp["w_branch_c"][l]], 0)
            for t in range(4):
                wbr[l, t] = _tile(br[:, t * 512:(t + 1) * 512])
                wout[l, t] = _tile(inp["w_out"][l][:, t * 512:(t + 1) * 512])
            for t in range(16):
                wmup[l, t] = _tile(inp["w_mlp_up"][l][:, t * 512:(t + 1) * 512])
                q, cg = t // 4, t % 4
                wmdn[l, t] = _tile(inp["w_mlp_down"][l][q * 2048:(q + 1) * 2048, cg * 512:(cg + 1) * 512])
        m["wg"], m["wbr"], m["wout"], m["wmup"], m["wmdn"] = wg, wbr, wout, wmup, wmdn
    return m


_CACHE = {}


def kernel(**inputs):
    inp = {k_: np.asarray(v, dtype=np.float32) for k_, v in inputs.items()}
    if "nc" not in _CACHE:
        _CACHE["nc"] = build()
    nc = _CACHE["nc"]
    shared = prep_shared(inp)
    maps = []
    for c in range(NCORES):
        m = dict(shared)
        m["xT"] = np.ascontiguousarray(inp["x"][c].T)
        maps.append(m)
    res = run_bass_kernel_spmd(nc, maps, core_ids=list(range(NCORES)))
    out = np.stack([np.ascontiguousarray(np.asarray(res.results[c]["outT"]).T) for c in range(NCORES)], 0)
    return out.astype(np.float32)
```

```python
import contextlib
import numpy as np
import concourse.bass as bass
import concourse.mybir as mybir
from concourse.bass_utils import run_bass_kernel_spmd

F32 = mybir.dt.float32
BF16 = mybir.dt.bfloat16
AF = mybir.ActivationFunctionType
ALU = mybir.AluOpType
AX = mybir.AxisListType


class Counter:
    def __init__(self, prog, name, step, epoch):
        self.prog, self.name, self.step, self.epoch = prog, name, step, epoch
        self.n = 0
        self.sems = []

    def next(self):
        self.n += 1
        ep = (self.n - 1) // self.epoch
        while len(self.sems) <= ep:
            self.sems.append(self.prog.new_sem(f"{self.name}_{len(self.sems)}"))
        return self.n

    def sem_val(self, n):
        ep = (n - 1) // self.epoch
        return self.sems[ep], ((n - 1) % self.epoch + 1) * self.step


class Buf:
    def __init__(self, t, name=""):
        self.t = t
        self.name = name
        self.last_write = None
        self.reads = {}

    def __getitem__(self, idx):
        return self.t[idx]

    def ap(self):
        return self.t.ap()


class Prog:
    ENGS = ("pe", "act", "dve", "pool", "sp")

    def __init__(self, nc):
        self.nc = nc
        self.root = contextlib.ExitStack()
        self.stacks = [self.root]
        self.eng = {"pe": nc.tensor, "act": nc.scalar, "dve": nc.vector,
                    "pool": nc.gpsimd, "sp": nc.sync}
        self.cnt = {e: Counter(self, "c" + e, 1, 30000) for e in self.ENGS}
        self.observed = {e: {} for e in self.ENGS}
        self.all_counters = list(self.cnt.values())
        self.n_inst = 0
        self.uid = 0

    def new_sem(self, name):
        return self.root.enter_context(self.nc.semaphore(name))

    def chan(self, name, step=16):
        self.uid += 1
        c = Counter(self, f"d{name}{self.uid}", step, 1800 if step == 16 else 30000)
        self.all_counters.append(c)
        return c

    @contextlib.contextmanager
    def scope(self):
        st = contextlib.ExitStack()
        self.stacks.append(st)
        try:
            yield
        finally:
            self.barrier()
            self.stacks.pop()
            st.close()

    def sbuf(self, name, shape, dtype):
        self.uid += 1
        t = self.stacks[-1].enter_context(
            self.nc.sbuf_tensor(f"{name}_{self.uid}", list(shape), dtype))
        return Buf(t, name)

    def psum(self, name, shape, dtype=F32):
        self.uid += 1
        t = self.stacks[-1].enter_context(
            self.nc.psum_tensor(f"{name}_{self.uid}", list(shape), dtype))
        return Buf(t, name)

    def dram(self, name, shape, dtype, kind="Internal"):
        return Buf(self.nc.dram_tensor(name, list(shape), dtype, kind=kind), name)

    def _need(self, engine, tok, waits):
        if tok is None:
            return
        c, n, teng = tok
        if teng == "pe" and engine == "pe":
            return
        if teng == "dma":
            n = c.n
        ob = self.observed[engine]
        if ob.get(id(c), 0) >= n:
            return
        ob[id(c)] = n
        waits.append((c, n))

    def op(self, engine, fn, reads=(), writes=(), chan=None):
        waits = []
        for b in reads:
            self._need(engine, b.last_write, waits)
        for b in writes:
            self._need(engine, b.last_write, waits)
            for t in b.reads.values():
                self._need(engine, t, waits)
        e = self.eng[engine]
        for c, n in waits:
            s, v = c.sem_val(n)
            e.wait_ge(s, v)
        ins = fn(e)
        c = chan if chan is not None else self.cnt[engine]
        n = c.next()
        s, _ = c.sem_val(n)
        ins.then_inc(s, c.step)
        tok = (c, n, engine if chan is None else "dma")
        for b in reads:
            b.reads[id(c)] = tok
        for b in writes:
            b.last_write = tok
            b.reads = {}
        self.n_inst += 1
        return tok

    def dma(self, queue, chan, out, in_, reads=(), writes=(), **kw):
        return self.op(queue, lambda e: e.dma_start(out=out, in_=in_, **kw),
                       reads=reads, writes=writes, chan=chan)

    def barrier(self):
        toks = [(c, c.n, "x") for c in self.all_counters if c.n > 0]
        for engine in self.ENGS:
            waits = []
            for t in toks:
                self._need(engine, t, waits)
            e = self.eng[engine]
            for c, n in waits:
                s, v = c.sem_val(n)
                e.wait_ge(s, v)

    def close(self):
        self.barrier()
        self.root.close()


D = 2048
T = 4096
KC = 16
L = 2
NCORES = 4
SCALE = 128.0 ** -0.5
EPS = 1e-6
NQK = 16
NRW = 27
V_NRM = 0
V_MU = 64
V_W0, V_A0, V_KK, V_KA, V_RK, V_LNW, V_LNB = 96, 104, 112, 120, 128, 136, 144
V_BF = 152
VEC_L = 288
C_ID = 0
C_ONES = 128
C_UTRI = 256
C_NEGT = 384
C_MUP = 512
C_MLO = 640
C_BLK = 768
C_RST = 896
C_FOXB = 1408
NCF = 1408 + 2048
C_SBM = NCF
NCONST = NCF + 2048
NCB = 896
EM05 = float(np.exp(-0.5))


class K:
    pass


def mm(P, ps, out_ap, lb, lhsT, rb, rhs, start, stop):
    P.op("pe", lambda e: e.matmul(out_ap, lhsT, rhs, start=start, stop=stop),
         reads=[lb, rb], writes=[ps])


def act(P, out_b, out_ap, in_b, in_ap, func, extra=(), **kw):
    P.op("act", lambda e: e.activation(out=out_ap, in_=in_ap, func=func, **kw),
         reads=[in_b] + list(extra), writes=[out_b])


def tt(P, eng, out_b, out_ap, a_b, a_ap, b_b, b_ap, op):
    P.op(eng, lambda e: e.tensor_tensor(out=out_ap, in0=a_ap, in1=b_ap, op=op),
         reads=[a_b, b_b], writes=[out_b])


def stt(P, eng, out_b, out_ap, a_b, a_ap, scalar, b_b, b_ap, op0, op1, extra=()):
    P.op(eng, lambda e: e.scalar_tensor_tensor(out=out_ap, in0=a_ap, scalar=scalar, in1=b_ap, op0=op0, op1=op1),
         reads=[a_b, b_b] + list(extra), writes=[out_b])


def ts(P, eng, out_b, out_ap, a_b, a_ap, s1, s2, op0, op1, extra=()):
    P.op(eng, lambda e: e.tensor_scalar(out=out_ap, in0=a_ap, scalar1=s1, scalar2=s2, op0=op0, op1=op1),
         reads=[a_b] + list(extra), writes=[out_b])


def cp(P, eng, out_b, out_ap, in_b, in_ap):
    if eng == "act":
        P.op("act", lambda e: e.copy(out=out_ap, in_=in_ap), reads=[in_b], writes=[out_b])
    else:
        P.op(eng, lambda e: e.tensor_copy(out=out_ap, in_=in_ap), reads=[in_b], writes=[out_b])


def load_consts(P, k):
    k.cf = P.sbuf("cf", [128, NCF], F32)
    k.cb = P.sbuf("cb", [128, NCB], BF16)
    k.sbm = P.sbuf("sbm", [128, 2048], BF16)
    k.vec = P.sbuf("vec", [128, L * VEC_L], F32)
    k.epsb = P.sbuf("epsb", [128, 4], F32)
    k.omka = P.sbuf("omka", [128, L * 8], F32)
    for i, v in enumerate((EPS, 1.0, 64e-5, 1e-24)):
        P.op("dve", lambda e: e.memset(k.epsb[:, i:i + 1], v), writes=[k.epsb])
    ch = P.chan("const")
    P.dma("sp", ch, k.cf[:], k.consts.ap()[:, 0:NCF], reads=[k.consts], writes=[k.cf])
    P.dma("pool", ch, k.cb[:], k.consts.ap()[:, 0:NCB], reads=[k.consts], writes=[k.cb])
    P.dma("pool", ch, k.sbm[:], k.consts.ap()[:, C_SBM:C_SBM + 2048], reads=[k.consts], writes=[k.sbm])
    P.dma("sp", ch, k.vec[:], k.vecs.ap(), reads=[k.vecs], writes=[k.vec])
    for l in range(L):
        o = l * VEC_L + V_KA
        ts(P, "dve", k.omka, k.omka[:, l * 8:l * 8 + 8], k.vec, k.vec[:, o:o + 8], -1.0, 1.0, ALU.mult, ALU.add)


def rms_stats(P, k, src, sq, ps, rs):
    for g in range(4):
        act(P, sq, sq[:, 4 * g:4 * g + 4, :], src, src[:, 4 * g:4 * g + 4, :], AF.Square)
    for c in range(KC):
        mm(P, ps, ps[:], k.cb, k.cb[:, C_ONES:C_ONES + 128], sq, sq[:, c, :], c == 0, c == KC - 1)
    act(P, rs, rs[:], ps, ps[:], AF.Ln, extra=[k.epsb], scale=1.0 / D, bias=k.epsb[:, 0:1])
    act(P, rs, rs[:], rs, rs[:], AF.Exp, scale=-0.5)


def phase_norm(P, k, src, gain_col, dst):
    with P.scope():
        xs = [P.sbuf("nx", [128, KC, 512], F32) for _ in range(2)]
        sq = [P.sbuf("nsq", [128, KC, 512], BF16) for _ in range(2)]
        us = [P.sbuf("nu", [128, KC, 512], BF16) for _ in range(2)]
        rs = [P.sbuf("nr", [128, 512], F32) for _ in range(2)]
        ps = [P.psum("nps", [128, 512]) for _ in range(2)]
        chl = [P.chan("nl") for _ in range(2)]
        chs = [P.chan("ns") for _ in range(2)]
        sv = src.ap().rearrange("(c p) t -> p c t", p=128)
        dv = dst.ap().rearrange("(c p) t -> p c t", p=128)
        nb = T // 512

        def load(tb):
            i = tb % 2
            for g in range(4):
                P.dma("sp", chl[i], xs[i][:, 4 * g:4 * g + 4, :],
                      sv[:, 4 * g:4 * g + 4, tb * 512:(tb + 1) * 512], reads=[src], writes=[xs[i]])
        load(0)
        for tb in range(nb):
            i = tb % 2
            if tb + 1 < nb:
                load(tb + 1)
            rms_stats(P, k, xs[i], sq[i], ps[i], rs[i])
            for c in range(KC):
                stt(P, "dve", us[i], us[i][:, c, :], xs[i], xs[i][:, c, :], k.vec[:, gain_col + c:gain_col + c + 1],
                    rs[i], rs[i][:], ALU.mult, ALU.mult, extra=[k.vec])
            for g in range(4):
                P.dma("sp", chs[i], dv[:, 4 * g:4 * g + 4, tb * 512:(tb + 1) * 512],
                      us[i][:, 4 * g:4 * g + 4, :], reads=[us[i]], writes=[dst])


def phase_b1(P, k, l):
    with P.scope():
        uT = P.sbuf("uT", [128, KC, T], BF16)
        wt = [P.sbuf("wt", [128, KC, 512], BF16) for _ in range(2)]
        wft = P.sbuf("wft", [128, KC, 4], BF16)
        qst = [P.sbuf("qst", [128, 512], BF16) for _ in range(2)]
        zst = [P.sbuf("zst", [128, 512], F32) for _ in range(2)]
        vst = [P.sbuf("vst", [128, 512], BF16) for _ in range(2)]
        zraw = [P.sbuf("zraw", [128, 513], F32) for _ in range(2)]
        dt = P.sbuf("dt", [128, 512], F32)
        fsb = P.sbuf("fsb", [128, 128], F32)
        fsp = P.sbuf("fsp", [128, 128], F32)
        tot = P.sbuf("tot", [128, 128], F32)
        pre = P.sbuf("pre", [128, 128], F32)
        cT = P.sbuf("cT", [32, 4, 128], F32)
        ps = [P.psum("b1ps", [128, 512]) for _ in range(4)]
        psf = P.psum("psf", [128, 128])
        psc = P.psum("psc", [128, 128])
        pst = P.psum("pst", [128, 128])
        chu = P.chan("u")
        chw = [P.chan("w") for _ in range(2)]
        chq = [P.chan("q") for _ in range(2)]
        chz = [P.chan("z") for _ in range(2)]
        chv = [P.chan("v") for _ in range(2)]
        chm = P.chan("m")

        uv = k.uT.ap().rearrange("(c p) t -> p c t", p=128)
        for hh in range(2):
            for g in range(4):
                P.dma("sp", chu, uT[:, 4 * g:4 * g + 4, hh * 2048:(hh + 1) * 2048],
                      uv[:, 4 * g:4 * g + 4, hh * 2048:(hh + 1) * 2048], reads=[k.uT], writes=[uT])
        P.dma("pool", chm, wft[:].rearrange("p c n -> p (c n)"), k.wf.ap()[l], reads=[k.wf], writes=[wft])

        tiles = [("qk", k.wqk, i) for i in range(4)] + [("v", k.wv, i) for i in range(2)] + \
                [("rw", k.wrw, i) for i in range(7)]

        def loadw(idx):
            kind, src, ti = tiles[idx]
            i = idx % 2
            P.dma("pool", chw[i], wt[i][:].rearrange("p c n -> p (c n)"), src.ap()[l, ti], reads=[src], writes=[wt[i]])
        loadw(0)
        pr = [0]
        zi = [0]
        for idx, (kind, src, ti) in enumerate(tiles):
            w = wt[idx % 2]
            if idx + 1 < len(tiles):
                loadw(idx + 1)
            if kind == "qk":
                for j in range(4):
                    nci = ti * 4 + j
                    isq = ti in (0, 2)
                    for tb in range(8):
                        p_ = ps[pr[0] % 4]
                        pr[0] += 1
                        for c in range(KC):
                            mm(P, p_, p_[:], w, w[:, c, j * 128:(j + 1) * 128], uT, uT[:, c, tb * 512:(tb + 1) * 512],
                               c == 0, c == KC - 1)
                        st = qst[tb % 2]
                        act(P, st, st[:], p_, p_[:], AF.Copy, scale=SCALE if isq else 1.0)
                        P.dma("sp", chq[tb % 2], k.qkT.ap()[nci, :, tb * 512:(tb + 1) * 512], st[:],
                              reads=[st], writes=[k.qkT])
            elif kind == "v":
                vview = k.vs.ap()[ti * 4:ti * 4 + 4].rearrange("h p s d -> p s h d")
                for s in range(32):
                    p_ = ps[pr[0] % 4]
                    pr[0] += 1
                    for c in range(KC):
                        mm(P, p_, p_[:], uT, uT[:, c, s * 128:(s + 1) * 128], w, w[:, c, :], c == 0, c == KC - 1)
                    st = vst[s % 2]
                    cp(P, "dve", st, st[:], p_, p_[:])
                    P.dma("sp", chv[s % 2], vview[:, s, :, :], st[:].rearrange("p (h d) -> p h d", h=4),
                          reads=[st], writes=[k.vs])
                if ti == 1:
                    for s in range(32):
                        for c in range(KC):
                            mm(P, psf, psf[:, 4 * s:4 * s + 4], uT, uT[:, c, s * 128:(s + 1) * 128], wft, wft[:, c, :],
                               c == 0, c == KC - 1)
                    vb = l * VEC_L + V_BF
                    tt(P, "dve", fsb, fsb[:], psf, psf[:], k.vec, k.vec[:, vb:vb + 128], ALU.add)
                    act(P, fsp, fsp[:], fsb, fsb[:], AF.Exp, scale=-1.0)
                    act(P, fsp, fsp[:], fsp, fsp[:], AF.Ln, extra=[k.epsb], bias=k.epsb[:, 1:2])
                    mm(P, psc, psc[:], k.cf, k.cf[:, C_UTRI:C_UTRI + 128], fsp, fsp[:], True, True)
                    mm(P, pst, pst[:], k.cf, k.cf[:, C_ONES:C_ONES + 128], fsp, fsp[:], True, True)
                    cp(P, "dve", tot, tot[:], pst, pst[:])
                    P.op("dve", lambda e: e.memset(pre[:, 0:4], 0.0), writes=[pre])
                    for s in range(1, 32):
                        tt(P, "dve", pre, pre[:, 4 * s:4 * s + 4], pre, pre[:, 4 * s - 4:4 * s],
                           tot, tot[:, 4 * s - 4:4 * s], ALU.add)
                    tt(P, "dve", k.csum, k.csum[:].rearrange("p h s -> p s h"),
                       psc, psc[:].rearrange("p (s h) -> p s h", h=4),
                       pre, pre[:].rearrange("p (s h) -> p s h", h=4), ALU.add)
                    for h in range(4):
                        mm(P, ps[h], ps[h][0:32, 0:128], k.csum, k.csum[:, h, :], k.cf, k.cf[:, C_ID:C_ID + 128],
                           True, True)
                        act(P, cT, cT[:, h, :], ps[h], ps[h][0:32, 0:128], AF.Copy, scale=-1.0)
                    P.dma("sp", chm, k.cdram.ap().rearrange("h (s p) -> s h p", p=128), cT[:],
                          reads=[cT], writes=[k.cdram])
            else:
                nj = 4 if ti < 6 else 3
                for j in range(nj):
                    nci = ti * 4 + j
                    M = 32 if nci == 26 else 128
                    mcol = l * VEC_L + V_MU + nci
                    for tb in range(8):
                        p_ = ps[pr[0] % 4]
                        pr[0] += 1
                        for c in range(KC):
                            mm(P, p_, p_[0:M, :], w, w[:, c, j * 128:j * 128 + M], uT, uT[:, c, tb * 512:(tb + 1) * 512],
                               c == 0, c == KC - 1)
                        zr = zraw[zi[0] % 2]
                        zp = zraw[(zi[0] + 1) % 2]
                        zi[0] += 1
                        if tb == 0:
                            P.op("dve", lambda e: e.memset(zr[0:M, 0:1], 0.0), writes=[zr])
                        else:
                            cp(P, "act", zr, zr[0:M, 0:1], zp, zp[0:M, 512:513])
                        cp(P, "act", zr, zr[0:M, 1:513], p_, p_[0:M, :])
                        tt(P, "dve", dt, dt[0:M, :], zr, zr[0:M, 0:512], zr, zr[0:M, 1:513], ALU.subtract)
                        st = zst[tb % 2]
                        stt(P, "dve", st, st[0:M, :], dt, dt[0:M, :],
                            k.vec[0:M, mcol:mcol + 1], zr, zr[0:M, 1:513], ALU.mult, ALU.add, extra=[k.vec])
                        P.dma("sp", chz[tb % 2], k.zsT.ap()[nci * 128:nci * 128 + M, tb * 512:(tb + 1) * 512],
                              st[0:M, :], reads=[st], writes=[k.zsT])


def phase_b2(P, k, l):
    with P.scope():
        qT = [P.sbuf("qT", [128, T], BF16) for _ in range(2)]
        kT = [P.sbuf("kT", [128, T], BF16) for _ in range(2)]
        vv = [P.sbuf("vv", [128, 32, 128], BF16) for _ in range(2)]
        cq = P.sbuf("cq", [128, T], F32)
        tts = [P.sbuf("tt", [128, 512], F32) for _ in range(3)]
        t2s3 = [P.sbuf("t2", [128, 512], F32) for _ in range(3)]
        e2s = [P.sbuf("e2", [128, 512], F32) for _ in range(3)]
        spb3 = [P.sbuf("spb", [128, 512], BF16) for _ in range(3)]
        pps = [P.sbuf("pp", [128, 512], BF16) for _ in range(3)]
        Rt = P.sbuf("Rt", [128, 512], F32)
        rec = P.sbuf("rec", [128, 512], F32)
        yst = [P.sbuf("yst", [128, 512], BF16) for _ in range(2)]
        pS = [P.psum("pS", [128, 512]) for _ in range(2)]
        pB = [P.psum("pB", [128, 512]) for _ in range(2)]
        pC = [P.psum("pC", [128, 512]) for _ in range(2)]
        pO = [P.psum("pO", [128, 512]) for _ in range(2)]
        chl = [P.chan("al") for _ in range(2)]
        chc = P.chan("ac")
        chy = [P.chan("ay") for _ in range(2)]
        heads = [("sb", h) for h in range(4)] + [("fox", h) for h in range(4)]

        def load(hi):
            kind, h = heads[hi]
            i = hi % 2
            base = 0 if kind == "sb" else 8
            P.dma("sp", chl[i], qT[i][:], k.qkT.ap()[base + h], reads=[k.qkT], writes=[qT[i]])
            P.dma("sp", chl[i], kT[i][:], k.qkT.ap()[base + 4 + h], reads=[k.qkT], writes=[kT[i]])
            P.dma("sp", chl[i], vv[i][:], k.vs.ap()[(0 if kind == "sb" else 4) + h], reads=[k.vs], writes=[vv[i]])
        load(0)
        it = [0]
        yi = [0]
        ones = k.cb[:, C_ONES:C_ONES + 128]
        for hi, (kind, h) in enumerate(heads):
            i = hi % 2
            if hi + 1 < len(heads):
                load(hi + 1)
            q_, k_, v_ = qT[i], kT[i], vv[i]
            if kind == "fox":
                P.dma("sp", chc, cq[:], k.cdram.ap()[h:h + 1, :].partition_broadcast(128), reads=[k.cdram], writes=[cq])
            for QB in range(8):
                nkb = 4 * QB + 4
                O = pO[QB % 2]
                qs = slice(QB * 512, (QB + 1) * 512)
                if kind == "fox":
                    DN = pC[QB % 2]

                    def ffront(kb):
                        n = it[0]
                        it[0] += 1
                        S = pS[n % 2]
                        mm(P, S, S[:], k_, k_[:, kb * 128:(kb + 1) * 128], q_, q_[:, qs], True, True)
                        t_ = tts[n % 3]
                        stt(P, "dve", t_, t_[:], S, S[:], k.csum[:, h, kb:kb + 1], cq, cq[:, qs], ALU.add, ALU.add,
                            extra=[k.csum])
                        if kb >= 4 * QB:
                            j = kb - 4 * QB
                            tt(P, "dve", t_, t_[:], t_, t_[:], k.cf, k.cf[:, C_FOXB + j * 512:C_FOXB + (j + 1) * 512],
                               ALU.add)
                        p_ = pps[n % 3]
                        act(P, p_, p_[:], t_, t_[:], AF.Exp)
                        return (kb, p_)

                    def fback(st_):
                        kb, p_ = st_
                        mm(P, O, O[:], v_, v_[:, kb, :], p_, p_[:], kb == 0, kb == nkb - 1)
                        mm(P, DN, DN[:], k.cb, ones, p_, p_[:], kb == 0, kb == nkb - 1)
                    prev = None
                    for kb in range(nkb):
                        cur_ = ffront(kb)
                        if prev is not None:
                            fback(prev)
                        prev = cur_
                    fback(prev)
                    P.op("dve", lambda e: e.reciprocal(out=rec[:], in_=DN[:]), reads=[DN], writes=[rec])
                    ys = yst[yi[0] % 2]
                    tt(P, "dve", ys, ys[:], O, O[:], rec, rec[:], ALU.mult)
                    row0 = 512 + h * 128
                else:
                    P.op("dve", lambda e: e.memset(Rt[:], 0.0), writes=[Rt])
                    def front(kb):
                        n = it[0]
                        it[0] += 1
                        S = pS[n % 2]
                        Bp = pB[n % 2]
                        Cp = pC[n % 2]
                        mm(P, S, S[:], k_, k_[:, kb * 128:(kb + 1) * 128], q_, q_[:, qs], True, True)
                        e1 = tts[n % 3]
                        e2 = e2s[n % 3]
                        sb_ = spb3[n % 3]
                        act(P, e1, e1[:], S, S[:], AF.Exp)
                        act(P, e2, e2[:], S, S[:], AF.Exp, scale=-1.0)
                        act(P, sb_, sb_[:], e1, e1[:], AF.Ln, extra=[k.epsb], bias=k.epsb[:, 1:2])
                        act(P, e2, e2[:], e2, e2[:], AF.Ln, extra=[k.epsb], bias=k.epsb[:, 1:2])
                        diag = kb >= 4 * QB
                        msk = None
                        if diag:
                            j = kb - 4 * QB
                            msk = k.sbm[:, j * 512:(j + 1) * 512]
                            tt(P, "dve", sb_, sb_[:], sb_, sb_[:], k.sbm, msk, ALU.mult)
                        mm(P, Bp, Bp[:], k.cb, k.cb[:, C_NEGT:C_NEGT + 128], sb_, sb_[:], True, True)
                        mm(P, Cp, Cp[:], k.cb, ones, sb_, sb_[:], True, True)
                        return (n, kb, Bp, Cp, e2, msk)

                    def back(st_):
                        n, kb, Bp, Cp, e2, msk = st_
                        t2 = t2s3[n % 3]
                        tt(P, "dve", t2, t2[:], Bp, Bp[:], Rt, Rt[:], ALU.add)
                        tt(P, "dve", Rt, Rt[:], Rt, Rt[:], Cp, Cp[:], ALU.subtract)
                        tt(P, "dve", t2, t2[:], t2, t2[:], e2, e2[:], ALU.subtract)
                        p_ = pps[n % 3]
                        act(P, p_, p_[:], t2, t2[:], AF.Exp)
                        if msk is not None:
                            tt(P, "dve", p_, p_[:], p_, p_[:], k.sbm, msk, ALU.mult)
                        mm(P, O, O[:], v_, v_[:, kb, :], p_, p_[:], kb == nkb - 1, kb == 0)
                    prev = None
                    for kb in reversed(range(nkb)):
                        cur_ = front(kb)
                        if prev is not None:
                            back(prev)
                        prev = cur_
                    back(prev)
                    ys = yst[yi[0] % 2]
                    cp(P, "act", ys, ys[:], O, O[:])
                    row0 = h * 128
                P.dma("sp", chy[yi[0] % 2], k.yT.ap()[row0:row0 + 128, qs], ys[:], reads=[ys], writes=[k.yT])
                yi[0] += 1


class _Stop(Exception):
    pass


def phase_b3(P, k, l):
    try:
        _phase_b3(P, k, l)
    except _Stop:
        pass


def _phase_b3(P, k, l):
    import os
    STOP = int(os.environ.get("B3_STOP", "99"))

    def stop(n):
        if STOP <= n:
            raise _Stop()
    with P.scope():
        vo = l * VEC_L
        NL = 6
        ld = {n: [P.sbuf("ld" + n, [128, 512], F32) for _ in range(2)] for n in ("r", "k", "v", "wa", "g0", "g1")}
        f32n = ("lw", "a", "kk", "tmp", "kp", "kka", "cl", "ex", "e4", "bv", "sq", "gT")
        f = {n: P.sbuf("f" + n, [128, 512], F32) for n in f32n}
        b16n = ("thad", "sg", "sg1", "rt", "kt", "bt", "kap", "Kp", "Bp", "vb")
        b = {n: P.sbuf("b" + n, [128, 512], BF16) for n in b16n}
        WL = P.sbuf("WL", [128, 4], F32)
        tm = {n: P.sbuf("tm" + n, [128, 4, 128], BF16) for n in ("V", "K", "B")}
        wa_t = P.sbuf("wa_t", [128, 128], BF16)
        g0_t = P.sbuf("g0_t", [128, 128], BF16)
        g1_t = P.sbuf("g1_t", [32, 128], BF16)
        def m16(n, cnt_):
            return [P.sbuf(n, [128, 128], BF16) for _ in range(cnt_)]
        A1T, B1T, B2T, TIV = m16("A1T", 8), m16("B1T", 8), m16("B2T", 8), m16("TIV", 8)
        Mp = [m16("Mp", 2) for _ in range(8)]
        Np = [m16("Np", 2) for _ in range(8)]
        R16 = [m16("R16", 2) for _ in range(8)]
        R32 = [P.sbuf("R32", [128, 128], F32) for _ in range(8)]
        S32 = P.sbuf("S32", [128, 128], F32)
        S16 = P.sbuf("S16", [128, 128], BF16)
        Gsb = P.sbuf("Gsb", [128, 128], BF16)
        nP = P.sbuf("nP", [128, 128], BF16)
        ysb4 = [P.sbuf("ysb", [128, 128], F32) for _ in range(4)]
        ysq = P.sbuf("ysq", [128, 128], F32)
        yn = P.sbuf("yn", [128, 128], F32)
        st1 = P.sbuf("st1", [128, 8], F32)
        o1 = P.sbuf("o1", [128, 128], F32)
        yst = [P.sbuf("yst3", [128, 512], BF16) for _ in range(2)]
        pbig = [P.psum("pbig", [128, 512]) for _ in range(2)]
        ptr_t = [P.psum("ptr", [128, 512]) for _ in range(2)]
        ptr = [Buf(ptr_t[i][:, 0:128], f"ptr{i}") for i in range(2)]
        pm_t = [P.psum("pm", [128, 512]) for _ in range(2)]
        pm = [Buf(pm_t[i][:, 0:128], f"pm{i}") for i in range(2)]
        p3_t = [P.psum("p3", [128, 512]) for _ in range(2)]
        p3 = [Buf(p3_t[i][:, 0:128], f"p3{i}") for i in range(2)]
        chl = [P.chan("rl") for _ in range(2)]
        chw = P.chan("rw")
        chy = [P.chan("ry") for _ in range(2)]
        cnt = {"big": 0, "tr": 0, "pm": 0, "p3": 0, "y": 0}

        def nxt(kind, lst):
            i = cnt[kind]
            cnt[kind] += 1
            return lst[i % len(lst)]
        idf = k.cf[:, C_ID:C_ID + 128]
        idb = k.cb[:, C_ID:C_ID + 128]
        blk = k.cf[:, C_BLK:C_BLK + 128]
        mup = k.cf[:, C_MUP:C_MUP + 128]
        mlo = k.cf[:, C_MLO:C_MLO + 128]
        mui = k.cf[:, C_UTRI:C_UTRI + 128]

        def load(idx):
            hp, tb = idx // 8, idx % 8
            i = idx % 2
            cs = slice(tb * 512, (tb + 1) * 512)
            for n, nci, M in (("r", hp, 128), ("k", 8 + hp, 128), ("v", 16 + hp, 128), ("wa", 24, 128),
                              ("g0", 25, 128), ("g1", 26, 32)):
                P.dma("sp", chl[i], ld[n][i][0:M, :], k.zsT.ap()[nci * 128:nci * 128 + M, cs],
                      reads=[k.zsT], writes=[ld[n][i]])
        load(0)
        import os
        NHP = int(os.environ.get("B3_HP", "8"))
        NTB = int(os.environ.get("B3_TB", "8"))
        for hp in range(NHP):
            hc = slice(hp * 128, (hp + 1) * 128)
            P.dma("pool", chw, wa_t[0:64, :], k.wup.ap()[l, :, hc], reads=[k.wup], writes=[wa_t])
            P.dma("pool", chw, wa_t[64:128, :], k.aup.ap()[l, :, hc], reads=[k.aup], writes=[wa_t])
            P.dma("pool", chw, g0_t[:], k.gup.ap()[l, 0:128, hc], reads=[k.gup], writes=[g0_t])
            P.dma("pool", chw, g1_t[:], k.gup.ap()[l, 128:160, hc], reads=[k.gup], writes=[g1_t])
            P.op("dve", lambda e: e.memset(S32[:], 0.0), writes=[S32])
            P.op("dve", lambda e: e.memset(S16[:], 0.0), writes=[S16])

            def vc(col):
                return k.vec[:, vo + col + hp:vo + col + hp + 1]
            for tb in range(NTB):
                idx = hp * 8 + tb
                i = idx % 2
                if tb + 1 < NTB or hp + 1 < NHP:
                    load(idx + 1 if tb + 1 < NTB else (hp + 1) * 8)
                r_, k_, v_, wa_, g0_, g1_ = (ld[n][i] for n in ("r", "k", "v", "wa", "g0", "g1"))
                act(P, b["thad"], b["thad"][0:64, :], wa_, wa_[0:64, :], AF.Tanh)
                cp(P, "act", b["thad"], b["thad"][64:128, :], wa_, wa_[64:128, :])
                act(P, b["sg"], b["sg"][:], g0_, g0_[:], AF.Sigmoid)
                act(P, b["sg1"], b["sg1"][0:32, :], g1_, g1_[0:32, :], AF.Sigmoid)
                pw = nxt("big", pbig)
                mm(P, pw, pw[:], wa_t, wa_t[0:64, :], b["thad"], b["thad"][0:64, :], True, True)
                act(P, f["lw"], f["lw"][:], pw, pw[:], AF.Sigmoid, extra=[k.vec], bias=vc(V_W0))
                pa = nxt("big", pbig)
                mm(P, pa, pa[:], wa_t, wa_t[64:128, :], b["thad"], b["thad"][64:128, :], True, True)
                act(P, f["a"], f["a"][:], pa, pa[:], AF.Sigmoid, extra=[k.vec], bias=vc(V_A0))
                pg = nxt("big", pbig)
                mm(P, pg, pg[:], g0_t, g0_t[:], b["sg"], b["sg"][:], True, False)
                mm(P, pg, pg[:], g1_t, g1_t[0:32, :], b["sg1"], b["sg1"][0:32, :], False, True)
                cp(P, "act", f["gT"], f["gT"][:], pg, pg[:])
                stop(1)
                P.op("dve", lambda e: e.tensor_scalar_mul(out=f["lw"][:], in0=f["lw"][:], scalar1=-EM05),
                     reads=[f["lw"]], writes=[f["lw"]])
                P.op("dve", lambda e: e.tensor_scalar_mul(out=f["kk"][:], in0=k_[:], scalar1=vc(V_KK)),
                     reads=[k_, k.vec], writes=[f["kk"]])
                ts(P, "dve", f["tmp"], f["tmp"][:], f["a"], f["a"][:], vc(V_KA), k.omka[:, l * 8 + hp:l * 8 + hp + 1],
                   ALU.mult, ALU.add, extra=[k.vec, k.omka])
                tt(P, "dve", f["kp"], f["kp"][:], k_, k_[:], f["tmp"], f["tmp"][:], ALU.mult)
                tt(P, "dve", f["sq"], f["sq"][:], f["kk"], f["kk"][:], f["kk"], f["kk"][:], ALU.mult)
                pq = nxt("big", pbig)
                mm(P, pq, pq[:], k.cf, blk, f["sq"], f["sq"][:], True, True)
                act(P, f["ex"], f["ex"][:], pq, pq[:], AF.Ln, extra=[k.epsb], bias=k.epsb[:, 3:4])
                act(P, f["ex"], f["ex"][:], f["ex"], f["ex"][:], AF.Exp, scale=-0.5)
                tt(P, "dve", f["kk"], f["kk"][:], f["kk"], f["kk"][:], f["ex"], f["ex"][:], ALU.mult)
                tt(P, "dve", f["kka"], f["kka"][:], f["kk"], f["kk"][:], f["a"], f["a"][:], ALU.mult)
                tt(P, "dve", f["tmp"], f["tmp"][:], r_, r_[:], f["kp"], f["kp"][:], ALU.mult)
                P.op("dve", lambda e: e.tensor_scalar_mul(out=f["sq"][:], in0=f["tmp"][:], scalar1=vc(V_RK)),
                     reads=[f["tmp"], k.vec], writes=[f["sq"]])
                pb_ = nxt("big", pbig)
                mm(P, pb_, pb_[:], k.cf, blk, f["sq"], f["sq"][:], True, True)
                tt(P, "dve", f["bv"], f["bv"][:], pb_, pb_[:], v_, v_[:], ALU.mult)
                cp(P, "act", b["vb"], b["vb"][:], v_, v_[:])
                stop(2)
                P.op("dve", lambda e: e.tensor_tensor_scan(out=f["cl"][:], data0=k.cf[:, C_RST:C_RST + 512],
                                                           data1=f["lw"][:], initial=0.0, op0=ALU.mult, op1=ALU.add),
                     reads=[k.cf, f["lw"]], writes=[f["cl"]])
                act(P, f["ex"], f["ex"][:], f["cl"], f["cl"][:], AF.Exp)
                tt(P, "dve", b["rt"], b["rt"][:], r_, r_[:], f["ex"], f["ex"][:], ALU.mult)
                act(P, f["ex"], f["ex"][:], f["cl"], f["cl"][:], AF.Exp, scale=-1.0)
                tt(P, "dve", b["kt"], b["kt"][:], f["kp"], f["kp"][:], f["ex"], f["ex"][:], ALU.mult)
                tt(P, "dve", b["bt"], b["bt"][:], f["kka"], f["kka"][:], f["ex"], f["ex"][:], ALU.mult)
                tt(P, "dve", f["tmp"], f["tmp"][:], f["cl"], f["cl"][:], f["lw"], f["lw"][:], ALU.subtract)
                act(P, f["ex"], f["ex"][:], f["tmp"], f["tmp"][:], AF.Exp)
                tt(P, "dve", b["kap"], b["kap"][:], f["kk"], f["kk"][:], f["ex"], f["ex"][:], ALU.mult)
                for c in range(4):
                    act(P, f["e4"], f["e4"][:, c * 128:(c + 1) * 128], f["cl"], f["cl"][:, c * 128:(c + 1) * 128],
                        AF.Exp, scale=-1.0, bias=f["cl"][:, c * 128 + 127:c * 128 + 128])
                    act(P, WL, WL[:, c:c + 1], f["cl"], f["cl"][:, c * 128 + 127:c * 128 + 128], AF.Exp)
                tt(P, "dve", b["Kp"], b["Kp"][:], f["kp"], f["kp"][:], f["e4"], f["e4"][:], ALU.mult)
                tt(P, "dve", b["Bp"], b["Bp"][:], f["kka"], f["kka"][:], f["e4"], f["e4"][:], ALU.mult)
                stop(3)
                TRN = os.environ.get("B3_TRN", "VKBe")
                for c in range(4):
                    for n, src_ in (("V", b["vb"]), ("K", b["Kp"]), ("B", b["Bp"])):
                        if n not in TRN:
                            continue
                        pt = nxt("tr", ptr)
                        mm(P, pt, pt[:], src_, src_[:, c * 128:(c + 1) * 128], k.cb, idb, True, True)
                        if "e" in TRN:
                            cp(P, "act" if n == "V" else "dve", tm[n], tm[n][:, c, :], pt, pt[:])
                stop(4)
                slots4 = pm + ptr

                def chain(ci, c, e):
                    cc = slice(c * 128, (c + 1) * 128)
                    er = slice(e * 64, (e + 1) * 64)
                    M_, N_, R16_, R32_ = Mp[ci], Np[ci], R16[ci], R32[ci]
                    p1 = nxt("pm", slots4)
                    mm(P, p1, p1[:], b["kt"], b["kt"][er, cc], b["kap"], b["kap"][er, cc], True, True)
                    tt(P, "dve", A1T[ci], A1T[ci][:], p1, p1[:], k.cf, mup, ALU.mult)
                    yield
                    p2 = nxt("pm", slots4)
                    mm(P, p2, p2[:], b["bt"], b["bt"][er, cc], b["kap"], b["kap"][er, cc], True, True)
                    stt(P, "dve", M_[0], M_[0][:], p2, p2[:], -1.0, k.cf, mup, ALU.mult, ALU.mult)
                    yield
                    p3_ = nxt("pm", slots4)
                    mm(P, p3_, p3_[:], b["kap"], b["kap"][er, cc], b["bt"], b["bt"][er, cc], True, True)
                    stt(P, "dve", N_[0], N_[0][:], p3_, p3_[:], -1.0, k.cf, mlo, ALU.mult, ALU.mult)
                    yield
                    p4 = nxt("pm", slots4)
                    mm(P, p4, p4[:], b["kt"], b["kt"][er, cc], b["rt"], b["rt"][er, cc], True, True)
                    tt(P, "dve", B1T[ci], B1T[ci][:], p4, p4[:], k.cf, mui, ALU.mult)
                    yield
                    p5 = nxt("pm", slots4)
                    mm(P, p5, p5[:], b["bt"], b["bt"][er, cc], b["rt"], b["rt"][er, cc], True, True)
                    tt(P, "dve", B2T[ci], B2T[ci][:], p5, p5[:], k.cf, mui, ALU.mult)
                    tt(P, "dve", R32_, R32_[:], M_[0], M_[0][:], k.cf, idf, ALU.add)
                    tt(P, "dve", R16_[0], R16_[0][:], M_[0], M_[0][:], k.cf, idf, ALU.add)
                    yield
                    cur = 0
                    for it_ in range(NL):
                        nx_ = 1 - cur
                        pn = nxt("pm", slots4)
                        mm(P, pn, pn[:], M_[cur], M_[cur][:], N_[cur], N_[cur][:], True, True)
                        cp(P, "act", N_[nx_], N_[nx_][:], pn, pn[:])
                        yield
                        if it_ < NL - 1:
                            pm_ = nxt("pm", slots4)
                            mm(P, pm_, pm_[:], N_[cur], N_[cur][:], M_[cur], M_[cur][:], True, True)
                            cp(P, "act", M_[nx_], M_[nx_][:], pm_, pm_[:])
                            yield
                        pr_ = nxt("pm", slots4)
                        mm(P, pr_, pr_[:], N_[nx_], N_[nx_][:], R16_[cur], R16_[cur][:], True, True)
                        tt(P, "dve", R32_, R32_[:], R32_, R32_[:], pr_, pr_[:], ALU.add)
                        last = it_ == NL - 1
                        dst = TIV[ci] if last else R16_[nx_]
                        cp(P, "act", dst, dst[:], R32_, R32_[:])
                        yield
                        cur = nx_
                gens = [chain(c * 2 + e, c, e) for c in range(4) for e in range(2)]
                while gens:
                    alive = []
                    for g_ in gens:
                        try:
                            next(g_)
                            alive.append(g_)
                        except StopIteration:
                            pass
                    gens = alive
                def chain_part(c):
                    cc = slice(c * 128, (c + 1) * 128)
                    stop(5)
                    G = nxt("p3", p3)
                    mm(P, G, G[:], b["kap"], b["kap"][:, cc], S16, S16[:], True, False)
                    for e in range(2):
                        ec = slice(e * 64, (e + 1) * 64)
                        mm(P, G, G[:, ec], A1T[c * 2 + e], A1T[c * 2 + e][:], tm["V"], tm["V"][:, c, ec], False, e == 1)
                    cp(P, "act", Gsb, Gsb[:], G, G[:])
                    Pp = nxt("p3", p3)
                    for e in range(2):
                        ec = slice(e * 64, (e + 1) * 64)
                        mm(P, Pp, Pp[:, ec], TIV[c * 2 + e], TIV[c * 2 + e][:], Gsb, Gsb[:, ec], True, True)
                    act(P, nP, nP[:], Pp, Pp[:], AF.Copy, scale=-1.0)
                    Y = nxt("p3", p3)
                    mm(P, Y, Y[:], b["rt"], b["rt"][:, cc], S16, S16[:], True, False)
                    for e in range(2):
                        ec = slice(e * 64, (e + 1) * 64)
                        mm(P, Y, Y[:, ec], B1T[c * 2 + e], B1T[c * 2 + e][:], tm["V"], tm["V"][:, c, ec], False, False)
                        mm(P, Y, Y[:, ec], B2T[c * 2 + e], B2T[c * 2 + e][:], nP, nP[:, ec], False, e == 1)
                    cp(P, "act", ysb4[c], ysb4[c][:], Y, Y[:])
                    U = nxt("p3", p3)
                    mm(P, U, U[:], tm["K"], tm["K"][:, c, :], tm["V"], tm["V"][:, c, :], True, False)
                    mm(P, U, U[:], tm["B"], tm["B"][:, c, :], nP, nP[:], False, True)
                    for e in range(2):
                        er = slice(e * 64, (e + 1) * 64)
                        stt(P, "dve", S32, S32[er, er], S32, S32[er, er], WL[er, c:c + 1], U, U[er, er],
                            ALU.mult, ALU.add, extra=[WL])
                        cp(P, "act", S16, S16[er, er], S32, S32[er, er])

                def out_part(c):
                    cc = slice(c * 128, (c + 1) * 128)
                    ys = yst[cnt["y"] % 2]
                    stop(6)
                    ysb = ysb4[c]
                    y3 = ysb[:].rearrange("p (e v) -> p e v", e=2)
                    P.op("dve", lambda e_: e_.reduce_sum(out=st1[:, 0:2], in_=y3, axis=AX.X), reads=[ysb], writes=[st1])
                    tt(P, "dve", ysq, ysq[:], ysb, ysb[:], ysb, ysb[:], ALU.mult)
                    P.op("dve", lambda e_: e_.reduce_sum(out=st1[:, 2:4], in_=ysq[:].rearrange("p (e v) -> p e v", e=2),
                                                         axis=AX.X), reads=[ysq], writes=[st1])
                    P.op("dve", lambda e_: e_.tensor_scalar_mul(out=st1[:, 0:2], in0=st1[:, 0:2], scalar1=1.0 / 64),
                         reads=[st1], writes=[st1])
                    tt(P, "dve", st1, st1[:, 4:6], st1, st1[:, 0:2], st1, st1[:, 0:2], ALU.mult)
                    stt(P, "dve", st1, st1[:, 6:8], st1, st1[:, 2:4], 1.0 / 64, st1, st1[:, 4:6], ALU.mult, ALU.subtract)
                    act(P, st1, st1[:, 6:8], st1, st1[:, 6:8], AF.Ln, extra=[k.epsb], bias=k.epsb[:, 2:3])
                    act(P, st1, st1[:, 6:8], st1, st1[:, 6:8], AF.Exp, scale=-0.5)
                    for e in range(2):
                        ec = slice(e * 64, (e + 1) * 64)
                        ts(P, "dve", yn, yn[:, ec], ysb, ysb[:, ec], st1[:, e:e + 1], st1[:, 6 + e:7 + e],
                           ALU.subtract, ALU.mult, extra=[st1])
                    YT = nxt("tr", ptr)
                    mm(P, YT, YT[:], yn, yn[:], k.cf, idf, True, True)
                    stt(P, "dve", o1, o1[:], YT, YT[:], vc(V_LNW), f["bv"], f["bv"][:, cc], ALU.mult, ALU.add,
                        extra=[k.vec])
                    ys = yst[cnt["y"] % 2]
                    stt(P, "dve", ys, ys[:, cc], o1, o1[:], vc(V_LNB), f["gT"], f["gT"][:, cc], ALU.add, ALU.mult,
                        extra=[k.vec])

                chain_part(0)
                for c in range(1, 4):
                    chain_part(c)
                    out_part(c - 1)
                out_part(3)
                ys = yst[cnt["y"] % 2]
                P.dma("sp", chy[cnt["y"] % 2], k.yT.ap()[1024 + hp * 128:1024 + (hp + 1) * 128, tb * 512:(tb + 1) * 512],
                      ys[:], reads=[ys], writes=[k.yT])
                cnt["y"] += 1


def c_order(k):
    order = []
    for ng in range(4):
        order += [(k.wg, i * 4 + ng) for i in range(3)] + [(k.wbr, ng)]
    order += [(k.wout, i) for i in range(4)]
    for q in range(4):
        order += [(k.wup_mlp, q * 4 + i) for i in range(4)] + [(k.wdn, q * 4 + i) for i in range(4)]
    return order


def convert_c_weights(P, k, l):
    ch = P.chan("cv")
    for t, (src, ti) in enumerate(c_order(k)):
        P.dma("pool", ch, k.wcb.ap()[l, t], src.ap()[l, ti], reads=[src], writes=[k.wcb])


def phase_c(P, k, l, xin, xout):
    with P.scope():
        vo = l * VEC_L + V_NRM
        xs = P.sbuf("cx", [128, KC, 512], F32)
        acc = P.sbuf("cacc", [128, KC, 512], F32)
        uT = P.sbuf("cu", [128, KC, 512], BF16)
        mT = P.sbuf("cm", [128, KC, 512], BF16)
        hd = P.sbuf("chd", [128, KC, 512], BF16)
        yb = P.sbuf("cy", [128, KC, 512], BF16)
        wt = [P.sbuf("cw", [128, KC, 512], BF16) for _ in range(2)]
        gs = [[P.sbuf("cg", [128, 512], BF16) for _ in range(4)] for _ in range(3)]
        rs = P.sbuf("crs", [128, 512], F32)
        tmp = [P.sbuf("ctmp", [128, 512], F32) for _ in range(2)]
        pg = [P.psum("cpg", [128, 512]) for _ in range(3)]
        pbr = [P.psum("cpb", [128, 512]) for _ in range(3)]
        pss = P.psum("cpss", [128, 512])
        chw = [P.chan("cw") for _ in range(2)]
        chx = P.chan("cx")
        chu = P.chan("cu")
        chyy = P.chan("cy")
        cho = P.chan("co")
        order = c_order(k)
        NT = len(order)
        seq = [0]

        def loadw(gidx):
            i = gidx % 2
            P.dma("sp", chw[i], wt[i][:].rearrange("p c n -> p (c n)"), k.wcb.ap()[l, gidx % NT],
                  reads=[k.wcb], writes=[wt[i]])

        def nextw():
            g = seq[0]
            seq[0] += 1
            if g + 1 < NT * 8:
                loadw(g + 1)
            return wt[g % 2]
        loadw(0)
        xv = xin.ap().rearrange("(c p) t -> p c t", p=128)
        ov = xout.ap().rearrange("(c p) t -> p c t", p=128)
        uv = k.uT.ap().rearrange("(c p) t -> p c t", p=128)
        yv = k.yT.ap().rearrange("(c p) t -> p c t", p=128)
        gi = [0]

        def gemm(w, j, rhs_b, out_ps):
            for c in range(KC):
                mm(P, out_ps, out_ps[:], w, w[:, c, j * 128:(j + 1) * 128], rhs_b, rhs_b[:, c, :], c == 0, c == KC - 1)

        def post_norm(gcol):
            rms_stats(P, k, acc, hd, pss, rs)
            for c in range(KC):
                stt(P, "dve", acc, acc[:, c, :], acc, acc[:, c, :], k.vec[:, gcol + c:gcol + c + 1], rs, rs[:],
                    ALU.mult, ALU.mult, extra=[k.vec])
                tt(P, "dve", xs, xs[:, c, :], xs, xs[:, c, :], acc, acc[:, c, :], ALU.add)
        for tb in range(8):
            cs = slice(tb * 512, (tb + 1) * 512)
            for g in range(4):
                gsl = slice(4 * g, 4 * g + 4)
                P.dma("sp", chx, xs[:, gsl, :], xv[:, gsl, cs], reads=[xin], writes=[xs])
                P.dma("sp", chu, uT[:, gsl, :], uv[:, gsl, cs], reads=[k.uT], writes=[uT])
                P.dma("sp", chyy, yb[:, gsl, :], yv[:, gsl, cs], reads=[k.yT], writes=[yb])
            for ng in range(4):
                for i in range(3):
                    w = nextw()
                    for j in range(4):
                        p_ = pg[gi[0] % 3]
                        gi[0] += 1
                        gemm(w, j, uT, p_)
                        act(P, gs[i][j], gs[i][j][:], p_, p_[:], AF.Sigmoid)
                w = nextw()
                for j in range(4):
                    n = ng * 4 + j
                    for i, (k0, nk) in enumerate(((0, 4), (4, 4), (8, 8))):
                        for c in range(nk):
                            mm(P, pbr[i], pbr[i][:], w, w[:, k0 + c, j * 128:(j + 1) * 128], yb, yb[:, k0 + c, :],
                               c == 0, c == nk - 1)
                    tt(P, "dve", tmp[0], tmp[0][:], pbr[0], pbr[0][:], gs[0][j], gs[0][j][:], ALU.mult)
                    tt(P, "dve", tmp[1], tmp[1][:], pbr[1], pbr[1][:], gs[1][j], gs[1][j][:], ALU.mult)
                    tt(P, "dve", tmp[0], tmp[0][:], tmp[0], tmp[0][:], tmp[1], tmp[1][:], ALU.add)
                    tt(P, "dve", tmp[1], tmp[1][:], pbr[2], pbr[2][:], gs[2][j], gs[2][j][:], ALU.mult)
                    tt(P, "dve", mT, mT[:, n, :], tmp[0], tmp[0][:], tmp[1], tmp[1][:], ALU.add)
            for t_ in range(4):
                w = nextw()
                for j in range(4):
                    p_ = pg[gi[0] % 3]
                    gi[0] += 1
                    gemm(w, j, mT, p_)
                    cp(P, "act", acc, acc[:, t_ * 4 + j, :], p_, p_[:])
            post_norm(vo + 16)
            rms_stats(P, k, xs, hd, pss, rs)
            for c in range(KC):
                stt(P, "dve", mT, mT[:, c, :], xs, xs[:, c, :], k.vec[:, vo + 32 + c:vo + 32 + c + 1], rs, rs[:],
                    ALU.mult, ALU.mult, extra=[k.vec])
            for q in range(4):
                for t_ in range(4):
                    w = nextw()
                    for j in range(4):
                        p_ = pg[gi[0] % 3]
                        gi[0] += 1
                        gemm(w, j, mT, p_)
                        tq = tmp[gi[0] % 2]
                        P.op("dve", lambda e: e.tensor_scalar_max(out=tq[:], in0=p_[:], scalar1=0.0), reads=[p_], writes=[tq])
                        tt(P, "dve", hd, hd[:, t_ * 4 + j, :], tq, tq[:], tq, tq[:], ALU.mult)
                for cg in range(4):
                    w = nextw()
                    for j in range(4):
                        p_ = pg[gi[0] % 3]
                        gi[0] += 1
                        gemm(w, j, hd, p_)
                        n = cg * 4 + j
                        if q == 0:
                            cp(P, "act", acc, acc[:, n, :], p_, p_[:])
                        else:
                            tt(P, "dve", acc, acc[:, n, :], acc, acc[:, n, :], p_, p_[:], ALU.add)
            post_norm(vo + 48)
            for g in range(4):
                gsl = slice(4 * g, 4 * g + 4)
                P.dma("sp", cho, ov[:, gsl, cs], xs[:, gsl, :], reads=[xs], writes=[xout])


def build(dbg=(), stages="abcdC", nl=L):
    nc = bass.Bass("TRN2", target_bir_lowering=False)
    P = Prog(nc)
    k = K()

    def dk(name):
        return "ExternalOutput" if name in dbg else "Internal"

    def inp(name, shape):
        return P.dram(name, shape, F32, kind="ExternalInput")
    k.xT = inp("xT", [D, T])
    k.consts = inp("consts", [128, NCONST])
    k.vecs = inp("vecs", [128, L * VEC_L])
    k.wqk = inp("wqk", [L, 4, 128, KC * 512])
    k.wv = inp("wv", [L, 2, 128, KC * 512])
    k.wf = inp("wf", [L, 128, KC * 4])
    k.wrw = inp("wrw", [L, 7, 128, KC * 512])
    k.wup = inp("wup", [L, 64, 1024])
    k.aup = inp("aup", [L, 64, 1024])
    k.gup = inp("gup", [L, 160, 1024])
    if "C" in stages:
        k.wg = inp("wg", [L, 12, 128, KC * 512])
        k.wbr = inp("wbr", [L, 4, 128, KC * 512])
        k.wout = inp("wout", [L, 4, 128, KC * 512])
        k.wup_mlp = inp("wmup", [L, 16, 128, KC * 512])
        k.wdn = inp("wmdn", [L, 16, 128, KC * 512])
    k.uT = P.dram("uT", [D, T], BF16, kind=dk("uT"))
    k.qkT = P.dram("qkT", [NQK, 128, T], BF16, kind=dk("qkT"))
    k.vs = P.dram("vs", [8, 128, 32, 128], BF16, kind=dk("vs"))
    k.zsT = P.dram("zsT", [NRW * 128, T], F32, kind=dk("zsT"))
    k.cdram = P.dram("cdram", [4, T], F32, kind=dk("cdram"))
    k.yT = P.dram("yT", [D, T], BF16, kind=dk("yT"))
    k.x1T = P.dram("x1T", [D, T], F32, kind=dk("x1T"))
    k.out = P.dram("outT", [D, T], F32, kind="ExternalOutput")
    if "C" in stages:
        k.wcb = P.dram("wcb", [L, 52, 128, KC * 512], BF16)
    k.csum = P.sbuf("csum", [128, 4, 32], F32)
    load_consts(P, k)
    for l in range(nl):
        xin = k.xT if l == 0 else k.x1T
        xout = k.x1T if l == 0 and nl == 2 else k.out
        if "a" in stages:
            phase_norm(P, k, xin, l * VEC_L + V_NRM + 0, k.uT)
        if "b" in stages:
            phase_b1(P, k, l)
        if "C" in stages:
            convert_c_weights(P, k, l)
        if "c" in stages:
            phase_b2(P, k, l)
        if "d" in stages:
            phase_b3(P, k, l)
        if "C" in stages:
            phase_c(P, k, l, xin, xout)
    P.close()
    print("instructions:", P.n_inst)
    return nc


QA0, KA0, VA0, QB0, KB0, VB0, FB0, RW0, GT0 = 0, 512, 1024, 1536, 2048, 2560, 3072, 3076, 6436


def _tile(w):
    kk, n = w.shape
    out = np.zeros((kk // 128, 128, 512), np.float32)
    out[:, :, :n] = w.reshape(kk // 128, 128, n)
    return np.ascontiguousarray(out.transpose(1, 0, 2)).reshape(128, -1)


def make_consts():
    c = np.zeros((128, NCONST), np.float32)
    i = np.arange(128)
    c[:, C_ID:C_ID + 128] = np.eye(128)
    c[:, C_ONES:C_ONES + 128] = 1.0
    c[:, C_UTRI:C_UTRI + 128] = (i[:, None] <= i[None, :])
    c[:, C_NEGT:C_NEGT + 128] = -1.0 * (i[:, None] > i[None, :])
    c[:, C_MUP:C_MUP + 128] = (i[:, None] < i[None, :])
    c[:, C_MLO:C_MLO + 128] = (i[:, None] > i[None, :])
    c[:, C_BLK:C_BLK + 128] = ((i[:, None] // 64) == (i[None, :] // 64))
    q = np.arange(512)
    c[:, C_RST:C_RST + 512] = (q % 128 != 0)[None, :]
    for j in range(4):
        kk = j * 128 + i
        c[:, C_FOXB + j * 512:C_FOXB + (j + 1) * 512] = np.where(kk[:, None] <= q[None, :], 0.0, -30000.0)
        c[:, C_SBM + j * 512:C_SBM + (j + 1) * 512] = (kk[:, None] < q[None, :])
    return c


def prep_shared(inp, with_c=True):
    m = {}
    m["consts"] = make_consts()
    vec = np.zeros((128, L * VEC_L), np.float32)
    wqk = np.zeros((L, 4, 128, KC * 512), np.float32)
    wv = np.zeros((L, 2, 128, KC * 512), np.float32)
    wf = np.zeros((L, 128, KC * 4), np.float32)
    wrw = np.zeros((L, 7, 128, KC * 512), np.float32)
    for l in range(L):
        o = l * VEC_L
        for wi, nm in enumerate(("norm_mix_pre", "norm_mix_post", "norm_mlp_pre", "norm_mlp_post")):
            vec[:, o + V_NRM + wi * 16:o + V_NRM + wi * 16 + 16] = inp[nm][l].reshape(16, 128).T
        mu_p = np.zeros(NRW * 128, np.float32)
        mu_p[:3360] = inp["rwkv_mu"][l]
        vec[:, o + V_MU:o + V_MU + NRW] = mu_p.reshape(NRW, 128).T
        for col, nm in ((V_W0, "rwkv_w0"), (V_A0, "rwkv_a0"), (V_KK, "rwkv_k_k"), (V_KA, "rwkv_k_a"),
                        (V_RK, "rwkv_r_k"), (V_LNW, "rwkv_ln_w"), (V_LNB, "rwkv_ln_b")):
            vec[:, o + col:o + col + 8] = inp[nm][l].reshape(8, 128).T
        vec[:, o + V_BF:o + V_BF + 128] = np.tile(inp["b_forget"][l], 32)[None, :]
        w = inp["w_in"][l]
        for ti, c0 in enumerate((QA0, KA0, QB0, KB0)):
            wqk[l, ti] = _tile(w[:, c0:c0 + 512])
        wv[l, 0] = _tile(w[:, VA0:VA0 + 512])
        wv[l, 1] = _tile(w[:, VB0:VB0 + 512])
        wf[l] = np.ascontiguousarray(w[:, FB0:FB0 + 4].reshape(16, 128, 4).transpose(1, 0, 2)).reshape(128, 64)
        for ti in range(7):
            wrw[l, ti] = _tile(w[:, RW0 + ti * 512:min(RW0 + (ti + 1) * 512, RW0 + 3360)])
    m["vecs"] = vec
    m["wqk"], m["wv"], m["wf"], m["wrw"] = wqk, wv, wf, wrw
    m["wup"] = np.ascontiguousarray(inp["rwkv_w_up"])
    m["aup"] = np.ascontiguousarray(inp["rwkv_a_up"])
    m["gup"] = np.ascontiguousarray(inp["rwkv_g_up"])
    if with_c:
        wg = np.zeros((L, 12, 128, KC * 512), np.float32)
        wbr = np.zeros((L, 4, 128, KC * 512), np.float32)
        wout = np.zeros((L, 4, 128, KC * 512), np.float32)
        wmup = np.zeros((L, 16, 128, KC * 512), np.float32)
        wmdn = np.zeros((L, 16, 128, KC * 512), np.float32)
        for l in range(L):
            w = inp["w_in"][l]
            for t in range(12):
                wg[l, t] = _tile(w[:, GT0 + t * 512:GT0 + (t + 1) * 512])
            br = np.concatenate([inp["w_branch_a"][l], inp["w_branch_b"][l], inp["w_branch_c"][l]], 0)
            for t in range(4):
                wbr[l, t] = _tile(br[:, t * 512:(t + 1) * 512])
                wout[l, t] = _tile(inp["w_out"][l][:, t * 512:(t + 1) * 512])
            for t in range(16):
                wmup[l, t] = _tile(inp["w_mlp_up"][l][:, t * 512:(t + 1) * 512])
                q, cg = t // 4, t % 4
                wmdn[l, t] = _tile(inp["w_mlp_down"][l][q * 2048:(q + 1) * 2048, cg * 512:(cg + 1) * 512])
        m["wg"], m["wbr"], m["wout"], m["wmup"], m["wmdn"] = wg, wbr, wout, wmup, wmdn
    return m


_CACHE = {}


def kernel(**inputs):
    inp = {k_: np.asarray(v, dtype=np.float32) for k_, v in inputs.items()}
    if "nc" not in _CACHE:
        _CACHE["nc"] = build()
    nc = _CACHE["nc"]
    shared = prep_shared(inp)
    maps = []
    for c in range(NCORES):
        m = dict(shared)
        m["xT"] = np.ascontiguousarray(inp["x"][c].T)
        maps.append(m)
    res = run_bass_kernel_spmd(nc, maps, core_ids=list(range(NCORES)))
    out = np.stack([np.ascontiguousarray(np.asarray(res.results[c]["outT"]).T) for c in range(NCORES)], 0)
    return out.astype(np.float32)
```
